# Optimizing a Trainium2 kernel written in Bass

```python
import math
import jax
import jax.numpy as jnp
from jax import lax
import numpy as np

D_MODEL = 1024
BATCH = 16
SEQ = 4096
DEPTH = 1
DEC_BATCH = 128
DEC_SEQ = 1
PAST_LEN = 8192
PAGE_SIZE = 128

GDN_HEADS = 8
GDN_DK = 128
GDN_DV = 128
GDN_QK = GDN_HEADS * GDN_DK
GDN_V = GDN_HEADS * GDN_DV
CONV_DIM = 2 * GDN_QK + GDN_V
CONV_WIDTH = 4
CHUNK = 64

DIL_GROUPS = ((128, 1), (512, 4), (2048, 16))
N_GROUPS = 3
DIL_HEADS = 4
DIL_HD = 128
DIL_W = DIL_HEADS * DIL_HD
DIL_QKV = N_GROUPS * 3 * DIL_W

REL_BUCKETS = 32
REL_MAX_DIST = 2048

IN_SIZES = (CONV_DIM, GDN_V, GDN_HEADS, GDN_HEADS, DIL_QKV, DIL_W, D_MODEL, D_MODEL)
IN_WIDTH = 11280
RMS_EPS = 1e-6
NEG = -1e30

kernel_name = 'hybrid_gdn_dilated_decoder_step'


def rms_norm(x, gain):
    xf = x.astype(jnp.float32)
    y = xf * lax.rsqrt(jnp.mean(xf * xf, axis=-1, keepdims=True) + RMS_EPS)
    return (y * gain.astype(jnp.float32)).astype(x.dtype)


def l2_normalize(x):
    return x * lax.rsqrt(jnp.sum(x * x, axis=-1, keepdims=True) + RMS_EPS)


def rel_bucket(dist):
    max_exact = REL_BUCKETS // 2
    d = jnp.maximum(dist, 1).astype(jnp.float32)
    large = max_exact + (jnp.log(d / max_exact) / math.log(REL_MAX_DIST / max_exact)
                         * (REL_BUCKETS - max_exact)).astype(jnp.int32)
    large = jnp.minimum(large, REL_BUCKETS - 1)
    return jnp.where(dist < max_exact, dist, large)


def group_biases(rel_table):
    biases = []
    for gi, (win, dil) in enumerate(DIL_GROUPS):
        dist = jnp.arange(win // dil + 1, dtype=jnp.int32) * dil
        b = rel_table[rel_bucket(dist)][:, gi * DIL_HEADS:(gi + 1) * DIL_HEADS]
        biases.append(b.T.astype(jnp.float32))
    return biases


def mixer_inputs(x, g_pre, w_in):
    xn = rms_norm(x, g_pre)
    proj = jnp.einsum('bld,de->ble', xn, w_in)
    offsets = np.cumsum(IN_SIZES)[:-1].tolist()
    return jnp.split(proj, offsets, axis=-1)


def short_conv_silu(x_ext, conv_w):
    rhs = conv_w.T[:, None, :].astype(x_ext.dtype)
    y = lax.conv_general_dilated(x_ext, rhs, window_strides=(1,), padding='VALID',
                                 dimension_numbers=('NWC', 'WIO', 'NWC'),
                                 feature_group_count=x_ext.shape[-1])
    return jax.nn.silu(y)


def gated_delta_rule(q, k, v, g, beta, s0):
    B, L, H, DK = q.shape
    DV = v.shape[-1]
    c = math.gcd(L, CHUNK)
    n = L // c

    def blocks(a):
        return jnp.moveaxis(a.reshape((B, n, c) + a.shape[2:]), 3, 2)

    q, k, v, g, beta = blocks(q), blocks(k), blocks(v), blocks(g), blocks(beta)
    G = jnp.cumsum(g, axis=-1)
    causal = jnp.tril(jnp.ones((c, c), dtype=bool))
    strict = jnp.tril(jnp.ones((c, c), dtype=bool), k=-1)
    decay = jnp.exp(jnp.where(causal, G[..., :, None] - G[..., None, :], NEG))
    kb = k * beta[..., None]
    a_mat = jnp.where(strict, jnp.einsum('bnhcd,bnhed->bnhce', kb, k) * decay, 0.0)

    def solve(rhs):
        return lax.linalg.triangular_solve(a_mat, rhs, left_side=True, lower=True,
                                           unit_diagonal=True)

    u = solve(v * beta[..., None])
    w = solve(kb * jnp.exp(G)[..., None])
    qk = jnp.einsum('bnhcd,bnhed->bnhce', q, k) * decay
    q_dec = q * jnp.exp(G)[..., None]
    k_dec = k * jnp.exp(G[..., -1:] - G)[..., None]
    g_last = jnp.exp(G[..., -1])
    xs = tuple(jnp.moveaxis(a, 1, 0) for a in (u, w, qk, q_dec, k_dec, g_last))

    def step(S, inp):
        u_c, w_c, qk_c, qd_c, kd_c, gl_c = inp
        v_new = u_c - jnp.einsum('bhcd,bhde->bhce', w_c, S)
        o_c = jnp.einsum('bhcd,bhde->bhce', qd_c, S) + jnp.einsum('bhce,bhef->bhcf', qk_c, v_new)
        S = S * gl_c[..., None, None] + jnp.einsum('bhcd,bhce->bhde', kd_c, v_new)
        return S, o_c

    s_fin, o = lax.scan(step, s0, xs)
    o = jnp.transpose(o, (1, 0, 3, 2, 4)).reshape(B, L, H, DV)
    return o, s_fin


def gdn_branch(x_ext, s0, z_a, a_a, b_a, conv_w, a_log, dt_bias, g_head_norm, w_proj_a):
    B, L, _ = z_a.shape
    f32 = jnp.float32
    qkv = short_conv_silu(x_ext, conv_w).astype(f32)
    q, k, v = jnp.split(qkv, [GDN_QK, 2 * GDN_QK], axis=-1)
    q = l2_normalize(q.reshape(B, L, GDN_HEADS, GDN_DK)) * (GDN_DK ** -0.5)
    k = l2_normalize(k.reshape(B, L, GDN_HEADS, GDN_DK))
    v = v.reshape(B, L, GDN_HEADS, GDN_DV)
    g = -jnp.exp(a_log.astype(f32)) * jax.nn.softplus(a_a.astype(f32) + dt_bias.astype(f32))
    beta = jax.nn.sigmoid(b_a.astype(f32))
    o, s_fin = gated_delta_rule(q, k, v, g, beta, s0.astype(f32))
    o = o * lax.rsqrt(jnp.mean(o * o, axis=-1, keepdims=True) + RMS_EPS) * g_head_norm.astype(f32)
    y = o.reshape(B, L, GDN_V) * jax.nn.silu(z_a.astype(f32))
    return jnp.einsum('ble,ed->bld', y.astype(z_a.dtype), w_proj_a), s_fin


def dilated_attn_prompt(q, k, v, bias_j, win, dil):
    f32 = jnp.float32
    B, S, H, D = q.shape
    blk = win // dil
    span = blk * dil
    s_pad = -(-S // span) * span
    nb = s_pad // span

    def to_blocks(a):
        a = jnp.pad(a.astype(f32), ((0, 0), (0, s_pad - S), (0, 0), (0, 0)))
        return a.reshape(B, nb, blk, dil, H, D).transpose(0, 3, 1, 2, 4, 5)

    def with_prev(a):
        prev = jnp.pad(a, ((0, 0), (0, 0), (1, 0), (0, 0), (0, 0), (0, 0)))[:, :, :-1]
        return jnp.concatenate([prev, a], axis=3)

    qb = to_blocks(q)
    kk = with_prev(to_blocks(k))
    vv = with_prev(to_blocks(v))
    logits = jnp.einsum('brnqhd,brnkhd->brnhqk', qb, kk) * (D ** -0.5)
    j = blk + jnp.arange(blk)[:, None] - jnp.arange(2 * blk)[None, :]
    in_band = (j >= 0) & (j <= blk)
    after_start = (jnp.arange(nb)[:, None, None] > 0) | (jnp.arange(2 * blk)[None, None, :] >= blk)
    mask = in_band[None] & after_start
    logits = logits + bias_j[:, jnp.clip(j, 0, blk)]
    logits = jnp.where(mask[:, None], logits, NEG)
    m = jnp.max(logits, axis=-1, keepdims=True)
    p = jnp.exp(logits - m)
    s = jnp.sum(p, axis=-1, keepdims=True)
    o = jnp.einsum('brnhqk,brnkhd->brnqhd', p, vv) / jnp.swapaxes(s, 3, 4)
    lse = (m + jnp.log(s))[..., 0]
    o = o.transpose(0, 2, 3, 1, 4, 5).reshape(B, s_pad, H, D)[:, :S]
    lse = lse.transpose(0, 2, 4, 1, 3).reshape(B, s_pad, H)[:, :S]
    return o, lse


def dilated_attn_sample(q, k_new, v_new, k_buf, v_buf, bias_j, win, dil):
    f32 = jnp.float32
    B, L, H, D = q.shape
    wb = k_buf.shape[1]
    nk = win // dil + 1
    kk = jnp.concatenate([k_buf.astype(f32), k_new.astype(f32)], axis=1)
    vv = jnp.concatenate([v_buf.astype(f32), v_new.astype(f32)], axis=1)
    idx = wb + jnp.arange(L)[:, None] - jnp.arange(nk)[None, :] * dil
    valid = idx >= 0
    idx = jnp.maximum(idx, 0)
    kg = kk[:, idx]
    vg = vv[:, idx]
    logits = jnp.einsum('blhd,blkhd->bhlk', q.astype(f32), kg) * (D ** -0.5) + bias_j[:, None, :]
    logits = jnp.where(valid, logits, NEG)
    m = jnp.max(logits, axis=-1, keepdims=True)
    p = jnp.exp(logits - m)
    s = jnp.sum(p, axis=-1, keepdims=True)
    o = jnp.einsum('bhlk,blkhd->blhd', p, vg) / jnp.swapaxes(s, 1, 2)
    lse = jnp.transpose((m + jnp.log(s))[..., 0], (0, 2, 1))
    return o, lse


def dilated_branch(outs, lses, z_b, w_proj_b):
    wts = jax.nn.softmax(jnp.stack(lses, axis=0), axis=0)
    o = jnp.einsum('gblh,gblhd->blhd', wts, jnp.stack(outs, axis=0))
    B, L = o.shape[:2]
    y = o.reshape(B, L, DIL_W) * jax.nn.silu(z_b.astype(jnp.float32))
    return jnp.einsum('ble,ed->bld', y.astype(z_b.dtype), w_proj_b)


def merge_residual(x, p_a, p_b, gate_a, gate_b, w_out, g_post):
    h = jax.nn.sigmoid(gate_a) * p_a + jax.nn.sigmoid(gate_b) * p_b
    out = jnp.einsum('bld,de->ble', h, w_out)
    return x + rms_norm(out, g_post)


def prompt_layer(x, biases, g_pre, w_in, conv_w, a_log, dt_bias, g_head_norm,
                 w_proj_a, w_proj_b, w_out, g_post):
    B, L, _ = x.shape
    qkv_a, z_a, a_a, b_a, qkv_b, z_b, gate_a, gate_b = mixer_inputs(x, g_pre, w_in)
    x_ext = jnp.pad(qkv_a, ((0, 0), (CONV_WIDTH - 1, 0), (0, 0)))
    s0 = jnp.zeros((B, GDN_HEADS, GDN_DK, GDN_DV), jnp.float32)
    p_a, s_fin = gdn_branch(x_ext, s0, z_a, a_a, b_a, conv_w, a_log, dt_bias, g_head_norm, w_proj_a)
    qkv_b = qkv_b.reshape(B, L, N_GROUPS, 3, DIL_HEADS, DIL_HD)
    outs, lses, rows = [], [], []
    for gi, (win, dil) in enumerate(DIL_GROUPS):
        q, k, v = qkv_b[:, :, gi, 0], qkv_b[:, :, gi, 1], qkv_b[:, :, gi, 2]
        o, lse = dilated_attn_prompt(q, k, v, biases[gi], win, dil)
        outs.append(o)
        lses.append(lse)
        keep = min(win, L)
        rows += [k[:, L - keep:], v[:, L - keep:]]
    p_b = dilated_branch(outs, lses, z_b, w_proj_b)
    y = merge_residual(x, p_a, p_b, gate_a, gate_b, w_out, g_post)
    return y, [s_fin, x_ext[:, L:]] + rows


def sample_layer(x, s_gdn, s_conv, k_bufs, v_bufs, biases, g_pre, w_in, conv_w, a_log, dt_bias,
                 g_head_norm, w_proj_a, w_proj_b, w_out, g_post):
    B, L, _ = x.shape
    qkv_a, z_a, a_a, b_a, qkv_b, z_b, gate_a, gate_b = mixer_inputs(x, g_pre, w_in)
    x_ext = jnp.concatenate([s_conv.astype(qkv_a.dtype), qkv_a], axis=1)
    p_a, s_fin = gdn_branch(x_ext, s_gdn, z_a, a_a, b_a, conv_w, a_log, dt_bias, g_head_norm, w_proj_a)
    qkv_b = qkv_b.reshape(B, L, N_GROUPS, 3, DIL_HEADS, DIL_HD)
    outs, lses, rows = [], [], []
    for gi, (win, dil) in enumerate(DIL_GROUPS):
        q, k, v = qkv_b[:, :, gi, 0], qkv_b[:, :, gi, 1], qkv_b[:, :, gi, 2]
        o, lse = dilated_attn_sample(q, k, v, k_bufs[gi], v_bufs[gi], biases[gi], win, dil)
        outs.append(o)
        lses.append(lse)
        rows += [k, v]
    p_b = dilated_branch(outs, lses, z_b, w_proj_b)
    y = merge_residual(x, p_a, p_b, gate_a, gate_b, w_out, g_post)
    return y, [s_fin, x_ext[:, L:]] + rows


def setup_inputs(seed: int = 0) -> dict:
    key = jax.random.key(seed)
    ks = jax.random.split(key, 24)
    f32 = jnp.float32

    def nrm(k, shape, scale):
        return jax.random.normal(k, shape, f32) * scale

    buf = [min(w, PAST_LEN) for w, _ in DIL_GROUPS]
    x_prompt = nrm(ks[0], (BATCH, SEQ, D_MODEL), 1.0)
    x_sample = nrm(ks[1], (DEC_BATCH, DEC_SEQ, D_MODEL), 1.0)
    state_gdn = nrm(ks[2], (DEPTH, DEC_BATCH, GDN_HEADS, GDN_DK, GDN_DV), GDN_DK ** -0.5)
    state_conv = nrm(ks[3], (DEPTH, DEC_BATCH, CONV_WIDTH - 1, CONV_DIM), 1.0)
    cache_k_w128 = nrm(ks[4], (DEPTH, DEC_BATCH, buf[0], DIL_HEADS, DIL_HD), 1.0)
    cache_v_w128 = nrm(ks[5], (DEPTH, DEC_BATCH, buf[0], DIL_HEADS, DIL_HD), 1.0)
    cache_k_w512 = nrm(ks[6], (DEPTH, DEC_BATCH, buf[1], DIL_HEADS, DIL_HD), 1.0)
    cache_v_w512 = nrm(ks[7], (DEPTH, DEC_BATCH, buf[1], DIL_HEADS, DIL_HD), 1.0)
    cache_k_w2048 = nrm(ks[8], (DEPTH, DEC_BATCH, buf[2], DIL_HEADS, DIL_HD), 1.0)
    cache_v_w2048 = nrm(ks[9], (DEPTH, DEC_BATCH, buf[2], DIL_HEADS, DIL_HD), 1.0)
    rel_table = nrm(ks[10], (REL_BUCKETS, N_GROUPS * DIL_HEADS), 0.5)
    g_pre = 1.0 + nrm(ks[11], (DEPTH, D_MODEL), 0.01)
    w_in = nrm(ks[12], (DEPTH, D_MODEL, IN_WIDTH), D_MODEL ** -0.5)
    conv_w = nrm(ks[13], (DEPTH, CONV_DIM, CONV_WIDTH), CONV_WIDTH ** -0.5)
    a_log = jnp.log(jax.random.uniform(ks[14], (DEPTH, GDN_HEADS), f32, 1.0, 16.0))
    dt = jnp.exp(jax.random.uniform(ks[15], (DEPTH, GDN_HEADS), f32, math.log(1e-3), math.log(1e-1)))
    dt_bias = dt + jnp.log(-jnp.expm1(-dt))
    g_head_norm = 1.0 + nrm(ks[16], (DEPTH, GDN_DV), 0.01)
    w_proj_a = nrm(ks[17], (DEPTH, GDN_V, D_MODEL), GDN_V ** -0.5)
    w_proj_b = nrm(ks[18], (DEPTH, DIL_W, D_MODEL), DIL_W ** -0.5)
    w_out = nrm(ks[19], (DEPTH, D_MODEL, D_MODEL), D_MODEL ** -0.5)
    g_post = 1.0 + nrm(ks[20], (DEPTH, D_MODEL), 0.01)
    return {'x_prompt': x_prompt, 'x_sample': x_sample, 'state_gdn': state_gdn,
            'state_conv': state_conv, 'cache_k_w128': cache_k_w128, 'cache_v_w128': cache_v_w128,
            'cache_k_w512': cache_k_w512, 'cache_v_w512': cache_v_w512,
            'cache_k_w2048': cache_k_w2048, 'cache_v_w2048': cache_v_w2048,
            'rel_table': rel_table, 'g_pre': g_pre, 'w_in': w_in, 'conv_w': conv_w,
            'a_log': a_log, 'dt_bias': dt_bias, 'g_head_norm': g_head_norm,
            'w_proj_a': w_proj_a, 'w_proj_b': w_proj_b, 'w_out': w_out, 'g_post': g_post}


def reference(x_prompt, x_sample, state_gdn, state_conv, cache_k_w128, cache_v_w128,
              cache_k_w512, cache_v_w512, cache_k_w2048, cache_v_w2048, rel_table,
              g_pre, w_in, conv_w, a_log, dt_bias, g_head_norm, w_proj_a, w_proj_b, w_out, g_post):
    biases = group_biases(rel_table)
    k_caches = (cache_k_w128, cache_k_w512, cache_k_w2048)
    v_caches = (cache_v_w128, cache_v_w512, cache_v_w2048)
    y_prompt, y_sample = x_prompt, x_sample
    p_lists = [[] for _ in range(2 + 2 * N_GROUPS)]
    s_lists = [[] for _ in range(2 + 2 * N_GROUPS)]
    for l in range(DEPTH):
        lw = (g_pre[l], w_in[l], conv_w[l], a_log[l], dt_bias[l], g_head_norm[l],
              w_proj_a[l], w_proj_b[l], w_out[l], g_post[l])
        y_prompt, p_st = prompt_layer(y_prompt, biases, *lw)
        y_sample, s_st = sample_layer(y_sample, state_gdn[l], state_conv[l],
                                      tuple(c[l] for c in k_caches), tuple(c[l] for c in v_caches),
                                      biases, *lw)
        for lst, st in zip(p_lists, p_st):
            lst.append(st)
        for lst, st in zip(s_lists, s_st):
            lst.append(st)
    p_gdn, p_conv, p_k128, p_v128, p_k512, p_v512, p_k2048, p_v2048 = [jnp.stack(a, 0) for a in p_lists]
    s_gdn, s_conv, s_k128, s_v128, s_k512, s_v512, s_k2048, s_v2048 = [jnp.stack(a, 0) for a in s_lists]
    return (y_prompt, y_sample, p_gdn, p_conv, p_k128, p_v128, p_k512, p_v512, p_k2048, p_v2048,
            s_gdn, s_conv, s_k128, s_v128, s_k512, s_v512, s_k2048, s_v2048)
```

```python
import os
from contextlib import ExitStack
import numpy as np
import concourse.bass as bass
import concourse.mybir as mybir
from concourse.bass_utils import run_bass_kernel_spmd

F32 = mybir.dt.float32
BF16 = mybir.dt.bfloat16
U8 = mybir.dt.uint8
AF = mybir.ActivationFunctionType
ALU = mybir.AluOpType
AX = mybir.AxisListType

NCORES = 8
L = 4096
NSEQ = 2
NS = 16
NTP = NSEQ * L
NT = NTP + NS
DM = 1024
INW = 11280
EPS = 1e-6
NEG = -1e30
DILS = (1, 4, 16)
O_ZA, O_A, O_QKVB, O_ZB, O_GA, O_GB = 3072, 4096, 4112, 8720, 9232, 10256


class Tl:
    __slots__ = ("w", "rd", "excl")

    def __init__(self, excl=False):
        self.w = {}
        self.rd = {}
        self.excl = excl


class V:
    __slots__ = ("t", "ap")

    def __init__(self, t, ap):
        self.t = t
        self.ap = ap

    def __getitem__(self, k):
        return V(self.t, self.ap[k])

    def re(self, s, **kw):
        return V(self.t, self.ap.rearrange(s, **kw))

    def raw(self, dims, off=0):
        return V(self.t, bass.AP(tensor=self.ap.tensor, offset=self.ap.offset + off, ap=dims))

    def bitcast(self, dt):
        return V(self.t, self.ap.bitcast(dt))


NDS = 12


class Prog:
    ENG = ("sp", "pe", "act", "dve", "pool")

    def __init__(self, nc):
        self.nc = nc
        self.streams = {e: [] for e in self.ENG}
        self.count = {e: 0 for e in self.ENG}
        self.known = {e: {} for e in self.ENG}
        self.dma_n = {"sp": 0, "pool": 0, "act": 0}
        self.latest = {}

    def _resolve(self, eng, deps):
        kn = self.known[eng]
        waits = []
        for k, v in deps.items():
            if k == "pe" and eng == "pe":
                continue
            if kn.get(k, 0) >= v:
                continue
            kn[k] = v
            waits.append((k, v))
        return waits

    @staticmethod
    def _deps(reads, writes, acc):
        deps = {}

        def add(d):
            for k, v in d.items():
                if deps.get(k, 0) < v:
                    deps[k] = v
        for t in reads:
            if t is not None:
                add(t.w)
                if t.excl:
                    add(t.rd)
        for t in writes:
            if t is not None:
                if not acc:
                    add(t.w)
                add(t.rd)
        return deps

    def _commit(self, tok, reads, writes, acc):
        k, v = tok
        self.latest[k] = max(self.latest.get(k, 0), v)
        for t in writes:
            if t is None:
                continue
            if acc:
                t.w[k] = max(t.w.get(k, 0), v)
            else:
                t.w = {k: v}
                t.rd = {}
        for t in reads:
            if t is None or t in writes:
                continue
            t.rd[k] = max(t.rd.get(k, 0), v)

    def op(self, eng, fn, reads=(), writes=(), acc=False):
        reads = [r.t if isinstance(r, V) else r for r in reads]
        writes = [r.t if isinstance(r, V) else r for r in writes]
        deps = self._deps(reads, writes, acc)
        waits = self._resolve(eng, deps)
        self.count[eng] += 1
        tok = (eng, self.count[eng])
        self.streams[eng].append((waits, fn, (eng, 1)))
        self._commit(tok, reads, writes, acc)

    def dma(self, q, out, in_, acc=True):
        n = self.dma_n[q]
        self.dma_n[q] += 1
        i = n % NDS
        val = 16 * (n // NDS + 1)
        key = ("d", q, i)
        reads = [in_.t]
        writes = [out.t]
        deps = self._deps(reads, writes, acc)
        if n >= NDS:
            deps[key] = max(deps.get(key, 0), val - 16)
        waits = self._resolve(q, deps)
        oa, ia = out.ap, in_.ap
        self.streams[q].append((waits, lambda e: e.dma_start(out=oa, in_=ia), (key, 16)))
        self._commit((key, val), reads, writes, acc)

    def barrier(self):
        for e in self.ENG:
            waits = self._resolve(e, dict(self.latest))
            if waits:
                self.streams[e].append((waits, None, None))

    def emit(self):
        nc = self.nc
        keys = list(self.ENG[1:]) + [("d", q, i) for q in ("sp", "pool", "act") for i in range(NDS)]
        with ExitStack() as st:
            st.enter_context(nc.allow_non_contiguous_dma(reason="small strided sample-path transfers"))
            sems = {}
            for k in keys:
                nm = k if isinstance(k, str) else "d%s%d" % (k[1], k[2])
                sems[k] = st.enter_context(nc.semaphore("s_" + nm))
            block = st.enter_context(nc.Block())
            decos = {"sp": block.sync, "pe": block.tensor, "act": block.scalar,
                     "dve": block.vector, "pool": block.gpsimd}
            for eng in self.ENG:
                stream = self.streams[eng]

                def body(e, stream=stream):
                    for waits, fn, inc in stream:
                        for k, v in waits:
                            e.wait_ge(sems[k], v)
                        if fn is not None:
                            fn(e).then_inc(sems[inc[0]], inc[1])
                decos[eng](body)


class Arena:
    def __init__(self, nc, nbytes):
        self.t = nc.alloc_sbuf_tensor("arena", [128, nbytes], U8)
        self.ap = self.t.ap()
        self.n = nbytes
        self.off = 0

    def alloc(self, free_shape, dt, parts=128):
        esz = 4 if dt == F32 else 2
        ne = int(np.prod(free_shape))
        nb = (ne * esz + 31) // 32 * 32
        assert self.off + nb <= self.n, "SBUF arena overflow %d + %d > %d" % (self.off, nb, self.n)
        a = self.ap[0:parts, self.off:self.off + ne * esz].bitcast(dt)
        self.off += nb
        if len(free_shape) == 2:
            a = a.rearrange("p (a b) -> p a b", a=free_shape[0])
        elif len(free_shape) == 3:
            a = a.rearrange("p (a b c) -> p a b c", a=free_shape[0], b=free_shape[1])
        return V(Tl(), a)


class Ctx:
    pass


def dram_in(nc, name, shape, dt=F32):
    return V(None, nc.dram_tensor(name, list(shape), dt, kind="ExternalInput").ap())


def dram_out(nc, name, shape, dt=F32):
    return V(None, nc.dram_tensor(name, list(shape), dt, kind="ExternalOutput").ap())


def dram_tmp(nc, name, shape, dt=F32):
    return V(None, nc.dram_tensor(name, list(shape), dt, kind="Internal").ap())


def mm(P, out, lhsT, rhs, start=True, stop=True):
    o, l, r = out.ap, lhsT.ap, rhs.ap
    P.op("pe", lambda e: e.matmul(o, l, r, start=start, stop=stop), [lhsT, rhs], [out])


def tr(P, out, in_, ident):
    o, i, d = out.ap, in_.ap, ident.ap
    P.op("pe", lambda e: e.transpose(o, i, d), [in_, ident], [out])


def act(P, out, in_, func, bias=0.0, scale=1.0, eng="act"):
    o, i = out.ap, in_.ap
    rd = [in_]
    b = bias
    if isinstance(bias, V):
        rd.append(bias)
        b = bias.ap
    s = scale
    if isinstance(scale, V):
        rd.append(scale)
        s = scale.ap
    P.op("act", lambda e: e.activation(o, i, func, bias=b, scale=s), rd, [out])


def cp(P, eng, out, in_):
    o, i = out.ap, in_.ap
    if eng == "act":
        P.op("act", lambda e: e.copy(o, i), [in_], [out])
    else:
        P.op(eng, lambda e: e.tensor_copy(o, i), [in_], [out])


def tt(P, eng, out, in0, in1, op):
    o, a, b = out.ap, in0.ap, in1.ap
    P.op(eng, lambda e: e.tensor_tensor(o, a, b, op), [in0, in1], [out])


def ts(P, eng, out, in0, s1, op0, s2=None, op1=None):
    o, a = out.ap, in0.ap
    rd = [in0]
    x1 = s1
    if isinstance(s1, V):
        rd.append(s1)
        x1 = s1.ap
    x2 = s2
    if isinstance(s2, V):
        rd.append(s2)
        x2 = s2.ap
    if op1 is None:
        P.op(eng, lambda e: e.tensor_scalar(o, a, x1, None, op0), rd, [out])
    else:
        P.op(eng, lambda e: e.tensor_scalar(o, a, x1, x2, op0, op1), rd, [out])


def stt(P, eng, out, in0, scalar, in1, op0, op1):
    o, a, b = out.ap, in0.ap, in1.ap
    rd = [in0, in1]
    s = scalar
    if isinstance(scalar, V):
        rd.append(scalar)
        s = scalar.ap
    P.op(eng, lambda e: e.scalar_tensor_tensor(o, a, s, b, op0, op1), rd, [out])


def memset(P, eng, out, val):
    o = out.ap
    P.op(eng, lambda e: e.memset(o, val), [], [out])


def rsum(P, eng, out, in_):
    o, i = out.ap, in_.ap
    P.op(eng, lambda e: e.reduce_sum(o, i, AX.X), [in_], [out])


def recip(P, out, in_):
    o, i = out.ap, in_.ap
    P.op("dve", lambda e: e.reciprocal(o, i), [in_], [out])


def rsqrt(P, out, in_, eps, scale=1.0):
    act(P, out, in_, AF.Sqrt, bias=eps, scale=scale)
    recip(P, out, out)


class Rot:
    def __init__(self, items):
        self.items = items
        self.i = 0

    def next(self):
        v = self.items[self.i % len(self.items)]
        self.i += 1
        return v


def build(phases="ABCD"):
    nc = bass.Bass("TRN2", target_bir_lowering=False)
    P = Prog(nc)
    C = Ctx()
    C.nc, C.P = nc, P
    I = {}
    I["x_p"] = dram_in(nc, "x_p", [NTP, DM])
    I["x_s"] = dram_in(nc, "x_s", [NS, DM])
    I["state_gdn"] = dram_in(nc, "state_gdn", [NS, 8, 128, 128])
    I["state_conv"] = dram_in(nc, "state_conv", [NS, 3, 3072])
    for g, w in enumerate((128, 512, 2048)):
        I["ck%d" % g] = dram_in(nc, "ck%d" % g, [NS, w, 512])
        I["cv%d" % g] = dram_in(nc, "cv%d" % g, [NS, w, 512])
    I["rel_table"] = dram_in(nc, "rel_table", [32, 12])
    I["g_pre"] = dram_in(nc, "g_pre", [1, DM])
    I["w_in"] = dram_in(nc, "w_in", [DM, INW])
    I["conv_w"] = dram_in(nc, "conv_w", [3072, 4])
    I["a_log"] = dram_in(nc, "a_log", [1, 8])
    I["dt_bias"] = dram_in(nc, "dt_bias", [1, 8])
    I["g_head_norm"] = dram_in(nc, "g_head_norm", [1, 128])
    I["w_proj_a"] = dram_in(nc, "w_proj_a", [1024, DM])
    I["w_proj_b"] = dram_in(nc, "w_proj_b", [512, DM])
    I["w_out"] = dram_in(nc, "w_out", [DM, DM])
    I["g_post"] = dram_in(nc, "g_post", [1, DM])
    I["c_ident"] = dram_in(nc, "c_ident", [128, 128])
    I["c_umat"] = dram_in(nc, "c_umat", [128, 128])
    I["c_maskT"] = dram_in(nc, "c_maskT", [128, 128])
    I["c_strictT"] = dram_in(nc, "c_strictT", [128, 128])
    I["c_oh"] = dram_in(nc, "c_oh", [3, 32, 384])
    I["c_negm"] = dram_in(nc, "c_negm", [128, 384])
    I["c_ohs"] = dram_in(nc, "c_ohs", [3, 32, 128])
    I["c_oh0"] = dram_in(nc, "c_oh0", [3, 32, 1])
    O = {}
    O["y_p"] = dram_out(nc, "y_p", [NTP, DM])
    O["y_s"] = dram_out(nc, "y_s", [NS, DM])
    O["p_gdn"] = dram_out(nc, "p_gdn", [NSEQ, 8, 128, 128])
    O["p_conv"] = dram_out(nc, "p_conv", [NSEQ, 3, 3072])
    for g, w in enumerate((128, 512, 2048)):
        O["p_k%d" % g] = dram_out(nc, "p_k%d" % g, [NSEQ, w, 512])
        O["p_v%d" % g] = dram_out(nc, "p_v%d" % g, [NSEQ, w, 512])
        O["s_k%d" % g] = dram_out(nc, "s_k%d" % g, [NS, 512])
        O["s_v%d" % g] = dram_out(nc, "s_v%d" % g, [NS, 512])
    O["s_gdn"] = dram_out(nc, "s_gdn", [NS, 8, 128, 128])
    O["s_conv"] = dram_out(nc, "s_conv", [NS, 3, 3072])
    S = {}
    S["qT"] = dram_tmp(nc, "s_qT", [1024, NT])
    S["kT"] = dram_tmp(nc, "s_kT", [1024, NT])
    S["vT"] = dram_tmp(nc, "s_vT", [1024, NT])
    S["zaT"] = dram_tmp(nc, "s_zaT", [1024, NT])
    S["zbT"] = dram_tmp(nc, "s_zbT", [512, NT])
    S["gaT"] = dram_tmp(nc, "s_gaT", [1024, NT])
    S["gbT"] = dram_tmp(nc, "s_gbT", [1024, NT])
    S["gbeta"] = dram_tmp(nc, "s_gbeta", [NT, 16])
    for g in range(3):
        S["qb%d" % g] = dram_tmp(nc, "s_qb%d" % g, [512, NTP], BF16)
        S["kb%d" % g] = dram_tmp(nc, "s_kb%d" % g, [512, NTP], BF16)
        S["vb%d" % g] = dram_tmp(nc, "s_vb%d" % g, [NTP, 512], BF16)
    S["vs"] = dram_tmp(nc, "s_vs", [NS, 3, 512])
    S["yaT"] = dram_tmp(nc, "s_yaT", [1024, NT], BF16)
    S["ybT"] = dram_tmp(nc, "s_ybT", [512, NT], BF16)
    S["bias"] = dram_tmp(nc, "s_bias", [12, 256, 384])
    C.I, C.O, C.S = I, O, S

    A = Arena(nc, 206 * 1024)
    C.A = A
    pst = nc.alloc_psum_tensor("psum", [128, 8, 512], F32)
    psa = pst.ap()
    C.banks = Rot([V(Tl(excl=True), psa[:, b, :]) for b in range(8)])

    K = Ctx()
    C.K = K
    K.ident = A.alloc([128], F32)
    K.identb = A.alloc([128], BF16)
    K.umat = A.alloc([128], F32)
    K.maskT = A.alloc([128], F32)
    K.strictT = A.alloc([128], F32)
    K.ones = A.alloc([128], F32)
    K.onesb = A.alloc([128], BF16)
    K.meanm = A.alloc([128], F32)
    K.c128 = A.alloc([128], F32)
    K.qs = A.alloc([12, NS], F32)
    K.ks = A.alloc([12, NS], F32)
    P.dma("sp", K.ident, I["c_ident"])
    P.dma("pool", K.identb, I["c_ident"])
    P.dma("sp", K.umat, I["c_umat"])
    P.dma("sp", K.maskT, I["c_maskT"])
    P.dma("sp", K.strictT, I["c_strictT"])
    memset(P, "pool", K.ones, 1.0)
    memset(P, "pool", K.onesb, 1.0)
    memset(P, "pool", K.meanm, 1.0 / 128.0)
    memset(P, "pool", K.c128, 128.0)
    C.mark0 = A.off

    if "A" in phases:
        phase_a(C)
    P.barrier()
    A.off = C.mark0
    if "B" in phases:
        phase_b(C)
    P.barrier()
    A.off = C.mark0
    if "C" in phases:
        phase_c(C)
    P.barrier()
    A.off = C.mark0
    if "D" in phases:
        phase_d(C)
    P.barrier()
    P.emit()
    return nc


def bcast_rows(v, n):
    return v.raw([[0, 128], [1, n]])


def phase_a(C):
    P, A, I, O, S, K = C.P, C.A, C.I, C.O, C.S, C.K
    banks = C.banks
    xnT = A.alloc([8, NT], BF16)
    gpre = A.alloc([DM], F32)
    convw = A.alloc([24, 4], F32)
    P.dma("sp", gpre, bcast_rows(I["g_pre"], DM))
    P.dma("sp", convw, I["conv_w"].re("(c p) w -> p c w", p=128))

    stT = A.alloc([24, 3, NS], F32)
    mark1 = A.off
    xts = Rot([A.alloc([DM], F32) for _ in range(3)])
    sqs = Rot([A.alloc([DM], F32) for _ in range(2)])
    xns = Rot([A.alloc([DM], BF16) for _ in range(2)])
    sts = Rot([A.alloc([4], F32) for _ in range(4)])
    DBG = int(os.environ.get("MK_DBG", "9"))
    for sub in (range(NTP // 128 + 1) if DBG >= 2 else []):
        if sub < NTP // 128:
            np_, src, t0 = 128, I["x_p"][sub * 128:(sub + 1) * 128, :], sub * 128
        else:
            np_, src, t0 = NS, I["x_s"], NTP
        xt = xts.next()[0:np_]
        sq = sqs.next()[0:np_]
        xn = xns.next()[0:np_]
        stt_ = sts.next()[0:np_]
        P.dma("sp", xt, src)
        act(P, sq, xt, AF.Square)
        rsum(P, "dve", stt_[:, 0:1], sq)
        rsqrt(P, stt_[:, 2:3], stt_[:, 0:1], EPS, 1.0 / DM)
        stt(P, "dve", xn, xt, stt_[:, 2:3], gpre[0:np_], ALU.mult, ALU.mult)
        bk = banks.next()
        bkb = bk.bitcast(BF16)
        for kc in range(8):
            tr(P, bkb[:, kc * 128:kc * 128 + np_], xn[:, kc * 128:(kc + 1) * 128], K.identb[0:np_, 0:np_])
        src_ps = bkb.re("p (k t) -> p k t", k=8)[:, :, 0:np_]
        cp(P, "act" if sub % 2 else "dve", xnT[:, :, t0:t0 + np_], src_ps)
    stin = A.alloc([3072], F32)
    P.dma("sp", stin[0:48], I["state_conv"].re("b r c -> (b r) c"))
    for c4 in range(6 if DBG >= 3 else 0):
        bk = banks.next()
        for m in range(4):
            c = c4 * 4 + m
            tr(P, bk[:, m * 48:(m + 1) * 48], stin[0:48, c * 128:(c + 1) * 128], K.ident[0:48, 0:48])
        cp(P, "dve", stT[:, c4 * 4:(c4 + 1) * 4, :, :].re("p c r b -> p c b r"),
           bk[:, 0:192].re("p (c b r) -> p c b r", c=4, b=NS))
    P.barrier()
    A.off = mark1

    wbs = Rot([A.alloc([8, 512], BF16) for _ in range(2)])
    w_view = I["w_in"].re("(k p) n -> p k n", p=128)

    def load_w(col0, width):
        wb = wbs.next()
        P.dma("pool", wb[:, :, 0:width], w_view[:, :, col0:col0 + width], acc=False)
        return wb

    def fm(bank, wb, m, t0, n):
        for kc in range(8):
            mm(P, bank[:, 0:n], wb[:, kc, m * 128:(m + 1) * 128], xnT[:, kc, t0:t0 + n], kc == 0, kc == 7)

    def tm(bank, wb, tok, np_, width=512):
        for kc in range(8):
            mm(P, bank[0:np_, 0:width], tok(kc), wb[:, kc, 0:width], kc == 0, kc == 7)

    TILES = [(ti * 512, 512) for ti in range(NTP // 512)] + [(NTP, NS)]
    evi = [0]

    def evac_eng():
        evi[0] += 1
        return "act" if evi[0] % 2 else "dve"

    stage = Rot([A.alloc([515], F32) for _ in range(3)])
    cvs = Rot([A.alloc([512], F32) for _ in range(2)])
    svs = Rot([A.alloc([512], F32) for _ in range(2)])
    sqq = Rot([A.alloc([512], F32) for _ in range(2)])
    rss = Rot([A.alloc([512], F32) for _ in range(2)])
    osb = Rot([A.alloc([512], F32) for _ in range(3)])
    tmb = Rot([A.alloc([512], F32) for _ in range(2)])
    if DBG >= 4:
        P.dma("sp", O["s_conv"][:, 0:2, :], I["state_conv"][:, 1:3, :])

    SUB = os.environ.get("MK_SUB", "conv,simple,ab,att").split(",")
    for j in range(6 if "conv" in SUB else 0):
        wb = load_w(j * 512, 512)
        for m in range(4):
            c = j * 4 + m
            kind = "q" if c < 8 else ("k" if c < 16 else "v")
            dst = S["qT"] if c < 8 else (S["kT"] if c < 16 else S["vT"])
            r0 = (c % 8) * 128
            prev = None
            for (t0, n) in TILES:
                bk = banks.next()
                fm(bk, wb, m, t0, n)
                cv = cvs.next()[:, 0:n]
                if n == 512:
                    sg = stage.next()
                    cp(P, "act", sg[:, 3:515], bk[:, 0:512])
                    if t0 % L == 0:
                        memset(P, "pool", sg[:, 0:3], 0.0)
                    else:
                        cp(P, "pool", sg[:, 0:3], prev[:, 512:515])
                    prev = sg
                    taps = [sg[:, w:w + 512] for w in range(4)]
                else:
                    sg = stage.next()
                    cp(P, "act", sg[:, 0:n], bk[:, 0:n])
                    taps = [stT[:, c, 0, :], stT[:, c, 1, :], stT[:, c, 2, :], sg[:, 0:n]]
                ts(P, "dve", cv, taps[0], convw[:, c, 0:1], ALU.mult)
                for w in range(1, 4):
                    stt(P, "dve", cv, taps[w], convw[:, c, w:w + 1], cv, ALU.mult, ALU.add)
                sv = svs.next()[:, 0:n]
                act(P, sv, cv, AF.Silu)
                if kind == "v":
                    P.dma("sp", dst[r0:r0 + 128, t0:t0 + n], sv)
                else:
                    sq = sqq.next()[:, 0:n]
                    tt(P, "pool", sq, sv, sv, ALU.mult)
                    b2 = banks.next()
                    mm(P, b2[:, 0:n], K.c128 if kind == "q" else K.ones, sq)
                    rs = rss.next()[:, 0:n]
                    rsqrt(P, rs, b2[:, 0:n], EPS * (128.0 if kind == "q" else 1.0))
                    ob = osb.next()[:, 0:n]
                    tt(P, "pool", ob, sv, rs, ALU.mult)
                    P.dma("sp", dst[r0:r0 + 128, t0:t0 + n], ob)
        for s in range(NSEQ):
            bk = banks.next()
            tm(bk, wb, lambda kc, s=s: xnT[:, kc, s * L + L - 128:s * L + L], 128)
            tb = tmb.next()
            cp(P, evac_eng(), tb, bk)
            P.dma("sp", O["p_conv"][s, :, j * 512:(j + 1) * 512], tb[125:128, :])
        bk = banks.next()
        tm(bk, wb, lambda kc: xnT[:, kc, NTP:NT], NS)
        tb = tmb.next()
        cp(P, evac_eng(), tb[0:NS], bk[0:NS])
        P.dma("sp", O["s_conv"][:, 2, j * 512:(j + 1) * 512], tb[0:NS, :])

    def simple_block(col0, dst, r0, func):
        wb = load_w(col0, 512)
        for m in range(4):
            for (t0, n) in TILES:
                bk = banks.next()
                fm(bk, wb, m, t0, n)
                ob = osb.next()[:, 0:n]
                act(P, ob, bk[:, 0:n], func)
                P.dma("sp", dst[r0 + m * 128:r0 + (m + 1) * 128, t0:t0 + n], ob)

    if "simple" in SUB:
        for j in range(2):
            simple_block(O_ZA + j * 512, S["zaT"], j * 512, AF.Silu)
        simple_block(O_ZB, S["zbT"], 0, AF.Silu)
        for j in range(2):
            simple_block(O_GA + j * 512, S["gaT"], j * 512, AF.Sigmoid)
        for j in range(2):
            simple_block(O_GB + j * 512, S["gbT"], j * 512, AF.Sigmoid)

    wb = load_w(O_A, 16)
    dtb = A.alloc([4, 8], F32)
    nega = A.alloc([4, 8], F32)
    for q in range(4):
        P.dma("sp", dtb[:, q, :], bcast_rows(I["dt_bias"], 8))
        P.dma("sp", nega[:, q, :], bcast_rows(I["a_log"], 8))
    act(P, nega, nega, AF.Exp)
    ts(P, "dve", nega, nega, -1.0, ALU.mult)
    abt = Rot([A.alloc([6, 4, 8], F32) for _ in range(2)])
    gbs = Rot([A.alloc([4, 16], F32) for _ in range(2)])
    for (t0, n) in (TILES if "ab" in SUB else []):
        nsub = 4 if n == 512 else 1
        np_ = 128 if n == 512 else NS
        bk = banks.next()
        for sb in range(nsub):
            for kc in range(8):
                mm(P, bk[0:np_, sb * 16:(sb + 1) * 16], xnT[:, kc, t0 + sb * 128:t0 + sb * 128 + np_],
                   wb[:, kc, 0:16], kc == 0, kc == 7)
        pv = bk[0:np_, 0:nsub * 16].re("p (s c) -> p s c", c=16)
        w_ = abt.next()[0:np_, :, 0:nsub, :]
        gb = gbs.next()[0:np_, 0:nsub, :]
        xx, ax, ee, ll = w_[:, 0], w_[:, 1], w_[:, 2], w_[:, 3]
        tt(P, "dve", xx, pv[:, :, 0:8], dtb[0:np_, 0:nsub, :], ALU.add)
        act(P, ax, xx, AF.Abs)
        act(P, ee, ax, AF.Exp, scale=-1.0)
        act(P, ll, ee, AF.Ln, bias=1.0)
        stt(P, "dve", xx, xx, 0.0, ll, ALU.max, ALU.add)
        tt(P, "dve", gb[:, :, 0:8], xx, nega[0:np_, 0:nsub, :], ALU.mult)
        act(P, gb[:, :, 8:16], pv[:, :, 8:16], AF.Sigmoid)
        if n == 512:
            P.dma("sp", S["gbeta"][t0:t0 + 512, :].re("(s p) c -> p s c", p=128), gb)
        else:
            P.dma("sp", S["gbeta"][t0:t0 + NS, :], gb[:, 0, :])

    stgb = Rot([A.alloc([2048], BF16) for _ in range(2)])
    vbb = Rot([A.alloc([512], BF16) for _ in range(3)])
    GS = [int(x) for x in os.environ.get("MK_G", "0,1,2").split(",")]
    for g, dil in (enumerate(DILS) if "att" in SUB else []):
        if g not in GS:
            continue
        lc = L // dil
        spc = 2048 // dil
        ATT = os.environ.get("MK_ATT", "qk,ktm,v").split(",")
        for t, nm in (((0, "qb"), (1, "kb")) if "qk" in ATT else []):
            wb = load_w(O_QKVB + g * 1536 + t * 512, 512)
            dstT = S["%s%d" % (nm, g)]
            for h in range(4):
                for spn in range(NTP // 2048):
                    sgb = stgb.next()
                    for sb in range(4):
                        bk = banks.next()
                        fm(bk, wb, h, spn * 2048 + sb * 512, 512)
                        i0 = sb * 512 // dil
                        cp(P, evac_eng(), sgb.re("p (r i) -> p r i", r=dil)[:, :, i0:i0 + 512 // dil],
                           bk.re("p (i r) -> p r i", r=dil))
                    s_, n_ = spn // 2, spn % 2
                    d = dstT[h * 128:(h + 1) * 128, s_ * L:(s_ + 1) * L].re("p (r i) -> p r i", r=dil)
                    P.dma("sp", d[:, :, n_ * spc:(n_ + 1) * spc], sgb.re("p (r i) -> p r i", r=dil))
                bk = banks.next()
                fm(bk, wb, h, NTP, NS)
                cp(P, evac_eng(), (K.qs if t == 0 else K.ks)[:, g * 4 + h, :], bk[:, 0:NS])
            if t == 1 and "ktm" in ATT:
                for s in range(NSEQ):
                    for r in range(dil):
                        base = s * L + L - 128 * dil + r
                        bk = banks.next()
                        tm(bk, wb, lambda kc, base=base: xnT[:, kc, base:base + 128 * dil:dil], 128)
                        tb = tmb.next()
                        cp(P, evac_eng(), tb, bk)
                        P.dma("sp", O["p_k%d" % g][s, r::dil, :], tb)
                bk = banks.next()
                tm(bk, wb, lambda kc: xnT[:, kc, NTP:NT], NS)
                tb = tmb.next()
                cp(P, evac_eng(), tb[0:NS], bk[0:NS])
                P.dma("sp", O["s_k%d" % g], tb[0:NS])
        if "v" not in ATT:
            continue
        wb = load_w(O_QKVB + g * 1536 + 1024, 512)
        nb = lc // 128
        MKV = os.environ.get("MK_V", "main,pv,samp").split(",")
        for s in range(NSEQ if "main" in MKV else 0):
            for r in range(dil):
                for n in range(nb):
                    base = s * L + n * 128 * dil + r
                    bk = banks.next()
                    tm(bk, wb, lambda kc, base=base: xnT[:, kc, base:base + 128 * dil:dil], 128)
                    vb = vbb.next()
                    cp(P, evac_eng(), vb, bk)
                    row0 = s * L + r * lc + n * 128
                    P.dma("sp", S["vb%d" % g][row0:row0 + 128, :], vb)
                    if n == nb - 1 and "pv" in MKV:
                        tb = tmb.next()
                        cp(P, evac_eng(), tb, bk)
                        P.dma("sp", O["p_v%d" % g][s, r::dil, :], tb)
        if "samp" not in MKV:
            continue
        bk = banks.next()
        tm(bk, wb, lambda kc: xnT[:, kc, NTP:NT], NS)
        tb = tmb.next()
        cp(P, evac_eng(), tb[0:NS], bk[0:NS])
        P.dma("sp", O["s_v%d" % g], tb[0:NS])
        P.dma("sp", S["vs"][:, g, :], tb[0:NS])


def bc(v, dims):
    p = v.ap.ap[0]
    return v.raw([[p[0], p[1]]] + dims)


def gdn_stream(C, T, c, cols, Sst, first_zero):
    P, K, S = C.P, C.K, C.S
    banks = C.banks
    kT_v = S["kT"].re("(h p) t -> p h t", p=128)
    qT_v = S["qT"].re("(h p) t -> p h t", p=128)
    vT_v = S["vT"].re("(h p) t -> p h t", p=128)
    za_v = S["zaT"].re("(h p) t -> p h t", p=128)
    ya_v = S["yaT"].re("(h p) t -> p h t", p=128)
    ghn = T["ghn"]
    for ci, col0 in enumerate(cols):
        kq = T["kq"][ci % 2][:, :, :, 0:c]
        vT = T["vT"][ci % 2][:, :, 0:c]
        za = T["za"][:, :, 0:c]
        gb = T["gb"][ci % 2][0:c]
        sm = T["sm"]
        P.dma("sp", kq[:, :, 0, :], kT_v[:, :, col0:col0 + c])
        P.dma("sp", kq[:, :, 1, :], qT_v[:, :, col0:col0 + c])
        P.dma("sp", vT, vT_v[:, :, col0:col0 + c])
        P.dma("sp", gb, S["gbeta"][col0:col0 + c, :])
        P.dma("sp", za, za_v[:, :, col0:col0 + c])
        t = [x[0:c, :, 0:c] for x in T["t"]]
        tf = [x[:, :, 0:c] for x in T["t"]]
        td = [x[0:c] for x in T["t"]]
        bk = banks.next()
        mm(P, bk[0:c, 0:8], K.umat[0:c, 0:c], gb[:, 0:8])
        mm(P, bk[:, 8:16], K.ones[0:c, :], gb[:, 0:8])
        cp(P, "dve", sm[0:c, 0:8], bk[0:c, 0:8])
        cp(P, "dve", sm[:, 8:16], bk[:, 8:16])
        tt(P, "pool", sm[0:c, 16:24], sm[0:c, 8:16], sm[0:c, 0:8], ALU.subtract)
        act(P, sm[0:c, 16:24], sm[0:c, 16:24], AF.Exp)
        act(P, sm[:, 24:32], sm[:, 8:16], AF.Exp)
        ts(P, "pool", sm[0:c, 32:40], gb[:, 8:16], -1.0, ALU.mult)
        yield
        Ug = t[0]
        tt(P, "dve", Ug, bc(K.umat[0:c, 0:c], [[0, 8], [1, c]]), bc(gb[:, 0:8], [[1, 8], [0, c]]), ALU.mult)
        bA, bB = banks.next(), banks.next()
        mm(P, bA[:, 0:4 * c].re("p (h i) -> p h i", h=4), K.ones[0:c, :], Ug[:, 0:4, :])
        mm(P, bB[:, 0:4 * c].re("p (h i) -> p h i", h=4), K.ones[0:c, :], Ug[:, 4:8, :])
        EGb = tf[2]
        act(P, EGb[:, 0:4, :], bA[:, 0:4 * c].re("p (h i) -> p h i", h=4), AF.Exp)
        act(P, EGb[:, 4:8, :], bB[:, 0:4 * c].re("p (h i) -> p h i", h=4), AF.Exp)
        dT = t[1]
        mk = bc(K.maskT[0:c, 0:c], [[0, 4], [1, c]])
        for hb, bX in ((0, bA), (4, bB)):
            for h in range(hb, hb + 4):
                stt(P, "dve", dT[:, h, :], bX[0:c, (h - hb) * c:(h - hb + 1) * c], sm[0:c, h:h + 1],
                    K.maskT[0:c, 0:c], ALU.subtract, ALU.add)
        decT = t[3]
        act(P, decT, dT, AF.Exp)
        DSb = t[4]
        tt(P, "pool", DSb, decT, bc(K.strictT[0:c, 0:c], [[0, 8], [1, c]]), ALU.mult)
        tt(P, "pool", DSb, DSb, bc(sm[0:c, 32:40], [[1, 8], [0, c]]), ALU.mult)
        kqd = T["kqd"][:, :, :, 0:c]
        tt(P, "pool", kqd, kq, bc(EGb, [[EGb.ap.ap[1][0], 8], [0, 2], [1, c]]), ALU.mult)
        yield
        vtok, kdec = td[5], td[6]
        for src, dst, scale in ((vT, vtok, None), (kq[:, :, 0, :], kdec, True)):
            for hb in (0, 4):
                bk = banks.next()
                for h in range(hb, hb + 4):
                    tr(P, bk[0:c, (h - hb) * 128:(h - hb + 1) * 128], src[:, h, :], K.ident)
                pv = bk[0:c, :].re("p (h d) -> p h d", h=4)
                if scale is None:
                    cp(P, "act", dst[:, hb:hb + 4, :], pv)
                else:
                    tt(P, "dve", dst[:, hb:hb + 4, :], pv, bc(sm[0:c, 16 + hb:20 + hb], [[1, 4], [0, 128]]), ALU.mult)
        yield
        pa, pb = T["pa"][0:c, :, :, 0:c], T["pb"][0:c, :, :, 0:c]
        qkT = t[9]
        for h2 in range(4):
            bk = banks.next()
            for hh in range(2):
                h = h2 * 2 + hh
                mm(P, bk[0:c, hh * 2 * c:(hh + 1) * 2 * c].re("p (k i) -> p k i", k=2), kq[:, h, 0, :], kq[:, h, :, :])
            pv = bk[0:c, 0:4 * c].re("p (h k i) -> p h k i", h=2, k=2)
            tt(P, "dve", pa[:, h2 * 2:h2 * 2 + 2, 0, :], pv[:, :, 0, :], DSb[:, h2 * 2:h2 * 2 + 2, :], ALU.mult)
            tt(P, "dve", qkT[:, h2 * 2:h2 * 2 + 2, :], pv[:, :, 1, :], decT[:, h2 * 2:h2 * 2 + 2, :], ALU.mult)
        yield
        Pm = t[8]
        if c > 1:
            for hb in (0, 4):
                bk = banks.next()
                for h in range(hb, hb + 4):
                    tr(P, bk[0:c, (h - hb) * c:(h - hb + 1) * c], pa[:, h, 0, :], K.ident[0:c, 0:c])
                cp(P, "act", pa[:, hb:hb + 4, 1, :], bk[0:c, 0:4 * c].re("p (h i) -> p h i", h=4))
            tt(P, "pool", Pm, pa[:, :, 0, :], bc(K.ident[0:c, 0:c], [[0, 8], [1, c]]), ALU.add)
            yield
            cur, nxt = pa, pb
            for lvl in range(1, 7):
                for h2 in range(4):
                    bk = banks.next()
                    for hh in range(2):
                        h = h2 * 2 + hh
                        if lvl < 6:
                            mm(P, bk[0:c, hh * 2 * c:hh * 2 * c + c], cur[:, h, 1, :], cur[:, h, 0, :])
                        mm(P, bk[0:c, hh * 2 * c + c:(hh + 1) * 2 * c], cur[:, h, 0, :], cur[:, h, 1, :])
                    pv = bk[0:c, 0:4 * c].re("p (h k i) -> p h k i", h=2, k=2)
                    if lvl < 6:
                        cp(P, "act" if h2 % 2 else "dve", nxt[:, h2 * 2:h2 * 2 + 2, :, :], pv)
                    else:
                        cp(P, "act" if h2 % 2 else "dve", nxt[:, h2 * 2:h2 * 2 + 2, 1, :], pv[:, :, 1, :])
                yield
                for hb in (0, 4):
                    bk = banks.next()
                    for h in range(hb, hb + 4):
                        mm(P, bk[0:c, (h - hb) * c:(h - hb + 1) * c], nxt[:, h, 1, :], Pm[:, h, :])
                    tt(P, "dve", Pm[:, hb:hb + 4, :], Pm[:, hb:hb + 4, :],
                       bk[0:c, 0:4 * c].re("p (h i) -> p h i", h=4), ALU.add)
                yield
                cur, nxt = nxt, cur
        else:
            memset(P, "pool", Pm, 1.0)
        if ci == 0 and first_zero:
            memset(P, "pool", Sst, 0.0)
        R = td[0]
        for hb in (0, 4):
            bk = banks.next()
            for h in range(hb, hb + 4):
                mm(P, bk[0:c, (h - hb) * 128:(h - hb + 1) * 128], kqd[:, h, 0, :], Sst[:, h, :])
            tt(P, "dve", R[:, hb:hb + 4, :], vtok[:, hb:hb + 4, :], bk[0:c, :].re("p (h d) -> p h d", h=4), ALU.subtract)
        yield
        vn = td[1]
        for hb in (0, 4):
            bk = banks.next()
            for h in range(hb, hb + 4):
                mm(P, bk[0:c, (h - hb) * 128:(h - hb + 1) * 128], Pm[:, h, :], R[:, h, :])
            tt(P, "dve", vn[:, hb:hb + 4, :], bk[0:c, :].re("p (h d) -> p h d", h=4),
               bc(gb[:, 8 + hb:12 + hb], [[1, 4], [0, 128]]), ALU.mult)
        yield
        oT = tf[5]
        for hb in (0, 4):
            bk = banks.next()
            for h in range(hb, hb + 4):
                o_ = bk[:, (h - hb) * c:(h - hb + 1) * c]
                mm(P, o_, Sst[:, h, :], kqd[:, h, 1, :], True, False)
                mm(P, o_, vn[:, h, :], qkT[:, h, :], False, True)
            cp(P, "act", oT[:, hb:hb + 4, :], bk[:, 0:4 * c].re("p (h i) -> p h i", h=4))
        yield
        bks = []
        for hb in (0, 4):
            bk = banks.next()
            bks.append(bk)
            for h in range(hb, hb + 4):
                mm(P, bk[:, (h - hb) * 128:(h - hb + 1) * 128], kdec[:, h, :], vn[:, h, :])
        tt(P, "pool", Sst, Sst, bc(sm[:, 24:32], [[1, 8], [0, 128]]), ALU.mult)
        for hb, bk in zip((0, 4), bks):
            tt(P, "dve", Sst[:, hb:hb + 4, :], Sst[:, hb:hb + 4, :], bk.re("p (h d) -> p h d", h=4), ALU.add)
        yield
        sq = tf[2]
        tt(P, "pool", sq, oT, oT, ALU.mult)
        rs = tf[3]
        for hb in (0, 4):
            bk = banks.next()
            mm(P, bk[:, 0:4 * c].re("p (h i) -> p h i", h=4), K.meanm, sq[:, hb:hb + 4, :])
            rsqrt(P, rs[:, hb:hb + 4, :], bk[:, 0:4 * c].re("p (h i) -> p h i", h=4), EPS)
        y1 = tf[4]
        stt(P, "dve", y1, oT, ghn[:, 0:1], rs, ALU.mult, ALU.mult)
        yb = T["yb"][:, :, 0:c]
        tt(P, "pool", yb, y1, za, ALU.mult)
        P.dma("sp", ya_v[:, :, col0:col0 + c], yb)
        yield


def gdn_tiles(C):
    A = C.A
    T = {}
    T["kq"] = [A.alloc([8, 2, 128], F32) for _ in range(2)]
    T["vT"] = [A.alloc([8, 128], F32) for _ in range(2)]
    T["gb"] = [A.alloc([16], F32) for _ in range(2)]
    T["za"] = A.alloc([8, 128], F32)
    T["sm"] = A.alloc([40], F32)
    T["t"] = [A.alloc([8, 128], F32) for _ in range(10)]
    T["kqd"] = A.alloc([8, 2, 128], F32)
    T["pa"] = A.alloc([8, 2, 128], F32)
    T["pb"] = A.alloc([8, 2, 128], F32)
    T["yb"] = A.alloc([8, 128], BF16)
    T["S"] = A.alloc([8, 128], F32)
    return T


def run_interleaved(gens):
    gens = list(gens)
    while gens:
        alive = []
        for g in gens:
            try:
                next(g)
                alive.append(g)
            except StopIteration:
                pass
        gens = alive


def phase_b(C):
    P, A, I, O, S, K = C.P, C.A, C.I, C.O, C.S, C.K
    ghn = A.alloc([1], F32)
    P.dma("sp", ghn, I["g_head_norm"].re("o d -> d o"))
    Ts = [gdn_tiles(C) for _ in range(2)]
    for T in Ts:
        T["ghn"] = ghn
    MODE = os.environ.get("MK_B", "prompt,sample").split(",")
    NCH = int(os.environ.get("MK_NCH", str(L // 128)))
    if "prompt" in MODE:
        gens = [gdn_stream(C, Ts[s], 128, [s * L + ch * 128 for ch in range(NCH)], Ts[s]["S"], True)
                for s in range(NSEQ)]
        run_interleaved(gens)
        for s in range(NSEQ):
            P.dma("sp", O["p_gdn"][s].re("h k v -> k h v"), Ts[s]["S"])
    if "sample" in MODE:
        def sample_gen(T, bs):
            for b in bs:
                P.dma("sp", T["S"], I["state_gdn"][b].re("h k v -> k h v"), acc=False)
                yield from gdn_stream(C, T, 1, [NTP + b], T["S"], False)
                P.dma("sp", O["s_gdn"][b].re("h k v -> k h v"), T["S"])
        run_interleaved([sample_gen(Ts[0], range(0, NS, 2)), sample_gen(Ts[1], range(1, NS, 2))])


def phase_c(C):
    P, A, I, O, S, K = C.P, C.A, C.I, C.O, C.S, C.K
    banks = C.banks
    SC = float(128 ** -0.5)
    rel = A.alloc([12], F32)
    P.dma("sp", rel[0:32], I["rel_table"])
    oh = A.alloc([3, 384], F32)
    P.dma("sp", oh[0:32], I["c_oh"].re("g b c -> b g c"))
    ohs = A.alloc([3, 128], F32)
    P.dma("sp", ohs[0:32], I["c_ohs"].re("g b c -> b g c"))
    oh0 = A.alloc([3, 1], F32)
    P.dma("sp", oh0[0:32], I["c_oh0"].re("g b c -> b g c"))
    negm = A.alloc([384], F32)
    P.dma("sp", negm, I["c_negm"])
    bias2 = A.alloc([12, 256], F32)
    biasS = A.alloc([12], F32)
    bias0 = A.alloc([12], F32)
    relb = A.alloc([128], F32)
    vp = Rot([A.alloc([384], F32) for _ in range(2)])
    for g in range(3):
        for h in range(4):
            gh = g * 4 + h
            ts(P, "dve", relb[0:32], K.ones[0:32], rel[0:32, gh:gh + 1], ALU.mult)
            bk = banks.next()
            mm(P, bk[:, 0:384], relb[0:32], oh[0:32, g, :])
            v_ = vp.next()
            tt(P, "dve", v_, bk[:, 0:384], negm, ALU.add)
            P.dma("sp", S["bias"][gh, 0:128, :], v_)
            P.dma("sp", S["bias"][gh, 128:256, :], v_)
        bk = banks.next()
        mm(P, bk[:, 0:4], ohs[0:32, g, :], rel[0:32, g * 4:(g + 1) * 4])
        cp(P, "dve", biasS[:, g * 4:(g + 1) * 4], bk[:, 0:4])
        bk = banks.next()
        mm(P, bk[0:1, 0:4], oh0[0:32, g, :], rel[0:32, g * 4:(g + 1) * 4])
        cp(P, "dve", bias0[0:1, g * 4:(g + 1) * 4], bk[0:1, 0:4])
    P.barrier()
    for gh in range(12):
        base = gh * 256 * 384 + 255
        P.dma("sp", bias2[:, gh, 0:128], S["bias"].raw([[383, 128], [1, 128]], off=base + 128 * 383))
        P.dma("sp", bias2[:, gh, 128:256], S["bias"].raw([[383, 128], [1, 128]], off=base))
    MODE = os.environ.get("MK_C", "prompt,sample").split(",")
    mark = A.off
    if "prompt" in MODE:
        acc2s = Rot([A.alloc([2, L], F32) for _ in range(2)])
        qcs = Rot([A.alloc([L], BF16) for _ in range(2)])
        kcs = Rot([A.alloc([L], BF16) for _ in range(2)])
        vcs = Rot([A.alloc([L // 128, 128], BF16) for _ in range(2)])
        lgs = Rot([A.alloc([256], F32) for _ in range(2)])
        Es = Rot([A.alloc([256], BF16) for _ in range(3)])
        zbt = A.alloc([L], F32)
        ybt = A.alloc([L], BF16)
        for s in range(NSEQ):
            for h in range(4):
                acc2 = acc2s.next()
                for g, dil in enumerate(DILS):
                    gh = g * 4 + h
                    lc = L // dil
                    nb = lc // 128
                    for r in range(dil):
                        qc, kc, vc = qcs.next()[:, 0:lc], kcs.next()[:, 0:lc], vcs.next()[:, 0:nb, :]
                        c0 = s * L + r * lc
                        P.dma("sp", qc, S["qb%d" % g][h * 128:(h + 1) * 128, c0:c0 + lc], acc=False)
                        P.dma("sp", kc, S["kb%d" % g][h * 128:(h + 1) * 128, c0:c0 + lc], acc=False)
                        P.dma("sp", vc, S["vb%d" % g][c0:c0 + lc, h * 128:(h + 1) * 128].re("(n p) d -> p n d", p=128),
                              acc=False)
                        Eprev = None
                        for n in range(nb):
                            nq = 256 if n < nb - 1 else 128
                            bk = banks.next()
                            mm(P, bk[:, 0:nq], kc[:, n * 128:(n + 1) * 128], qc[:, n * 128:n * 128 + nq])
                            lg = lgs.next()
                            stt(P, "dve", lg[:, 0:nq], bk[:, 0:nq], SC, bias2[:, gh, 0:nq], ALU.mult, ALU.add)
                            E = Es.next()
                            act(P, E[:, 0:nq], lg[:, 0:nq], AF.Exp)
                            b2 = banks.next()
                            if n > 0:
                                mm(P, b2[:, 0:128], vc[:, n - 1, :], Eprev[:, 128:256], True, False)
                            mm(P, b2[:, 0:128], vc[:, n, :], E[:, 0:128], n == 0, True)
                            if n > 0:
                                mm(P, b2[:, 128:256], K.onesb, Eprev[:, 128:256], True, False)
                            mm(P, b2[:, 128:256], K.onesb, E[:, 0:128], n == 0, True)
                            Eprev = E
                            lo = r + n * 128 * dil
                            dst = acc2[:, :, lo:lo + 127 * dil + 1:dil]
                            src = b2[:, 0:256].re("p (a q) -> p a q", a=2)
                            if g == 0:
                                cp(P, "act", dst, src)
                            else:
                                tt(P, "dve", dst, dst, src, ALU.add)
                P.dma("sp", zbt, S["zbT"][h * 128:(h + 1) * 128, s * L:(s + 1) * L], acc=False)
                for qd in range(4):
                    sl = slice(qd * 1024, (qd + 1) * 1024)
                    recip(P, acc2[:, 1, sl], acc2[:, 1, sl])
                    tt(P, "pool", acc2[:, 0, sl], acc2[:, 0, sl], acc2[:, 1, sl], ALU.mult)
                    tt(P, "pool", ybt[:, sl], acc2[:, 0, sl], zbt[:, sl], ALU.mult)
                P.dma("sp", S["ybT"][h * 128:(h + 1) * 128, s * L:(s + 1) * L], ybt)
    P.barrier()
    A.off = mark
    if "sample" in MODE:
        Kcs = [Rot([A.alloc([512], F32) for _ in range(2)]) for g in range(3)]
        Vcs = [Rot([A.alloc([512], F32) for _ in range(2)]) for g in range(3)]
        KcT = Rot([A.alloc([4, 128], F32) for _ in range(2)])
        vsb = Rot([A.alloc([3, 512], F32) for _ in range(2)])
        zbs = A.alloc([4, NS], F32)
        ybs = A.alloc([4, NS], BF16)
        sw = Rot([A.alloc([64], F32) for _ in range(2)])
        P.dma("sp", zbs, S["zbT"].re("(h p) t -> p h t", p=128)[:, :, NTP:NT])
        for b in range(NS):
            kk = [Kcs[g].next() for g in range(3)]
            vv = [Vcs[g].next() for g in range(3)]
            for g, dil in enumerate(DILS):
                P.dma("sp", kk[g], I["ck%d" % g][b, 0::dil, :], acc=False)
                P.dma("sp", vv[g], I["cv%d" % g][b, 0::dil, :], acc=False)
            vs_ = vsb.next()
            P.dma("sp", vs_[0:1], S["vs"][b:b + 1], acc=False)
            w_ = sw.next()
            bL = banks.next()
            for g in range(3):
                bk = banks.next()
                for h in range(4):
                    tr(P, bk[:, h * 128:(h + 1) * 128], kk[g][:, h * 128:(h + 1) * 128], K.ident)
                kt = KcT.next()
                cp(P, "act", kt, bk.re("p (h k) -> p h k", h=4))
                for h in range(4):
                    gh = g * 4 + h
                    mm(P, bL[:, gh:gh + 1], kt[:, h, :], K.qs[:, gh, b:b + 1])
            for gh in range(12):
                mm(P, bL[0:1, 16 + gh:17 + gh], K.ks[:, gh, b:b + 1], K.qs[:, gh, b:b + 1])
            lgS, ES, lg0, E0 = w_[:, 0:12], w_[:, 12:24], w_[0:1, 24:36], w_[0:1, 36:48]
            stt(P, "dve", lgS, bL[:, 0:12], SC, biasS, ALU.mult, ALU.add)
            act(P, ES, lgS, AF.Exp)
            stt(P, "dve", lg0, bL[0:1, 16:28], SC, bias0[0:1], ALU.mult, ALU.add)
            act(P, E0, lg0, AF.Exp)
            bO = banks.next()
            for h in range(4):
                for g in range(3):
                    gh = g * 4 + h
                    mm(P, bO[:, h:h + 1], vv[g][:, h * 128:(h + 1) * 128], ES[:, gh:gh + 1], g == 0, False)
                    mm(P, bO[:, h:h + 1], vs_[0:1, g, h * 128:(h + 1) * 128], E0[0:1, gh:gh + 1], False, g == 2)
                for g in range(3):
                    gh = g * 4 + h
                    mm(P, bO[:, 4 + h:5 + h], K.ones, ES[:, gh:gh + 1], g == 0, False)
                    mm(P, bO[:, 4 + h:5 + h], K.ones[0:1, :], E0[0:1, gh:gh + 1], False, g == 2)
            ob = w_[:, 48:56]
            cp(P, "dve", ob, bO[:, 0:8])
            recip(P, ob[:, 4:8], ob[:, 4:8])
            tt(P, "pool", ob[:, 0:4], ob[:, 0:4], ob[:, 4:8], ALU.mult)
            tt(P, "pool", ybs[:, :, b], ob[:, 0:4], zbs[:, :, b], ALU.mult)
        P.dma("sp", S["ybT"].re("(h p) t -> p h t", p=128)[:, :, NTP:NT], ybs)


def phase_d(C):
    P, A, I, O, S, K = C.P, C.A, C.I, C.O, C.S, C.K
    banks = C.banks
    wpa = A.alloc([8, DM], BF16)
    wpb = A.alloc([4, DM], BF16)
    wo = A.alloc([8, DM], BF16)
    gpost = A.alloc([DM], F32)
    P.dma("pool", wpa, I["w_proj_a"].re("(k p) n -> p k n", p=128))
    P.dma("pool", wpb, I["w_proj_b"].re("(k p) n -> p k n", p=128))
    P.dma("pool", wo, I["w_out"].re("(k p) n -> p k n", p=128))
    P.dma("sp", gpost, bcast_rows(I["g_post"], DM))
    yas = Rot([A.alloc([8, 512], BF16) for _ in range(2)])
    ybs_ = Rot([A.alloc([4, 512], BF16) for _ in range(2)])
    gas = Rot([A.alloc([8, 512], F32) for _ in range(1)])
    gbs_ = Rot([A.alloc([8, 512], F32) for _ in range(1)])
    xts = Rot([A.alloc([4, DM], F32) for _ in range(2)])
    hTs = Rot([A.alloc([8, 512], BF16) for _ in range(2)])
    t1s = Rot([A.alloc([512], F32) for _ in range(2)])
    t2s = Rot([A.alloc([512], F32) for _ in range(2)])
    junk = A.alloc([DM], F32)
    ysbs = Rot([A.alloc([DM], F32) for _ in range(2)])
    sts = Rot([A.alloc([4], F32) for _ in range(4)])
    fmv = lambda nm: S[nm].re("(k p) t -> p k t", p=128)
    TILES = [(ti * 512, 512) for ti in range(NTP // 512)] + [(NTP, NS)]
    for (t0, n) in TILES:
        nsub = 4 if n == 512 else 1
        np_ = 128 if n == 512 else NS
        ya, yb, ga, gb_, xt, hT = yas.next(), ybs_.next(), gas.next(), gbs_.next(), xts.next(), hTs.next()
        P.dma("sp", ya[:, :, 0:n], fmv("yaT")[:, :, t0:t0 + n], acc=False)
        P.dma("sp", yb[:, :, 0:n], fmv("ybT")[:, :, t0:t0 + n], acc=False)
        P.dma("sp", ga[:, :, 0:n], fmv("gaT")[:, :, t0:t0 + n], acc=False)
        P.dma("sp", gb_[:, :, 0:n], fmv("gbT")[:, :, t0:t0 + n], acc=False)
        if n == 512:
            P.dma("sp", xt, I["x_p"][t0:t0 + 512, :].re("(s p) d -> p s d", p=128), acc=False)
        else:
            P.dma("sp", xt[0:NS, 0, :], I["x_s"], acc=False)
        for e in range(8):
            pA, pB = banks.next(), banks.next()
            for k in range(8):
                mm(P, pA[:, 0:n], wpa[:, k, e * 128:(e + 1) * 128], ya[:, k, 0:n], k == 0, k == 7)
            for k in range(4):
                mm(P, pB[:, 0:n], wpb[:, k, e * 128:(e + 1) * 128], yb[:, k, 0:n], k == 0, k == 3)
            t1, t2 = t1s.next()[:, 0:n], t2s.next()[:, 0:n]
            tt(P, "dve", t1, ga[:, e, 0:n], pA[:, 0:n], ALU.mult)
            tt(P, "dve", t2, gb_[:, e, 0:n], pB[:, 0:n], ALU.mult)
            tt(P, "pool", hT[:, e, 0:n], t1, t2, ALU.add)
        for sb in range(nsub):
            bks = [banks.next(), banks.next()]
            for half in range(2):
                for k in range(8):
                    mm(P, bks[half][0:np_, :], hT[:, k, sb * 128:sb * 128 + np_], wo[:, k, half * 512:(half + 1) * 512],
                       k == 0, k == 7)
            for half in range(2):
                act(P, junk[0:np_, half * 512:(half + 1) * 512], bks[half][0:np_, :], AF.Square)
            st_ = sts.next()[0:np_]
            rsum(P, "dve", st_[:, 0:1], junk[0:np_])
            rsqrt(P, st_[:, 1:2], st_[:, 0:1], EPS, 1.0 / DM)
            ysb = ysbs.next()[0:np_]
            for half in range(2):
                hs = slice(half * 512, (half + 1) * 512)
                stt(P, "dve", ysb[:, hs], bks[half][0:np_, :], st_[:, 1:2], gpost[0:np_, hs], ALU.mult, ALU.mult)
            tt(P, "pool", ysb, ysb, xt[0:np_, sb, :], ALU.add)
            if n == 512:
                P.dma("sp", O["y_p"][t0 + sb * 128:t0 + (sb + 1) * 128, :], ysb)
            else:
                P.dma("sp", O["y_s"], ysb)


def rel_bucket_np(dist):
    import math
    max_exact = 16
    d = np.maximum(dist, 1).astype(np.float32)
    large = max_exact + (np.log(d / max_exact) / math.log(2048 / max_exact) * (32 - max_exact)).astype(np.int32)
    large = np.minimum(large, 31)
    return np.where(dist < max_exact, dist, large)


def host_consts():
    c = {}
    c["c_ident"] = np.eye(128, dtype=np.float32)
    k = np.arange(128)
    c["c_umat"] = (k[:, None] <= k[None, :]).astype(np.float32)
    c["c_maskT"] = np.where(k[None, :] >= k[:, None], 0.0, NEG).astype(np.float32)
    c["c_strictT"] = (k[None, :] > k[:, None]).astype(np.float32)
    oh = np.zeros((3, 32, 384), np.float32)
    ohs = np.zeros((3, 32, 128), np.float32)
    oh0 = np.zeros((3, 32, 1), np.float32)
    for g, dil in enumerate(DILS):
        bk = rel_bucket_np(np.arange(129, dtype=np.int32) * dil)
        for j in range(129):
            oh[g, bk[j], 127 + j] = 1.0
        for i in range(128):
            ohs[g, bk[128 - i], i] = 1.0
        oh0[g, bk[0], 0] = 1.0
    c["c_oh"] = oh
    negm = np.full((128, 384), NEG, np.float32)
    negm[:, 127:256] = 0.0
    c["c_negm"] = negm
    c["c_ohs"] = ohs
    c["c_oh0"] = oh0
    return c


_NC_CACHE = {}


def kernel(x_prompt, x_sample, state_gdn, state_conv, cache_k_w128, cache_v_w128,
           cache_k_w512, cache_v_w512, cache_k_w2048, cache_v_w2048, rel_table,
           g_pre, w_in, conv_w, a_log, dt_bias, g_head_norm, w_proj_a, w_proj_b, w_out, g_post):
    phases = os.environ.get("MK_PHASES", "ABCD")
    ncores = int(os.environ.get("MK_CORES", str(NCORES)))
    if phases not in _NC_CACHE:
        _NC_CACHE[phases] = build(phases)
    nc = _NC_CACHE[phases]
    f = lambda a: np.ascontiguousarray(np.asarray(a, dtype=np.float32))
    consts = host_consts()
    caches = ((cache_k_w128, cache_v_w128), (cache_k_w512, cache_v_w512), (cache_k_w2048, cache_v_w2048))
    in_maps = []
    for c in range(ncores):
        m = {}
        m["x_p"] = f(x_prompt[NSEQ * c:NSEQ * (c + 1)]).reshape(NTP, DM)
        m["x_s"] = f(x_sample[NS * c:NS * (c + 1)]).reshape(NS, DM)
        m["state_gdn"] = f(state_gdn[0, NS * c:NS * (c + 1)])
        m["state_conv"] = f(state_conv[0, NS * c:NS * (c + 1)])
        for g, (ck, cv) in enumerate(caches):
            m["ck%d" % g] = f(ck[0, NS * c:NS * (c + 1)]).reshape(NS, -1, 512)
            m["cv%d" % g] = f(cv[0, NS * c:NS * (c + 1)]).reshape(NS, -1, 512)
        m["rel_table"] = f(rel_table)
        m["g_pre"] = f(g_pre)
        m["w_in"] = f(w_in[0])
        m["conv_w"] = f(conv_w[0])
        m["a_log"] = f(a_log)
        m["dt_bias"] = f(dt_bias)
        m["g_head_norm"] = f(g_head_norm)
        m["w_proj_a"] = f(w_proj_a[0])
        m["w_proj_b"] = f(w_proj_b[0])
        m["w_out"] = f(w_out[0])
        m["g_post"] = f(g_post)
        m.update(consts)
        in_maps.append(m)
    res = run_bass_kernel_spmd(nc, in_maps, core_ids=list(range(ncores)))
    R = res.results
    cat = lambda k: np.concatenate([np.asarray(r[k]) for r in R], axis=0)
    B = NSEQ * ncores
    SB = NS * ncores
    outs = [cat("y_p").reshape(B, L, DM), cat("y_s").reshape(SB, 1, DM),
            cat("p_gdn").reshape(1, B, 8, 128, 128), cat("p_conv").reshape(1, B, 3, 3072)]
    for g, w in enumerate((128, 512, 2048)):
        outs.append(cat("p_k%d" % g).reshape(1, B, w, 4, 128))
        outs.append(cat("p_v%d" % g).reshape(1, B, w, 4, 128))
    outs.append(cat("s_gdn").reshape(1, SB, 8, 128, 128))
    outs.append(cat("s_conv").reshape(1, SB, 3, 3072))
    for g in range(3):
        outs.append(cat("s_k%d" % g).reshape(1, SB, 1, 4, 128))
        outs.append(cat("s_v%d" % g).reshape(1, SB, 1, 4, 128))
    return tuple(o.astype(np.float32) for o in outs)
```

```python
import os
from contextlib import ExitStack
import numpy as np
import concourse.bass as bass
import concourse.mybir as mybir
from concourse.bass_utils import run_bass_kernel_spmd

F32 = mybir.dt.float32
BF16 = mybir.dt.bfloat16
U8 = mybir.dt.uint8
AF = mybir.ActivationFunctionType
ALU = mybir.AluOpType
AX = mybir.AxisListType

NCORES = 8
L = 4096
NSEQ = 2
NS = 16
NTP = NSEQ * L
NT = NTP + NS
DM = 1024
INW = 11280
EPS = 1e-6
NEG = -1e30
DILS = (1, 4, 16)
O_ZA, O_A, O_QKVB, O_ZB, O_GA, O_GB = 3072, 4096, 4112, 8720, 9232, 10256


class Tl:
    __slots__ = ("w", "rd", "excl")

    def __init__(self, excl=False):
        self.w = {}
        self.rd = {}
        self.excl = excl


class V:
    __slots__ = ("t", "ap")

    def __init__(self, t, ap):
        self.t = t
        self.ap = ap

    def __getitem__(self, k):
        return V(self.t, self.ap[k])

    def re(self, s, **kw):
        return V(self.t, self.ap.rearrange(s, **kw))

    def raw(self, dims, off=0):
        return V(self.t, bass.AP(tensor=self.ap.tensor, offset=self.ap.offset + off, ap=dims))

    def bitcast(self, dt):
        return V(self.t, self.ap.bitcast(dt))


NDS = 12


class Prog:
    ENG = ("sp", "pe", "act", "dve", "pool")

    def __init__(self, nc):
        self.nc = nc
        self.streams = {e: [] for e in self.ENG}
        self.count = {e: 0 for e in self.ENG}
        self.known = {e: {} for e in self.ENG}
        self.dma_n = {"sp": 0, "pool": 0, "act": 0}
        self.latest = {}

    def _resolve(self, eng, deps):
        kn = self.known[eng]
        waits = []
        for k, v in deps.items():
            if k == "pe" and eng == "pe":
                continue
            if kn.get(k, 0) >= v:
                continue
            kn[k] = v
            waits.append((k, v))
        return waits

    @staticmethod
    def _deps(reads, writes, acc):
        deps = {}

        def add(d):
            for k, v in d.items():
                if deps.get(k, 0) < v:
                    deps[k] = v
        for t in reads:
            if t is not None:
                add(t.w)
                if t.excl:
                    add(t.rd)
        for t in writes:
            if t is not None:
                if not acc:
                    add(t.w)
                add(t.rd)
        return deps

    def _commit(self, tok, reads, writes, acc):
        k, v = tok
        self.latest[k] = max(self.latest.get(k, 0), v)
        for t in writes:
            if t is None:
                continue
            if acc:
                t.w[k] = max(t.w.get(k, 0), v)
            else:
                t.w = {k: v}
                t.rd = {}
        for t in reads:
            if t is None or t in writes:
                continue
            t.rd[k] = max(t.rd.get(k, 0), v)

    def op(self, eng, fn, reads=(), writes=(), acc=False):
        reads = [r.t if isinstance(r, V) else r for r in reads]
        writes = [r.t if isinstance(r, V) else r for r in writes]
        deps = self._deps(reads, writes, acc)
        waits = self._resolve(eng, deps)
        self.count[eng] += 1
        tok = (eng, self.count[eng])
        self.streams[eng].append((waits, fn, (eng, 1)))
        self._commit(tok, reads, writes, acc)

    def dma(self, q, out, in_, acc=True):
        if os.environ.get("MK_NOST") and out.t is None and out.ap.tensor.name.startswith("s_"):
            return
        n = self.dma_n[q]
        self.dma_n[q] += 1
        i = n % NDS
        val = 16 * (n // NDS + 1)
        key = ("d", q, i)
        reads = [in_.t]
        writes = [out.t]
        deps = self._deps(reads, writes, acc)
        if n >= NDS:
            deps[key] = max(deps.get(key, 0), val - 16)
        waits = self._resolve(q, deps)
        oa, ia = out.ap, in_.ap
        self.streams[q].append((waits, lambda e: e.dma_start(out=oa, in_=ia), (key, 16)))
        self._commit((key, val), reads, writes, acc)

    def barrier(self):
        for e in self.ENG:
            waits = self._resolve(e, dict(self.latest))
            if waits:
                self.streams[e].append((waits, None, None))

    def emit(self):
        nc = self.nc
        keys = list(self.ENG[1:]) + [("d", q, i) for q in ("sp", "pool", "act") for i in range(NDS)]
        with ExitStack() as st:
            st.enter_context(nc.allow_non_contiguous_dma(reason="small strided sample-path transfers"))
            sems = {}
            for k in keys:
                nm = k if isinstance(k, str) else "d%s%d" % (k[1], k[2])
                sems[k] = st.enter_context(nc.semaphore("s_" + nm))
            block = st.enter_context(nc.Block())
            decos = {"sp": block.sync, "pe": block.tensor, "act": block.scalar,
                     "dve": block.vector, "pool": block.gpsimd}
            for eng in self.ENG:
                stream = self.streams[eng]

                def body(e, stream=stream):
                    for waits, fn, inc in stream:
                        for k, v in waits:
                            e.wait_ge(sems[k], v)
                        if fn is not None:
                            fn(e).then_inc(sems[inc[0]], inc[1])
                decos[eng](body)


class Arena:
    def __init__(self, nc, nbytes):
        self.t = nc.alloc_sbuf_tensor("arena", [128, nbytes], U8)
        self.ap = self.t.ap()
        self.n = nbytes
        self.off = 0

    def alloc(self, free_shape, dt, parts=128):
        esz = 4 if dt == F32 else 2
        ne = int(np.prod(free_shape))
        nb = (ne * esz + 31) // 32 * 32
        assert self.off + nb <= self.n, "SBUF arena overflow %d + %d > %d" % (self.off, nb, self.n)
        a = self.ap[0:parts, self.off:self.off + ne * esz].bitcast(dt)
        self.off += nb
        if len(free_shape) == 2:
            a = a.rearrange("p (a b) -> p a b", a=free_shape[0])
        elif len(free_shape) == 3:
            a = a.rearrange("p (a b c) -> p a b c", a=free_shape[0], b=free_shape[1])
        return V(Tl(), a)


class Ctx:
    pass


def dram_in(nc, name, shape, dt=F32):
    return V(None, nc.dram_tensor(name, list(shape), dt, kind="ExternalInput").ap())


def dram_out(nc, name, shape, dt=F32):
    return V(None, nc.dram_tensor(name, list(shape), dt, kind="ExternalOutput").ap())


def dram_tmp(nc, name, shape, dt=F32):
    return V(None, nc.dram_tensor(name, list(shape), dt, kind="Internal").ap())


def mm(P, out, lhsT, rhs, start=True, stop=True):
    o, l, r = out.ap, lhsT.ap, rhs.ap
    P.op("pe", lambda e: e.matmul(o, l, r, start=start, stop=stop), [lhsT, rhs], [out])


def tr(P, out, in_, ident):
    o, i, d = out.ap, in_.ap, ident.ap
    P.op("pe", lambda e: e.transpose(o, i, d), [in_, ident], [out])


def act(P, out, in_, func, bias=0.0, scale=1.0, eng="act"):
    o, i = out.ap, in_.ap
    rd = [in_]
    b = bias
    if isinstance(bias, V):
        rd.append(bias)
        b = bias.ap
    s = scale
    if isinstance(scale, V):
        rd.append(scale)
        s = scale.ap
    P.op("act", lambda e: e.activation(o, i, func, bias=b, scale=s), rd, [out])


def cp(P, eng, out, in_):
    o, i = out.ap, in_.ap
    if eng == "act":
        P.op("act", lambda e: e.copy(o, i), [in_], [out])
    else:
        P.op(eng, lambda e: e.tensor_copy(o, i), [in_], [out])


def tt(P, eng, out, in0, in1, op):
    o, a, b = out.ap, in0.ap, in1.ap
    P.op(eng, lambda e: e.tensor_tensor(o, a, b, op), [in0, in1], [out])


def ts(P, eng, out, in0, s1, op0, s2=None, op1=None):
    o, a = out.ap, in0.ap
    rd = [in0]
    x1 = s1
    if isinstance(s1, V):
        rd.append(s1)
        x1 = s1.ap
    x2 = s2
    if isinstance(s2, V):
        rd.append(s2)
        x2 = s2.ap
    if op1 is None:
        P.op(eng, lambda e: e.tensor_scalar(o, a, x1, None, op0), rd, [out])
    else:
        P.op(eng, lambda e: e.tensor_scalar(o, a, x1, x2, op0, op1), rd, [out])


def stt(P, eng, out, in0, scalar, in1, op0, op1):
    o, a, b = out.ap, in0.ap, in1.ap
    rd = [in0, in1]
    s = scalar
    if isinstance(scalar, V):
        rd.append(scalar)
        s = scalar.ap
    P.op(eng, lambda e: e.scalar_tensor_tensor(o, a, s, b, op0, op1), rd, [out])


def memset(P, eng, out, val):
    o = out.ap
    P.op(eng, lambda e: e.memset(o, val), [], [out])


def rsum(P, eng, out, in_):
    o, i = out.ap, in_.ap
    P.op(eng, lambda e: e.reduce_sum(o, i, AX.X), [in_], [out])


def recip(P, out, in_):
    o, i = out.ap, in_.ap
    P.op("dve", lambda e: e.reciprocal(o, i), [in_], [out])


def rsqrt(P, out, in_, eps, scale=1.0):
    act(P, out, in_, AF.Sqrt, bias=eps, scale=scale)
    recip(P, out, out)


class Rot:
    def __init__(self, items):
        self.items = items
        self.i = 0

    def next(self):
        v = self.items[self.i % len(self.items)]
        self.i += 1
        return v


def build(phases="ABCD"):
    nc = bass.Bass("TRN2", target_bir_lowering=False)
    P = Prog(nc)
    C = Ctx()
    C.nc, C.P = nc, P
    I = {}
    I["x_p"] = dram_in(nc, "x_p", [NTP, DM])
    I["x_s"] = dram_in(nc, "x_s", [NS, DM])
    I["state_gdn"] = dram_in(nc, "state_gdn", [NS, 8, 128, 128])
    I["state_conv"] = dram_in(nc, "state_conv", [NS, 3, 3072])
    for g, w in enumerate((128, 512, 2048)):
        I["ck%d" % g] = dram_in(nc, "ck%d" % g, [NS, w, 512])
        I["cv%d" % g] = dram_in(nc, "cv%d" % g, [NS, w, 512])
    I["rel_table"] = dram_in(nc, "rel_table", [32, 12])
    I["g_pre"] = dram_in(nc, "g_pre", [1, DM])
    I["w_in"] = dram_in(nc, "w_in", [DM, INW])
    I["conv_w"] = dram_in(nc, "conv_w", [3072, 4])
    I["a_log"] = dram_in(nc, "a_log", [1, 8])
    I["dt_bias"] = dram_in(nc, "dt_bias", [1, 8])
    I["g_head_norm"] = dram_in(nc, "g_head_norm", [1, 128])
    I["w_proj_a"] = dram_in(nc, "w_proj_a", [1024, DM])
    I["w_proj_b"] = dram_in(nc, "w_proj_b", [512, DM])
    I["w_out"] = dram_in(nc, "w_out", [DM, DM])
    I["g_post"] = dram_in(nc, "g_post", [1, DM])
    I["c_ident"] = dram_in(nc, "c_ident", [128, 128])
    I["c_umat"] = dram_in(nc, "c_umat", [128, 128])
    I["c_maskT"] = dram_in(nc, "c_maskT", [128, 128])
    I["c_strictT"] = dram_in(nc, "c_strictT", [128, 128])
    I["c_oh"] = dram_in(nc, "c_oh", [3, 32, 384])
    I["c_negm"] = dram_in(nc, "c_negm", [128, 384])
    I["c_ohs"] = dram_in(nc, "c_ohs", [3, 32, 128])
    I["c_oh0"] = dram_in(nc, "c_oh0", [3, 32, 1])
    O = {}
    O["y_p"] = dram_out(nc, "y_p", [NTP, DM])
    O["y_s"] = dram_out(nc, "y_s", [NS, DM])
    O["p_gdn"] = dram_out(nc, "p_gdn", [NSEQ, 8, 128, 128])
    O["p_conv"] = dram_out(nc, "p_conv", [NSEQ, 3, 3072])
    for g, w in enumerate((128, 512, 2048)):
        O["p_k%d" % g] = dram_out(nc, "p_k%d" % g, [NSEQ, w, 512])
        O["p_v%d" % g] = dram_out(nc, "p_v%d" % g, [NSEQ, w, 512])
        O["s_k%d" % g] = dram_out(nc, "s_k%d" % g, [NS, 512])
        O["s_v%d" % g] = dram_out(nc, "s_v%d" % g, [NS, 512])
    O["s_gdn"] = dram_out(nc, "s_gdn", [NS, 8, 128, 128])
    O["s_conv"] = dram_out(nc, "s_conv", [NS, 3, 3072])
    S = {}
    S["qT"] = dram_tmp(nc, "s_qT", [1024, NT])
    S["kT"] = dram_tmp(nc, "s_kT", [1024, NT])
    S["vT"] = dram_tmp(nc, "s_vT", [1024, NT])
    S["zaT"] = dram_tmp(nc, "s_zaT", [1024, NT])
    S["zbT"] = dram_tmp(nc, "s_zbT", [512, NT])
    S["gaT"] = dram_tmp(nc, "s_gaT", [1024, NT])
    S["gbT"] = dram_tmp(nc, "s_gbT", [1024, NT])
    S["gbeta"] = dram_tmp(nc, "s_gbeta", [NT, 16])
    for g in range(3):
        S["qb%d" % g] = dram_tmp(nc, "s_qb%d" % g, [512, NTP], BF16)
        S["kb%d" % g] = dram_tmp(nc, "s_kb%d" % g, [512, NTP], BF16)
        S["vb%d" % g] = dram_tmp(nc, "s_vb%d" % g, [NTP, 512], BF16)
    S["vs"] = dram_tmp(nc, "s_vs", [NS, 3, 512])
    S["yaT"] = dram_tmp(nc, "s_yaT", [1024, NT], BF16)
    S["ybT"] = dram_tmp(nc, "s_ybT", [512, NT], BF16)
    S["bias"] = dram_tmp(nc, "s_bias", [12, 256, 384])
    C.I, C.O, C.S = I, O, S

    A = Arena(nc, 206 * 1024)
    C.A = A
    pst = nc.alloc_psum_tensor("psum", [128, 8, 512], F32)
    psa = pst.ap()
    C.banks = Rot([V(Tl(excl=True), psa[:, b, :]) for b in range(8)])

    K = Ctx()
    C.K = K
    K.ident = A.alloc([128], F32)
    K.identb = A.alloc([128], BF16)
    K.umat = A.alloc([128], F32)
    K.maskT = A.alloc([128], F32)
    K.strictT = A.alloc([128], F32)
    K.ones = A.alloc([128], F32)
    K.onesb = A.alloc([128], BF16)
    K.meanm = A.alloc([128], F32)
    K.c128 = A.alloc([128], F32)
    K.qs = A.alloc([12, NS], F32)
    K.ks = A.alloc([12, NS], F32)
    P.dma("sp", K.ident, I["c_ident"])
    P.dma("pool", K.identb, I["c_ident"])
    P.dma("sp", K.umat, I["c_umat"])
    P.dma("sp", K.maskT, I["c_maskT"])
    P.dma("sp", K.strictT, I["c_strictT"])
    memset(P, "pool", K.ones, 1.0)
    memset(P, "pool", K.onesb, 1.0)
    memset(P, "pool", K.meanm, 1.0 / 128.0)
    memset(P, "pool", K.c128, 128.0)
    C.mark0 = A.off

    if "A" in phases:
        phase_a(C)
    P.barrier()
    A.off = C.mark0
    if "B" in phases:
        phase_b(C)
    P.barrier()
    A.off = C.mark0
    if "C" in phases:
        phase_c(C)
    P.barrier()
    A.off = C.mark0
    if "D" in phases:
        phase_d(C)
    P.barrier()
    P.emit()
    return nc


def bcast_rows(v, n):
    return v.raw([[0, 128], [1, n]])


def phase_a(C):
    P, A, I, O, S, K = C.P, C.A, C.I, C.O, C.S, C.K
    banks = C.banks
    xnT = A.alloc([8, NT], BF16)
    gpre = A.alloc([DM], F32)
    convw = A.alloc([24, 4], F32)
    P.dma("sp", gpre, bcast_rows(I["g_pre"], DM))
    P.dma("sp", convw, I["conv_w"].re("(c p) w -> p c w", p=128))

    stT = A.alloc([24, 3, NS], F32)
    mark1 = A.off
    xts = Rot([A.alloc([DM], F32) for _ in range(3)])
    sqs = Rot([A.alloc([DM], F32) for _ in range(2)])
    xns = Rot([A.alloc([DM], BF16) for _ in range(2)])
    sts = Rot([A.alloc([4], F32) for _ in range(4)])
    DBG = int(os.environ.get("MK_DBG", "9"))
    for sub in (range(NTP // 128 + 1) if DBG >= 2 else []):
        if sub < NTP // 128:
            np_, src, t0 = 128, I["x_p"][sub * 128:(sub + 1) * 128, :], sub * 128
        else:
            np_, src, t0 = NS, I["x_s"], NTP
        xt = xts.next()[0:np_]
        sq = sqs.next()[0:np_]
        xn = xns.next()[0:np_]
        stt_ = sts.next()[0:np_]
        P.dma("sp", xt, src)
        act(P, sq, xt, AF.Square)
        rsum(P, "dve", stt_[:, 0:1], sq)
        rsqrt(P, stt_[:, 2:3], stt_[:, 0:1], EPS, 1.0 / DM)
        stt(P, "dve", xn, xt, stt_[:, 2:3], gpre[0:np_], ALU.mult, ALU.mult)
        bk = banks.next()
        bkb = bk.bitcast(BF16)
        for kc in range(8):
            tr(P, bkb[:, kc * 128:kc * 128 + np_], xn[:, kc * 128:(kc + 1) * 128], K.identb[0:np_, 0:np_])
        src_ps = bkb.re("p (k t) -> p k t", k=8)[:, :, 0:np_]
        cp(P, "act" if sub % 2 else "dve", xnT[:, :, t0:t0 + np_], src_ps)
    stin = A.alloc([3072], F32)
    P.dma("sp", stin[0:48], I["state_conv"].re("b r c -> (b r) c"))
    for c4 in range(6 if DBG >= 3 else 0):
        bk = banks.next()
        for m in range(4):
            c = c4 * 4 + m
            tr(P, bk[:, m * 48:(m + 1) * 48], stin[0:48, c * 128:(c + 1) * 128], K.ident[0:48, 0:48])
        cp(P, "dve", stT[:, c4 * 4:(c4 + 1) * 4, :, :].re("p c r b -> p c b r"),
           bk[:, 0:192].re("p (c b r) -> p c b r", c=4, b=NS))
    P.barrier()
    A.off = mark1

    wbs = Rot([A.alloc([8, 512], BF16) for _ in range(2)])
    w_view = I["w_in"].re("(k p) n -> p k n", p=128)

    def load_w(col0, width):
        wb = wbs.next()
        P.dma("pool", wb[:, :, 0:width], w_view[:, :, col0:col0 + width], acc=False)
        return wb

    def fm(bank, wb, m, t0, n):
        for kc in range(8):
            mm(P, bank[:, 0:n], wb[:, kc, m * 128:(m + 1) * 128], xnT[:, kc, t0:t0 + n], kc == 0, kc == 7)

    def tm(bank, wb, tok, np_, width=512):
        for kc in range(8):
            mm(P, bank[0:np_, 0:width], tok(kc), wb[:, kc, 0:width], kc == 0, kc == 7)

    TILES = [(ti * 512, 512) for ti in range(NTP // 512)] + [(NTP, NS)]
    evi = [0]

    def evac_eng():
        evi[0] += 1
        return "act" if evi[0] % 2 else "dve"

    osb = Rot([A.alloc([512], F32) for _ in range(3)])
    tmb = Rot([A.alloc([512], F32) for _ in range(2)])
    mark2 = A.off
    stage = Rot([A.alloc([515], F32) for _ in range(3)])
    cvs = Rot([A.alloc([512], F32) for _ in range(2)])
    svs = Rot([A.alloc([512], F32) for _ in range(4)])
    sqq = Rot([A.alloc([512], F32) for _ in range(3)])
    rss = Rot([A.alloc([512], F32) for _ in range(2)])
    pending = []
    DEFER = int(os.environ.get('MK_DEFER', '2'))
    if DBG >= 4:
        P.dma("sp", O["s_conv"][:, 0:2, :], I["state_conv"][:, 1:3, :])

    SUB = os.environ.get("MK_SUB", "conv,simple,ab,att").split(",")
    for j in range(6 if "conv" in SUB else 0):
        wb = load_w(j * 512, 512)
        for m in range(4):
            c = j * 4 + m
            kind = "q" if c < 8 else ("k" if c < 16 else "v")
            dst = S["qT"] if c < 8 else (S["kT"] if c < 16 else S["vT"])
            r0 = (c % 8) * 128
            prev = None
            for (t0, n) in TILES:
                bk = banks.next()
                fm(bk, wb, m, t0, n)
                cv = cvs.next()[:, 0:n]
                if n == 512:
                    sg = stage.next()
                    cp(P, "act", sg[:, 3:515], bk[:, 0:512])
                    if t0 % L == 0:
                        memset(P, "pool", sg[:, 0:3], 0.0)
                    else:
                        cp(P, "pool", sg[:, 0:3], prev[:, 512:515])
                    prev = sg
                    taps = [sg[:, w:w + 512] for w in range(4)]
                else:
                    sg = stage.next()
                    cp(P, "act", sg[:, 0:n], bk[:, 0:n])
                    taps = [stT[:, c, 0, :], stT[:, c, 1, :], stT[:, c, 2, :], sg[:, 0:n]]
                ts(P, "dve", cv, taps[0], convw[:, c, 0:1], ALU.mult)
                for w in range(1, 4):
                    stt(P, "dve", cv, taps[w], convw[:, c, w:w + 1], cv, ALU.mult, ALU.add)
                sv = svs.next()[:, 0:n]
                act(P, sv, cv, AF.Silu)
                if kind == "v":
                    P.dma("sp", dst[r0:r0 + 128, t0:t0 + n], sv)
                else:
                    sq = sqq.next()[:, 0:n]
                    tt(P, "pool", sq, sv, sv, ALU.mult)

                    def stage2(sq=sq, sv=sv, n=n, kind=kind, dst=dst, r0=r0, t0=t0):
                        b2 = banks.next()
                        mm(P, b2[:, 0:n], K.c128 if kind == "q" else K.ones, sq)
                        rs = rss.next()[:, 0:n]
                        rsqrt(P, rs, b2[:, 0:n], EPS * (128.0 if kind == "q" else 1.0))
                        ob = osb.next()[:, 0:n]
                        tt(P, "pool", ob, sv, rs, ALU.mult)
                        P.dma("sp", dst[r0:r0 + 128, t0:t0 + n], ob)
                    pending.append(stage2)
                    if len(pending) > DEFER:
                        pending.pop(0)()
        while pending:
            pending.pop(0)()
        for s in range(NSEQ):
            bk = banks.next()
            tm(bk, wb, lambda kc, s=s: xnT[:, kc, s * L + L - 128:s * L + L], 128)
            tb = tmb.next()
            cp(P, evac_eng(), tb, bk)
            P.dma("sp", O["p_conv"][s, :, j * 512:(j + 1) * 512], tb[125:128, :])
        bk = banks.next()
        tm(bk, wb, lambda kc: xnT[:, kc, NTP:NT], NS)
        tb = tmb.next()
        cp(P, evac_eng(), tb[0:NS], bk[0:NS])
        P.dma("sp", O["s_conv"][:, 2, j * 512:(j + 1) * 512], tb[0:NS, :])

    while pending:
        pending.pop(0)()
    P.barrier()
    A.off = mark2
    def simple_block(col0, dst, r0, func):
        wb = load_w(col0, 512)
        for m in range(4):
            for (t0, n) in TILES:
                bk = banks.next()
                fm(bk, wb, m, t0, n)
                ob = osb.next()[:, 0:n]
                act(P, ob, bk[:, 0:n], func)
                P.dma("sp", dst[r0 + m * 128:r0 + (m + 1) * 128, t0:t0 + n], ob)

    if "simple" in SUB:
        for j in range(2):
            simple_block(O_ZA + j * 512, S["zaT"], j * 512, AF.Silu)
        simple_block(O_ZB, S["zbT"], 0, AF.Silu)
        for j in range(2):
            simple_block(O_GA + j * 512, S["gaT"], j * 512, AF.Sigmoid)
        for j in range(2):
            simple_block(O_GB + j * 512, S["gbT"], j * 512, AF.Sigmoid)

    wb = load_w(O_A, 16)
    dtb = A.alloc([4, 8], F32)
    nega = A.alloc([4, 8], F32)
    for q in range(4):
        P.dma("sp", dtb[:, q, :], bcast_rows(I["dt_bias"], 8))
        P.dma("sp", nega[:, q, :], bcast_rows(I["a_log"], 8))
    act(P, nega, nega, AF.Exp)
    ts(P, "dve", nega, nega, -1.0, ALU.mult)
    abt = Rot([A.alloc([6, 4, 8], F32) for _ in range(2)])
    gbs = Rot([A.alloc([4, 16], F32) for _ in range(2)])
    for (t0, n) in (TILES if "ab" in SUB else []):
        nsub = 4 if n == 512 else 1
        np_ = 128 if n == 512 else NS
        bk = banks.next()
        for sb in range(nsub):
            for kc in range(8):
                mm(P, bk[0:np_, sb * 16:(sb + 1) * 16], xnT[:, kc, t0 + sb * 128:t0 + sb * 128 + np_],
                   wb[:, kc, 0:16], kc == 0, kc == 7)
        pv = bk[0:np_, 0:nsub * 16].re("p (s c) -> p s c", c=16)
        w_ = abt.next()[0:np_, :, 0:nsub, :]
        gb = gbs.next()[0:np_, 0:nsub, :]
        xx, ax, ee, ll = w_[:, 0], w_[:, 1], w_[:, 2], w_[:, 3]
        tt(P, "dve", xx, pv[:, :, 0:8], dtb[0:np_, 0:nsub, :], ALU.add)
        act(P, ax, xx, AF.Abs)
        act(P, ee, ax, AF.Exp, scale=-1.0)
        act(P, ll, ee, AF.Ln, bias=1.0)
        stt(P, "dve", xx, xx, 0.0, ll, ALU.max, ALU.add)
        tt(P, "dve", gb[:, :, 0:8], xx, nega[0:np_, 0:nsub, :], ALU.mult)
        act(P, gb[:, :, 8:16], pv[:, :, 8:16], AF.Sigmoid)
        if n == 512:
            P.dma("sp", S["gbeta"][t0:t0 + 512, :].re("(s p) c -> p s c", p=128), gb)
        else:
            P.dma("sp", S["gbeta"][t0:t0 + NS, :], gb[:, 0, :])

    stgb = Rot([A.alloc([2048], BF16) for _ in range(2)])
    vbb = Rot([A.alloc([512], BF16) for _ in range(3)])
    GS = [int(x) for x in os.environ.get("MK_G", "0,1,2").split(",")]
    for g, dil in (enumerate(DILS) if "att" in SUB else []):
        if g not in GS:
            continue
        lc = L // dil
        spc = 2048 // dil
        ATT = os.environ.get("MK_ATT", "qk,ktm,v").split(",")
        for t, nm in (((0, "qb"), (1, "kb")) if "qk" in ATT else []):
            wb = load_w(O_QKVB + g * 1536 + t * 512, 512)
            dstT = S["%s%d" % (nm, g)]
            for h in range(4):
                for spn in range(NTP // 2048):
                    sgb = stgb.next()
                    for sb in range(4):
                        bk = banks.next()
                        fm(bk, wb, h, spn * 2048 + sb * 512, 512)
                        i0 = sb * 512 // dil
                        cp(P, evac_eng(), sgb.re("p (r i) -> p r i", r=dil)[:, :, i0:i0 + 512 // dil],
                           bk.re("p (i r) -> p r i", r=dil))
                    s_, n_ = spn // 2, spn % 2
                    d = dstT[h * 128:(h + 1) * 128, s_ * L:(s_ + 1) * L].re("p (r i) -> p r i", r=dil)
                    P.dma("sp", d[:, :, n_ * spc:(n_ + 1) * spc], sgb.re("p (r i) -> p r i", r=dil))
                bk = banks.next()
                fm(bk, wb, h, NTP, NS)
                cp(P, evac_eng(), (K.qs if t == 0 else K.ks)[:, g * 4 + h, :], bk[:, 0:NS])
            if t == 1 and "ktm" in ATT:
                for s in range(NSEQ):
                    for r in range(dil):
                        base = s * L + L - 128 * dil + r
                        bk = banks.next()
                        tm(bk, wb, lambda kc, base=base: xnT[:, kc, base:base + 128 * dil:dil], 128)
                        tb = tmb.next()
                        cp(P, evac_eng(), tb, bk)
                        P.dma("sp", O["p_k%d" % g][s, r::dil, :], tb)
                bk = banks.next()
                tm(bk, wb, lambda kc: xnT[:, kc, NTP:NT], NS)
                tb = tmb.next()
                cp(P, evac_eng(), tb[0:NS], bk[0:NS])
                P.dma("sp", O["s_k%d" % g], tb[0:NS])
        if "v" not in ATT:
            continue
        wb = load_w(O_QKVB + g * 1536 + 1024, 512)
        nb = lc // 128
        MKV = os.environ.get("MK_V", "main,pv,samp").split(",")
        for s in range(NSEQ if "main" in MKV else 0):
            for r in range(dil):
                for n in range(nb):
                    base = s * L + n * 128 * dil + r
                    bk = banks.next()
                    tm(bk, wb, lambda kc, base=base: xnT[:, kc, base:base + 128 * dil:dil], 128)
                    vb = vbb.next()
                    cp(P, evac_eng(), vb, bk)
                    row0 = s * L + r * lc + n * 128
                    P.dma("sp", S["vb%d" % g][row0:row0 + 128, :], vb)
                    if n == nb - 1 and "pv" in MKV:
                        tb = tmb.next()
                        cp(P, evac_eng(), tb, bk)
                        P.dma("sp", O["p_v%d" % g][s, r::dil, :], tb)
        if "samp" not in MKV:
            continue
        bk = banks.next()
        tm(bk, wb, lambda kc: xnT[:, kc, NTP:NT], NS)
        tb = tmb.next()
        cp(P, evac_eng(), tb[0:NS], bk[0:NS])
        P.dma("sp", O["s_v%d" % g], tb[0:NS])
        P.dma("sp", S["vs"][:, g, :], tb[0:NS])


def bc(v, dims):
    p = v.ap.ap[0]
    return v.raw([[p[0], p[1]]] + dims)


def gdn_stream(C, T, c, cols, Sst, first_zero):
    P, K, S = C.P, C.K, C.S
    banks = C.banks
    kT_v = S["kT"].re("(h p) t -> p h t", p=128)
    qT_v = S["qT"].re("(h p) t -> p h t", p=128)
    vT_v = S["vT"].re("(h p) t -> p h t", p=128)
    za_v = S["zaT"].re("(h p) t -> p h t", p=128)
    ya_v = S["yaT"].re("(h p) t -> p h t", p=128)
    ghn = T["ghn"]
    for ci, col0 in enumerate(cols):
        kq = T["kq"][ci % 2][:, :, :, 0:c]
        vT = T["vT"][ci % 2][:, :, 0:c]
        za = T["za"][:, :, 0:c]
        gb = T["gb"][ci % 2][0:c]
        sm = T["sm"]
        P.dma("sp", kq[:, :, 0, :], kT_v[:, :, col0:col0 + c])
        P.dma("sp", kq[:, :, 1, :], qT_v[:, :, col0:col0 + c])
        P.dma("sp", vT, vT_v[:, :, col0:col0 + c])
        P.dma("sp", gb, S["gbeta"][col0:col0 + c, :])
        P.dma("sp", za, za_v[:, :, col0:col0 + c])
        t = [x[0:c, :, 0:c] for x in T["t"]]
        tf = [x[:, :, 0:c] for x in T["t"]]
        td = [x[0:c] for x in T["t"]]
        bk = banks.next()
        mm(P, bk[0:c, 0:8], K.umat[0:c, 0:c], gb[:, 0:8])
        mm(P, bk[:, 8:16], K.ones[0:c, :], gb[:, 0:8])
        cp(P, "dve", sm[0:c, 0:8], bk[0:c, 0:8])
        cp(P, "dve", sm[:, 8:16], bk[:, 8:16])
        tt(P, "pool", sm[0:c, 16:24], sm[0:c, 8:16], sm[0:c, 0:8], ALU.subtract)
        act(P, sm[0:c, 16:24], sm[0:c, 16:24], AF.Exp)
        act(P, sm[:, 24:32], sm[:, 8:16], AF.Exp)
        ts(P, "pool", sm[0:c, 32:40], gb[:, 8:16], -1.0, ALU.mult)
        yield
        Ug = t[0]
        tt(P, "dve", Ug, bc(K.umat[0:c, 0:c], [[0, 8], [1, c]]), bc(gb[:, 0:8], [[1, 8], [0, c]]), ALU.mult)
        bA, bB = banks.next(), banks.next()
        mm(P, bA[:, 0:4 * c].re("p (h i) -> p h i", h=4), K.ones[0:c, :], Ug[:, 0:4, :])
        mm(P, bB[:, 0:4 * c].re("p (h i) -> p h i", h=4), K.ones[0:c, :], Ug[:, 4:8, :])
        EGb = tf[2]
        act(P, EGb[:, 0:4, :], bA[:, 0:4 * c].re("p (h i) -> p h i", h=4), AF.Exp)
        act(P, EGb[:, 4:8, :], bB[:, 0:4 * c].re("p (h i) -> p h i", h=4), AF.Exp)
        dT = t[1]
        mk = bc(K.maskT[0:c, 0:c], [[0, 4], [1, c]])
        for hb, bX in ((0, bA), (4, bB)):
            for h in range(hb, hb + 4):
                stt(P, "dve", dT[:, h, :], bX[0:c, (h - hb) * c:(h - hb + 1) * c], sm[0:c, h:h + 1],
                    K.maskT[0:c, 0:c], ALU.subtract, ALU.add)
        decT = t[3]
        act(P, decT, dT, AF.Exp)
        DSb = t[4]
        tt(P, "pool", DSb, decT, bc(K.strictT[0:c, 0:c], [[0, 8], [1, c]]), ALU.mult)
        tt(P, "pool", DSb, DSb, bc(sm[0:c, 32:40], [[1, 8], [0, c]]), ALU.mult)
        kqd = T["kqd"][:, :, :, 0:c]
        tt(P, "pool", kqd, kq, bc(EGb, [[EGb.ap.ap[1][0], 8], [0, 2], [1, c]]), ALU.mult)
        yield
        vtok, kdec = td[5], td[6]
        for src, dst, scale in ((vT, vtok, None), (kq[:, :, 0, :], kdec, True)):
            for hb in (0, 4):
                bk = banks.next()
                for h in range(hb, hb + 4):
                    tr(P, bk[0:c, (h - hb) * 128:(h - hb + 1) * 128], src[:, h, :], K.ident)
                pv = bk[0:c, :].re("p (h d) -> p h d", h=4)
                if scale is None:
                    cp(P, "act", dst[:, hb:hb + 4, :], pv)
                else:
                    tt(P, "dve", dst[:, hb:hb + 4, :], pv, bc(sm[0:c, 16 + hb:20 + hb], [[1, 4], [0, 128]]), ALU.mult)
        yield
        pa, pb = T["pa"][0:c, :, :, 0:c], T["pb"][0:c, :, :, 0:c]
        qkT = t[9]
        for h2 in range(4):
            bk = banks.next()
            for hh in range(2):
                h = h2 * 2 + hh
                mm(P, bk[0:c, hh * 2 * c:(hh + 1) * 2 * c].re("p (k i) -> p k i", k=2), kq[:, h, 0, :], kq[:, h, :, :])
            pv = bk[0:c, 0:4 * c].re("p (h k i) -> p h k i", h=2, k=2)
            tt(P, "dve", pa[:, h2 * 2:h2 * 2 + 2, 0, :], pv[:, :, 0, :], DSb[:, h2 * 2:h2 * 2 + 2, :], ALU.mult)
            tt(P, "dve", qkT[:, h2 * 2:h2 * 2 + 2, :], pv[:, :, 1, :], decT[:, h2 * 2:h2 * 2 + 2, :], ALU.mult)
        yield
        Pm = t[8]
        if c > 1:
            for hb in (0, 4):
                bk = banks.next()
                for h in range(hb, hb + 4):
                    tr(P, bk[0:c, (h - hb) * c:(h - hb + 1) * c], pa[:, h, 0, :], K.ident[0:c, 0:c])
                cp(P, "act", pa[:, hb:hb + 4, 1, :], bk[0:c, 0:4 * c].re("p (h i) -> p h i", h=4))
            tt(P, "pool", Pm, pa[:, :, 0, :], bc(K.ident[0:c, 0:c], [[0, 8], [1, c]]), ALU.add)
            yield
            cur, nxt = pa, pb
            for lvl in range(1, 7):
                for h2 in range(4):
                    bk = banks.next()
                    for hh in range(2):
                        h = h2 * 2 + hh
                        if lvl < 6:
                            mm(P, bk[0:c, hh * 2 * c:hh * 2 * c + c], cur[:, h, 1, :], cur[:, h, 0, :])
                        mm(P, bk[0:c, hh * 2 * c + c:(hh + 1) * 2 * c], cur[:, h, 0, :], cur[:, h, 1, :])
                    pv = bk[0:c, 0:4 * c].re("p (h k i) -> p h k i", h=2, k=2)
                    if lvl < 6:
                        cp(P, "act" if h2 % 2 else "dve", nxt[:, h2 * 2:h2 * 2 + 2, :, :], pv)
                    else:
                        cp(P, "act" if h2 % 2 else "dve", nxt[:, h2 * 2:h2 * 2 + 2, 1, :], pv[:, :, 1, :])
                yield
                for hb in (0, 4):
                    bk = banks.next()
                    for h in range(hb, hb + 4):
                        mm(P, bk[0:c, (h - hb) * c:(h - hb + 1) * c], nxt[:, h, 1, :], Pm[:, h, :])
                    tt(P, "dve", Pm[:, hb:hb + 4, :], Pm[:, hb:hb + 4, :],
                       bk[0:c, 0:4 * c].re("p (h i) -> p h i", h=4), ALU.add)
                yield
                cur, nxt = nxt, cur
        else:
            memset(P, "pool", Pm, 1.0)
        if ci == 0 and first_zero:
            memset(P, "pool", Sst, 0.0)
        R = td[0]
        for hb in (0, 4):
            bk = banks.next()
            for h in range(hb, hb + 4):
                mm(P, bk[0:c, (h - hb) * 128:(h - hb + 1) * 128], kqd[:, h, 0, :], Sst[:, h, :])
            tt(P, "dve", R[:, hb:hb + 4, :], vtok[:, hb:hb + 4, :], bk[0:c, :].re("p (h d) -> p h d", h=4), ALU.subtract)
        yield
        vn = td[1]
        for hb in (0, 4):
            bk = banks.next()
            for h in range(hb, hb + 4):
                mm(P, bk[0:c, (h - hb) * 128:(h - hb + 1) * 128], Pm[:, h, :], R[:, h, :])
            tt(P, "dve", vn[:, hb:hb + 4, :], bk[0:c, :].re("p (h d) -> p h d", h=4),
               bc(gb[:, 8 + hb:12 + hb], [[1, 4], [0, 128]]), ALU.mult)
        yield
        oT = tf[5]
        for hb in (0, 4):
            bk = banks.next()
            for h in range(hb, hb + 4):
                o_ = bk[:, (h - hb) * c:(h - hb + 1) * c]
                mm(P, o_, Sst[:, h, :], kqd[:, h, 1, :], True, False)
                mm(P, o_, vn[:, h, :], qkT[:, h, :], False, True)
            cp(P, "act", oT[:, hb:hb + 4, :], bk[:, 0:4 * c].re("p (h i) -> p h i", h=4))
        yield
        bks = []
        for hb in (0, 4):
            bk = banks.next()
            bks.append(bk)
            for h in range(hb, hb + 4):
                mm(P, bk[:, (h - hb) * 128:(h - hb + 1) * 128], kdec[:, h, :], vn[:, h, :])
        tt(P, "pool", Sst, Sst, bc(sm[:, 24:32], [[1, 8], [0, 128]]), ALU.mult)
        for hb, bk in zip((0, 4), bks):
            tt(P, "dve", Sst[:, hb:hb + 4, :], Sst[:, hb:hb + 4, :], bk.re("p (h d) -> p h d", h=4), ALU.add)
        yield
        sq = tf[2]
        tt(P, "pool", sq, oT, oT, ALU.mult)
        rs = tf[3]
        for hb in (0, 4):
            bk = banks.next()
            mm(P, bk[:, 0:4 * c].re("p (h i) -> p h i", h=4), K.meanm, sq[:, hb:hb + 4, :])
            rsqrt(P, rs[:, hb:hb + 4, :], bk[:, 0:4 * c].re("p (h i) -> p h i", h=4), EPS)
        y1 = tf[4]
        stt(P, "dve", y1, oT, ghn[:, 0:1], rs, ALU.mult, ALU.mult)
        yb = T["yb"][:, :, 0:c]
        tt(P, "pool", yb, y1, za, ALU.mult)
        P.dma("sp", ya_v[:, :, col0:col0 + c], yb)
        yield


def gdn_tiles(C):
    A = C.A
    T = {}
    T["kq"] = [A.alloc([8, 2, 128], F32) for _ in range(2)]
    T["vT"] = [A.alloc([8, 128], F32) for _ in range(2)]
    T["gb"] = [A.alloc([16], F32) for _ in range(2)]
    T["za"] = A.alloc([8, 128], F32)
    T["sm"] = A.alloc([40], F32)
    T["t"] = [A.alloc([8, 128], F32) for _ in range(10)]
    T["kqd"] = A.alloc([8, 2, 128], F32)
    T["pa"] = A.alloc([8, 2, 128], F32)
    T["pb"] = A.alloc([8, 2, 128], F32)
    T["yb"] = A.alloc([8, 128], BF16)
    T["S"] = A.alloc([8, 128], F32)
    return T


def run_interleaved(gens):
    gens = list(gens)
    while gens:
        alive = []
        for g in gens:
            try:
                next(g)
                alive.append(g)
            except StopIteration:
                pass
        gens = alive


def phase_b(C):
    P, A, I, O, S, K = C.P, C.A, C.I, C.O, C.S, C.K
    ghn = A.alloc([1], F32)
    P.dma("sp", ghn, I["g_head_norm"].re("o d -> d o"))
    Ts = [gdn_tiles(C) for _ in range(2)]
    for T in Ts:
        T["ghn"] = ghn
    MODE = os.environ.get("MK_B", "prompt,sample").split(",")
    NCH = int(os.environ.get("MK_NCH", str(L // 128)))
    if "prompt" in MODE:
        gens = [gdn_stream(C, Ts[s], 128, [s * L + ch * 128 for ch in range(NCH)], Ts[s]["S"], True)
                for s in range(NSEQ)]
        run_interleaved(gens)
        for s in range(NSEQ):
            P.dma("sp", O["p_gdn"][s].re("h k v -> k h v"), Ts[s]["S"])
    if "sample" in MODE:
        def sample_gen(T, bs):
            for b in bs:
                P.dma("sp", T["S"], I["state_gdn"][b].re("h k v -> k h v"), acc=False)
                yield from gdn_stream(C, T, 1, [NTP + b], T["S"], False)
                P.dma("sp", O["s_gdn"][b].re("h k v -> k h v"), T["S"])
        run_interleaved([sample_gen(Ts[0], range(0, NS, 2)), sample_gen(Ts[1], range(1, NS, 2))])


def phase_c(C):
    P, A, I, O, S, K = C.P, C.A, C.I, C.O, C.S, C.K
    banks = C.banks
    SC = float(128 ** -0.5)
    rel = A.alloc([12], F32)
    P.dma("sp", rel[0:32], I["rel_table"])
    oh = A.alloc([3, 384], F32)
    P.dma("sp", oh[0:32], I["c_oh"].re("g b c -> b g c"))
    ohs = A.alloc([3, 128], F32)
    P.dma("sp", ohs[0:32], I["c_ohs"].re("g b c -> b g c"))
    oh0 = A.alloc([3, 1], F32)
    P.dma("sp", oh0[0:32], I["c_oh0"].re("g b c -> b g c"))
    negm = A.alloc([384], F32)
    P.dma("sp", negm, I["c_negm"])
    bias2 = A.alloc([12, 256], F32)
    biasS = A.alloc([12], F32)
    bias0 = A.alloc([12], F32)
    relb = A.alloc([128], F32)
    vp = Rot([A.alloc([384], F32) for _ in range(2)])
    for g in range(3):
        for h in range(4):
            gh = g * 4 + h
            ts(P, "dve", relb[0:32], K.ones[0:32], rel[0:32, gh:gh + 1], ALU.mult)
            bk = banks.next()
            mm(P, bk[:, 0:384], relb[0:32], oh[0:32, g, :])
            v_ = vp.next()
            tt(P, "dve", v_, bk[:, 0:384], negm, ALU.add)
            P.dma("sp", S["bias"][gh, 0:128, :], v_)
            P.dma("sp", S["bias"][gh, 128:256, :], v_)
        bk = banks.next()
        mm(P, bk[:, 0:4], ohs[0:32, g, :], rel[0:32, g * 4:(g + 1) * 4])
        cp(P, "dve", biasS[:, g * 4:(g + 1) * 4], bk[:, 0:4])
        bk = banks.next()
        mm(P, bk[0:1, 0:4], oh0[0:32, g, :], rel[0:32, g * 4:(g + 1) * 4])
        cp(P, "dve", bias0[0:1, g * 4:(g + 1) * 4], bk[0:1, 0:4])
    P.barrier()
    for gh in range(12):
        base = gh * 256 * 384 + 255
        P.dma("sp", bias2[:, gh, 0:128], S["bias"].raw([[383, 128], [1, 128]], off=base + 128 * 383))
        P.dma("sp", bias2[:, gh, 128:256], S["bias"].raw([[383, 128], [1, 128]], off=base))
    MODE = os.environ.get("MK_C", "prompt,sample").split(",")
    mark = A.off
    if "prompt" in MODE:
        acc2s = Rot([A.alloc([2, L], F32) for _ in range(2)])
        qcs = Rot([A.alloc([L], BF16) for _ in range(2)])
        kcs = Rot([A.alloc([L], BF16) for _ in range(2)])
        vcs = Rot([A.alloc([L // 128, 128], BF16) for _ in range(2)])
        lgs = Rot([A.alloc([256], F32) for _ in range(3)])
        Es = Rot([A.alloc([256], BF16) for _ in range(4)])
        zbt = A.alloc([L], F32)
        ybt = A.alloc([L], BF16)
        for s in range(NSEQ):
            for h in range(4):
                acc2 = acc2s.next()
                for g, dil in enumerate(DILS):
                    gh = g * 4 + h
                    lc = L // dil
                    nb = lc // 128
                    for r in range(dil):
                        qc, kc, vc = qcs.next()[:, 0:lc], kcs.next()[:, 0:lc], vcs.next()[:, 0:nb, :]
                        c0 = s * L + r * lc
                        P.dma("sp", qc, S["qb%d" % g][h * 128:(h + 1) * 128, c0:c0 + lc], acc=False)
                        P.dma("sp", kc, S["kb%d" % g][h * 128:(h + 1) * 128, c0:c0 + lc], acc=False)
                        P.dma("sp", vc, S["vb%d" % g][c0:c0 + lc, h * 128:(h + 1) * 128].re("(n p) d -> p n d", p=128),
                              acc=False)
                        Eprev = None

                        def logits(n):
                            nq = 256 if n < nb - 1 else 128
                            bk = banks.next()
                            mm(P, bk[:, 0:nq], kc[:, n * 128:(n + 1) * 128], qc[:, n * 128:n * 128 + nq])
                            lg = lgs.next()
                            stt(P, "dve", lg[:, 0:nq], bk[:, 0:nq], SC, bias2[:, gh, 0:nq], ALU.mult, ALU.add)
                            E = Es.next()
                            act(P, E[:, 0:nq], lg[:, 0:nq], AF.Exp)
                            return E
                        Enext = logits(0)
                        for n in range(nb):
                            E = Enext
                            if n + 1 < nb:
                                Enext = logits(n + 1)
                            b2 = banks.next()
                            if n > 0:
                                mm(P, b2[:, 0:128], vc[:, n - 1, :], Eprev[:, 128:256], True, False)
                            mm(P, b2[:, 0:128], vc[:, n, :], E[:, 0:128], n == 0, True)
                            if n > 0:
                                mm(P, b2[:, 128:256], K.onesb, Eprev[:, 128:256], True, False)
                            mm(P, b2[:, 128:256], K.onesb, E[:, 0:128], n == 0, True)
                            Eprev = E
                            lo = r + n * 128 * dil
                            dst = acc2[:, :, lo:lo + 127 * dil + 1:dil]
                            src = b2[:, 0:256].re("p (a q) -> p a q", a=2)
                            if g == 0:
                                cp(P, "act", dst, src)
                            else:
                                tt(P, "dve", dst, dst, src, ALU.add)
                P.dma("sp", zbt, S["zbT"][h * 128:(h + 1) * 128, s * L:(s + 1) * L], acc=False)
                for qd in range(4):
                    sl = slice(qd * 1024, (qd + 1) * 1024)
                    recip(P, acc2[:, 1, sl], acc2[:, 1, sl])
                    tt(P, "pool", acc2[:, 0, sl], acc2[:, 0, sl], acc2[:, 1, sl], ALU.mult)
                    tt(P, "pool", ybt[:, sl], acc2[:, 0, sl], zbt[:, sl], ALU.mult)
                P.dma("sp", S["ybT"][h * 128:(h + 1) * 128, s * L:(s + 1) * L], ybt)
    P.barrier()
    A.off = mark
    if "sample" in MODE:
        Kcs = [Rot([A.alloc([512], F32) for _ in range(2)]) for g in range(3)]
        Vcs = [Rot([A.alloc([512], F32) for _ in range(2)]) for g in range(3)]
        KcT = Rot([A.alloc([4, 128], F32) for _ in range(2)])
        vsb = Rot([A.alloc([3, 512], F32) for _ in range(2)])
        zbs = A.alloc([4, NS], F32)
        ybs = A.alloc([4, NS], BF16)
        sw = Rot([A.alloc([64], F32) for _ in range(2)])
        P.dma("sp", zbs, S["zbT"].re("(h p) t -> p h t", p=128)[:, :, NTP:NT])
        for b in range(NS):
            kk = [Kcs[g].next() for g in range(3)]
            vv = [Vcs[g].next() for g in range(3)]
            for g, dil in enumerate(DILS):
                P.dma("sp", kk[g], I["ck%d" % g][b, 0::dil, :], acc=False)
                P.dma("sp", vv[g], I["cv%d" % g][b, 0::dil, :], acc=False)
            vs_ = vsb.next()
            P.dma("sp", vs_[0:1], S["vs"][b:b + 1], acc=False)
            w_ = sw.next()
            bL = banks.next()
            for g in range(3):
                bk = banks.next()
                for h in range(4):
                    tr(P, bk[:, h * 128:(h + 1) * 128], kk[g][:, h * 128:(h + 1) * 128], K.ident)
                kt = KcT.next()
                cp(P, "act", kt, bk.re("p (h k) -> p h k", h=4))
                for h in range(4):
                    gh = g * 4 + h
                    mm(P, bL[:, gh:gh + 1], kt[:, h, :], K.qs[:, gh, b:b + 1])
            for gh in range(12):
                mm(P, bL[0:1, 16 + gh:17 + gh], K.ks[:, gh, b:b + 1], K.qs[:, gh, b:b + 1])
            lgS, ES, lg0, E0 = w_[:, 0:12], w_[:, 12:24], w_[0:1, 24:36], w_[0:1, 36:48]
            stt(P, "dve", lgS, bL[:, 0:12], SC, biasS, ALU.mult, ALU.add)
            act(P, ES, lgS, AF.Exp)
            stt(P, "dve", lg0, bL[0:1, 16:28], SC, bias0[0:1], ALU.mult, ALU.add)
            act(P, E0, lg0, AF.Exp)
            bO = banks.next()
            for h in range(4):
                for g in range(3):
                    gh = g * 4 + h
                    mm(P, bO[:, h:h + 1], vv[g][:, h * 128:(h + 1) * 128], ES[:, gh:gh + 1], g == 0, False)
                    mm(P, bO[:, h:h + 1], vs_[0:1, g, h * 128:(h + 1) * 128], E0[0:1, gh:gh + 1], False, g == 2)
                for g in range(3):
                    gh = g * 4 + h
                    mm(P, bO[:, 4 + h:5 + h], K.ones, ES[:, gh:gh + 1], g == 0, False)
                    mm(P, bO[:, 4 + h:5 + h], K.ones[0:1, :], E0[0:1, gh:gh + 1], False, g == 2)
            ob = w_[:, 48:56]
            cp(P, "dve", ob, bO[:, 0:8])
            recip(P, ob[:, 4:8], ob[:, 4:8])
            tt(P, "pool", ob[:, 0:4], ob[:, 0:4], ob[:, 4:8], ALU.mult)
            tt(P, "pool", ybs[:, :, b], ob[:, 0:4], zbs[:, :, b], ALU.mult)
        P.dma("sp", S["ybT"].re("(h p) t -> p h t", p=128)[:, :, NTP:NT], ybs)


def phase_d(C):
    P, A, I, O, S, K = C.P, C.A, C.I, C.O, C.S, C.K
    banks = C.banks
    wpa = A.alloc([8, DM], BF16)
    wpb = A.alloc([4, DM], BF16)
    wo = A.alloc([8, DM], BF16)
    gpost = A.alloc([DM], F32)
    P.dma("pool", wpa, I["w_proj_a"].re("(k p) n -> p k n", p=128))
    P.dma("pool", wpb, I["w_proj_b"].re("(k p) n -> p k n", p=128))
    P.dma("pool", wo, I["w_out"].re("(k p) n -> p k n", p=128))
    P.dma("sp", gpost, bcast_rows(I["g_post"], DM))
    yas = Rot([A.alloc([8, 512], BF16) for _ in range(2)])
    ybs_ = Rot([A.alloc([4, 512], BF16) for _ in range(2)])
    gas = Rot([A.alloc([8, 512], F32) for _ in range(int(os.environ.get('MK_GB', '2')))])
    gbs_ = Rot([A.alloc([8, 512], F32) for _ in range(int(os.environ.get('MK_GB', '2')))])
    xts = Rot([A.alloc([4, DM], F32) for _ in range(2)])
    hTs = Rot([A.alloc([8, 512], BF16) for _ in range(2)])
    t1s = Rot([A.alloc([512], F32) for _ in range(2)])
    t2s = Rot([A.alloc([512], F32) for _ in range(2)])
    junk = A.alloc([DM], F32)
    ysbs = Rot([A.alloc([DM], F32) for _ in range(2)])
    sts = Rot([A.alloc([4], F32) for _ in range(4)])
    fmv = lambda nm: S[nm].re("(k p) t -> p k t", p=128)
    TILES = [(ti * 512, 512) for ti in range(NTP // 512)] + [(NTP, NS)]
    for (t0, n) in TILES:
        nsub = 4 if n == 512 else 1
        np_ = 128 if n == 512 else NS
        ya, yb, ga, gb_, xt, hT = yas.next(), ybs_.next(), gas.next(), gbs_.next(), xts.next(), hTs.next()
        P.dma("sp", ya[:, :, 0:n], fmv("yaT")[:, :, t0:t0 + n], acc=False)
        P.dma("sp", yb[:, :, 0:n], fmv("ybT")[:, :, t0:t0 + n], acc=False)
        P.dma("sp", ga[:, :, 0:n], fmv("gaT")[:, :, t0:t0 + n], acc=False)
        P.dma("sp", gb_[:, :, 0:n], fmv("gbT")[:, :, t0:t0 + n], acc=False)
        if n == 512:
            P.dma("sp", xt, I["x_p"][t0:t0 + 512, :].re("(s p) d -> p s d", p=128), acc=False)
        else:
            P.dma("sp", xt[0:NS, 0, :], I["x_s"], acc=False)
        for e in range(8):
            pA, pB = banks.next(), banks.next()
            for k in range(8):
                mm(P, pA[:, 0:n], wpa[:, k, e * 128:(e + 1) * 128], ya[:, k, 0:n], k == 0, k == 7)
            for k in range(4):
                mm(P, pB[:, 0:n], wpb[:, k, e * 128:(e + 1) * 128], yb[:, k, 0:n], k == 0, k == 3)
            t1, t2 = t1s.next()[:, 0:n], t2s.next()[:, 0:n]
            tt(P, "dve", t1, ga[:, e, 0:n], pA[:, 0:n], ALU.mult)
            tt(P, "dve", t2, gb_[:, e, 0:n], pB[:, 0:n], ALU.mult)
            tt(P, "pool", hT[:, e, 0:n], t1, t2, ALU.add)
        for sb in range(nsub):
            bks = [banks.next(), banks.next()]
            for half in range(2):
                for k in range(8):
                    mm(P, bks[half][0:np_, :], hT[:, k, sb * 128:sb * 128 + np_], wo[:, k, half * 512:(half + 1) * 512],
                       k == 0, k == 7)
            for half in range(2):
                act(P, junk[0:np_, half * 512:(half + 1) * 512], bks[half][0:np_, :], AF.Square)
            st_ = sts.next()[0:np_]
            rsum(P, "dve", st_[:, 0:1], junk[0:np_])
            rsqrt(P, st_[:, 1:2], st_[:, 0:1], EPS, 1.0 / DM)
            ysb = ysbs.next()[0:np_]
            for half in range(2):
                hs = slice(half * 512, (half + 1) * 512)
                stt(P, "dve", ysb[:, hs], bks[half][0:np_, :], st_[:, 1:2], gpost[0:np_, hs], ALU.mult, ALU.mult)
            tt(P, "pool", ysb, ysb, xt[0:np_, sb, :], ALU.add)
            if n == 512:
                P.dma("sp", O["y_p"][t0 + sb * 128:t0 + (sb + 1) * 128, :], ysb)
            else:
                P.dma("sp", O["y_s"], ysb)


def rel_bucket_np(dist):
    import math
    max_exact = 16
    d = np.maximum(dist, 1).astype(np.float32)
    large = max_exact + (np.log(d / max_exact) / math.log(2048 / max_exact) * (32 - max_exact)).astype(np.int32)
    large = np.minimum(large, 31)
    return np.where(dist < max_exact, dist, large)


def host_consts():
    c = {}
    c["c_ident"] = np.eye(128, dtype=np.float32)
    k = np.arange(128)
    c["c_umat"] = (k[:, None] <= k[None, :]).astype(np.float32)
    c["c_maskT"] = np.where(k[None, :] >= k[:, None], 0.0, NEG).astype(np.float32)
    c["c_strictT"] = (k[None, :] > k[:, None]).astype(np.float32)
    oh = np.zeros((3, 32, 384), np.float32)
    ohs = np.zeros((3, 32, 128), np.float32)
    oh0 = np.zeros((3, 32, 1), np.float32)
    for g, dil in enumerate(DILS):
        bk = rel_bucket_np(np.arange(129, dtype=np.int32) * dil)
        for j in range(129):
            oh[g, bk[j], 127 + j] = 1.0
        for i in range(128):
            ohs[g, bk[128 - i], i] = 1.0
        oh0[g, bk[0], 0] = 1.0
    c["c_oh"] = oh
    negm = np.full((128, 384), NEG, np.float32)
    negm[:, 127:256] = 0.0
    c["c_negm"] = negm
    c["c_ohs"] = ohs
    c["c_oh0"] = oh0
    return c


_NC_CACHE = {}


def kernel(x_prompt, x_sample, state_gdn, state_conv, cache_k_w128, cache_v_w128,
           cache_k_w512, cache_v_w512, cache_k_w2048, cache_v_w2048, rel_table,
           g_pre, w_in, conv_w, a_log, dt_bias, g_head_norm, w_proj_a, w_proj_b, w_out, g_post):
    phases = os.environ.get("MK_PHASES", "ABCD")
    ncores = int(os.environ.get("MK_CORES", str(NCORES)))
    if phases not in _NC_CACHE:
        _NC_CACHE[phases] = build(phases)
    nc = _NC_CACHE[phases]
    f = lambda a: np.ascontiguousarray(np.asarray(a, dtype=np.float32))
    consts = host_consts()
    caches = ((cache_k_w128, cache_v_w128), (cache_k_w512, cache_v_w512), (cache_k_w2048, cache_v_w2048))
    in_maps = []
    for c in range(ncores):
        m = {}
        m["x_p"] = f(x_prompt[NSEQ * c:NSEQ * (c + 1)]).reshape(NTP, DM)
        m["x_s"] = f(x_sample[NS * c:NS * (c + 1)]).reshape(NS, DM)
        m["state_gdn"] = f(state_gdn[0, NS * c:NS * (c + 1)])
        m["state_conv"] = f(state_conv[0, NS * c:NS * (c + 1)])
        for g, (ck, cv) in enumerate(caches):
            m["ck%d" % g] = f(ck[0, NS * c:NS * (c + 1)]).reshape(NS, -1, 512)
            m["cv%d" % g] = f(cv[0, NS * c:NS * (c + 1)]).reshape(NS, -1, 512)
        m["rel_table"] = f(rel_table)
        m["g_pre"] = f(g_pre)
        m["w_in"] = f(w_in[0])
        m["conv_w"] = f(conv_w[0])
        m["a_log"] = f(a_log)
        m["dt_bias"] = f(dt_bias)
        m["g_head_norm"] = f(g_head_norm)
        m["w_proj_a"] = f(w_proj_a[0])
        m["w_proj_b"] = f(w_proj_b[0])
        m["w_out"] = f(w_out[0])
        m["g_post"] = f(g_post)
        m.update(consts)
        in_maps.append(m)
    if os.environ.get("MK_TRACE"):
        res = run_bass_kernel_spmd(nc, in_maps, core_ids=list(range(ncores)), trace=True)
        print("EXEC_TIME_NS", phases, res.exec_time_ns)
    else:
        res = run_bass_kernel_spmd(nc, in_maps, core_ids=list(range(ncores)))
    R = res.results
    cat = lambda k: np.concatenate([np.asarray(r[k]) for r in R], axis=0)
    B = NSEQ * ncores
    SB = NS * ncores
    outs = [cat("y_p").reshape(B, L, DM), cat("y_s").reshape(SB, 1, DM),
            cat("p_gdn").reshape(1, B, 8, 128, 128), cat("p_conv").reshape(1, B, 3, 3072)]
    for g, w in enumerate((128, 512, 2048)):
        outs.append(cat("p_k%d" % g).reshape(1, B, w, 4, 128))
        outs.append(cat("p_v%d" % g).reshape(1, B, w, 4, 128))
    outs.append(cat("s_gdn").reshape(1, SB, 8, 128, 128))
    outs.append(cat("s_conv").reshape(1, SB, 3, 3072))
    for g in range(3):
        outs.append(cat("s_k%d" % g).reshape(1, SB, 1, 4, 128))
        outs.append(cat("s_v%d" % g).reshape(1, SB, 1, 4, 128))
    return tuple(o.astype(np.float32) for o in outs)
```

```python
import os
from contextlib import ExitStack
import numpy as np
import concourse.bass as bass
import concourse.mybir as mybir
from concourse.bass_utils import run_bass_kernel_spmd

F32 = mybir.dt.float32
BF16 = mybir.dt.bfloat16
U8 = mybir.dt.uint8
AF = mybir.ActivationFunctionType
ALU = mybir.AluOpType
AX = mybir.AxisListType

NCORES = 8
L = 4096
NSEQ = 2
NS = 16
NTP = NSEQ * L
NT = NTP + NS
DM = 1024
INW = 11280
EPS = 1e-6
NEG = -1e30
DILS = (1, 4, 16)
O_ZA, O_A, O_QKVB, O_ZB, O_GA, O_GB = 3072, 4096, 4112, 8720, 9232, 10256


class Tl:
    __slots__ = ("w", "rd", "excl")

    def __init__(self, excl=False):
        self.w = {}
        self.rd = {}
        self.excl = excl


class V:
    __slots__ = ("t", "ap")

    def __init__(self, t, ap):
        self.t = t
        self.ap = ap

    def __getitem__(self, k):
        return V(self.t, self.ap[k])

    def re(self, s, **kw):
        return V(self.t, self.ap.rearrange(s, **kw))

    def raw(self, dims, off=0):
        return V(self.t, bass.AP(tensor=self.ap.tensor, offset=self.ap.offset + off, ap=dims))

    def bitcast(self, dt):
        return V(self.t, self.ap.bitcast(dt))


NDS = 12


class Prog:
    ENG = ("sp", "pe", "act", "dve", "pool")

    def __init__(self, nc):
        self.nc = nc
        self.streams = {e: [] for e in self.ENG}
        self.count = {e: 0 for e in self.ENG}
        self.known = {e: {} for e in self.ENG}
        self.dma_n = {"sp": 0, "pool": 0, "act": 0}
        self.latest = {}

    def _resolve(self, eng, deps):
        kn = self.known[eng]
        waits = []
        for k, v in deps.items():
            if k == "pe" and eng == "pe":
                continue
            if kn.get(k, 0) >= v:
                continue
            kn[k] = v
            waits.append((k, v))
        return waits

    @staticmethod
    def _deps(reads, writes, acc):
        deps = {}

        def add(d):
            for k, v in d.items():
                if deps.get(k, 0) < v:
                    deps[k] = v
        for t in reads:
            if t is not None:
                add(t.w)
                if t.excl:
                    add(t.rd)
        for t in writes:
            if t is not None:
                if not acc:
                    add(t.w)
                add(t.rd)
        return deps

    def _commit(self, tok, reads, writes, acc):
        k, v = tok
        self.latest[k] = max(self.latest.get(k, 0), v)
        for t in writes:
            if t is None:
                continue
            if acc:
                t.w[k] = max(t.w.get(k, 0), v)
            else:
                t.w = {k: v}
                t.rd = {}
        for t in reads:
            if t is None or t in writes:
                continue
            t.rd[k] = max(t.rd.get(k, 0), v)

    def op(self, eng, fn, reads=(), writes=(), acc=False):
        reads = [r.t if isinstance(r, V) else r for r in reads]
        writes = [r.t if isinstance(r, V) else r for r in writes]
        deps = self._deps(reads, writes, acc)
        waits = self._resolve(eng, deps)
        self.count[eng] += 1
        tok = (eng, self.count[eng])
        self.streams[eng].append((waits, fn, (eng, 1)))
        self._commit(tok, reads, writes, acc)

    def dma(self, q, out, in_, acc=True):
        if os.environ.get("MK_NOST") and out.t is None and out.ap.tensor.name.startswith("s_"):
            return
        n = self.dma_n[q]
        self.dma_n[q] += 1
        i = n % NDS
        val = 16 * (n // NDS + 1)
        key = ("d", q, i)
        reads = [in_.t]
        writes = [out.t]
        deps = self._deps(reads, writes, acc)
        if n >= NDS:
            deps[key] = max(deps.get(key, 0), val - 16)
        waits = self._resolve(q, deps)
        oa, ia = out.ap, in_.ap
        self.streams[q].append((waits, lambda e: e.dma_start(out=oa, in_=ia), (key, 16)))
        self._commit((key, val), reads, writes, acc)

    def barrier(self):
        for e in self.ENG:
            waits = self._resolve(e, dict(self.latest))
            if waits:
                self.streams[e].append((waits, None, None))

    def emit(self):
        nc = self.nc
        keys = list(self.ENG[1:]) + [("d", q, i) for q in ("sp", "pool", "act") for i in range(NDS)]
        with ExitStack() as st:
            st.enter_context(nc.allow_non_contiguous_dma(reason="small strided sample-path transfers"))
            sems = {}
            for k in keys:
                nm = k if isinstance(k, str) else "d%s%d" % (k[1], k[2])
                sems[k] = st.enter_context(nc.semaphore("s_" + nm))
            block = st.enter_context(nc.Block())
            decos = {"sp": block.sync, "pe": block.tensor, "act": block.scalar,
                     "dve": block.vector, "pool": block.gpsimd}
            for eng in self.ENG:
                stream = self.streams[eng]

                def body(e, stream=stream):
                    for waits, fn, inc in stream:
                        for k, v in waits:
                            e.wait_ge(sems[k], v)
                        if fn is not None:
                            fn(e).then_inc(sems[inc[0]], inc[1])
                decos[eng](body)


class Arena:
    def __init__(self, nc, nbytes):
        self.t = nc.alloc_sbuf_tensor("arena", [128, nbytes], U8)
        self.ap = self.t.ap()
        self.n = nbytes
        self.off = 0

    def alloc(self, free_shape, dt, parts=128):
        esz = 4 if dt == F32 else 2
        ne = int(np.prod(free_shape))
        nb = (ne * esz + 31) // 32 * 32
        assert self.off + nb <= self.n, "SBUF arena overflow %d + %d > %d" % (self.off, nb, self.n)
        a = self.ap[0:parts, self.off:self.off + ne * esz].bitcast(dt)
        self.off += nb
        if len(free_shape) == 2:
            a = a.rearrange("p (a b) -> p a b", a=free_shape[0])
        elif len(free_shape) == 3:
            a = a.rearrange("p (a b c) -> p a b c", a=free_shape[0], b=free_shape[1])
        return V(Tl(), a)


class Ctx:
    pass


def dram_in(nc, name, shape, dt=F32):
    return V(None, nc.dram_tensor(name, list(shape), dt, kind="ExternalInput").ap())


def dram_out(nc, name, shape, dt=F32):
    return V(None, nc.dram_tensor(name, list(shape), dt, kind="ExternalOutput").ap())


def dram_tmp(nc, name, shape, dt=F32):
    return V(None, nc.dram_tensor(name, list(shape), dt, kind="Internal").ap())


def mm(P, out, lhsT, rhs, start=True, stop=True):
    o, l, r = out.ap, lhsT.ap, rhs.ap
    P.op("pe", lambda e: e.matmul(o, l, r, start=start, stop=stop), [lhsT, rhs], [out])


def tr(P, out, in_, ident):
    o, i, d = out.ap, in_.ap, ident.ap
    P.op("pe", lambda e: e.transpose(o, i, d), [in_, ident], [out])


def act(P, out, in_, func, bias=0.0, scale=1.0, eng="act"):
    o, i = out.ap, in_.ap
    rd = [in_]
    b = bias
    if isinstance(bias, V):
        rd.append(bias)
        b = bias.ap
    s = scale
    if isinstance(scale, V):
        rd.append(scale)
        s = scale.ap
    P.op("act", lambda e: e.activation(o, i, func, bias=b, scale=s), rd, [out])


def cp(P, eng, out, in_):
    o, i = out.ap, in_.ap
    if eng == "act":
        P.op("act", lambda e: e.copy(o, i), [in_], [out])
    else:
        P.op(eng, lambda e: e.tensor_copy(o, i), [in_], [out])


def tt(P, eng, out, in0, in1, op):
    o, a, b = out.ap, in0.ap, in1.ap
    P.op(eng, lambda e: e.tensor_tensor(o, a, b, op), [in0, in1], [out])


def ts(P, eng, out, in0, s1, op0, s2=None, op1=None):
    o, a = out.ap, in0.ap
    rd = [in0]
    x1 = s1
    if isinstance(s1, V):
        rd.append(s1)
        x1 = s1.ap
    x2 = s2
    if isinstance(s2, V):
        rd.append(s2)
        x2 = s2.ap
    if op1 is None:
        P.op(eng, lambda e: e.tensor_scalar(o, a, x1, None, op0), rd, [out])
    else:
        P.op(eng, lambda e: e.tensor_scalar(o, a, x1, x2, op0, op1), rd, [out])


def stt(P, eng, out, in0, scalar, in1, op0, op1):
    o, a, b = out.ap, in0.ap, in1.ap
    rd = [in0, in1]
    s = scalar
    if isinstance(scalar, V):
        rd.append(scalar)
        s = scalar.ap
    P.op(eng, lambda e: e.scalar_tensor_tensor(o, a, s, b, op0, op1), rd, [out])


def memset(P, eng, out, val):
    o = out.ap
    P.op(eng, lambda e: e.memset(o, val), [], [out])


def rsum(P, eng, out, in_):
    o, i = out.ap, in_.ap
    P.op(eng, lambda e: e.reduce_sum(o, i, AX.X), [in_], [out])


def recip(P, out, in_):
    o, i = out.ap, in_.ap
    P.op("dve", lambda e: e.reciprocal(o, i), [in_], [out])


def rsqrt(P, out, in_, eps, scale=1.0):
    act(P, out, in_, AF.Ln, bias=eps, scale=scale)
    act(P, out, out, AF.Exp, scale=-0.5)


class Rot:
    def __init__(self, items):
        self.items = items
        self.i = 0

    def next(self):
        v = self.items[self.i % len(self.items)]
        self.i += 1
        return v


def build(phases="ABCD"):
    nc = bass.Bass("TRN2", target_bir_lowering=False)
    P = Prog(nc)
    C = Ctx()
    C.nc, C.P = nc, P
    I = {}
    I["x_p"] = dram_in(nc, "x_p", [NTP, DM])
    I["x_s"] = dram_in(nc, "x_s", [NS, DM])
    I["state_gdn"] = dram_in(nc, "state_gdn", [NS, 8, 128, 128])
    I["state_conv"] = dram_in(nc, "state_conv", [NS, 3, 3072])
    for g, w in enumerate((128, 512, 2048)):
        I["ck%d" % g] = dram_in(nc, "ck%d" % g, [NS, w, 512])
        I["cv%d" % g] = dram_in(nc, "cv%d" % g, [NS, w, 512])
    I["rel_table"] = dram_in(nc, "rel_table", [32, 12])
    I["g_pre"] = dram_in(nc, "g_pre", [1, DM])
    I["w_in"] = dram_in(nc, "w_in", [DM, INW])
    I["conv_w"] = dram_in(nc, "conv_w", [3072, 4])
    I["a_log"] = dram_in(nc, "a_log", [1, 8])
    I["dt_bias"] = dram_in(nc, "dt_bias", [1, 8])
    I["g_head_norm"] = dram_in(nc, "g_head_norm", [1, 128])
    I["w_proj_a"] = dram_in(nc, "w_proj_a", [1024, DM])
    I["w_proj_b"] = dram_in(nc, "w_proj_b", [512, DM])
    I["w_out"] = dram_in(nc, "w_out", [DM, DM])
    I["g_post"] = dram_in(nc, "g_post", [1, DM])
    I["c_ident"] = dram_in(nc, "c_ident", [128, 128])
    I["c_umat"] = dram_in(nc, "c_umat", [128, 128])
    I["c_maskT"] = dram_in(nc, "c_maskT", [128, 128])
    I["c_strictT"] = dram_in(nc, "c_strictT", [128, 128])
    I["c_oh"] = dram_in(nc, "c_oh", [3, 32, 384])
    I["c_negm"] = dram_in(nc, "c_negm", [128, 384])
    I["c_ohs"] = dram_in(nc, "c_ohs", [3, 32, 128])
    I["c_oh0"] = dram_in(nc, "c_oh0", [3, 32, 1])
    O = {}
    O["y_p"] = dram_out(nc, "y_p", [NTP, DM])
    O["y_s"] = dram_out(nc, "y_s", [NS, DM])
    O["p_gdn"] = dram_out(nc, "p_gdn", [NSEQ, 8, 128, 128])
    O["p_conv"] = dram_out(nc, "p_conv", [NSEQ, 3, 3072])
    for g, w in enumerate((128, 512, 2048)):
        O["p_k%d" % g] = dram_out(nc, "p_k%d" % g, [NSEQ, w, 512])
        O["p_v%d" % g] = dram_out(nc, "p_v%d" % g, [NSEQ, w, 512])
        O["s_k%d" % g] = dram_out(nc, "s_k%d" % g, [NS, 512])
        O["s_v%d" % g] = dram_out(nc, "s_v%d" % g, [NS, 512])
    O["s_gdn"] = dram_out(nc, "s_gdn", [NS, 8, 128, 128])
    O["s_conv"] = dram_out(nc, "s_conv", [NS, 3, 3072])
    S = {}
    S["qT"] = dram_tmp(nc, "s_qT", [1024, NT])
    S["kT"] = dram_tmp(nc, "s_kT", [1024, NT])
    S["vT"] = dram_tmp(nc, "s_vT", [1024, NT])
    S["zaT"] = dram_tmp(nc, "s_zaT", [1024, NT])
    S["zbT"] = dram_tmp(nc, "s_zbT", [512, NT])
    S["gaT"] = dram_tmp(nc, "s_gaT", [1024, NT])
    S["gbT"] = dram_tmp(nc, "s_gbT", [1024, NT])
    S["gbeta"] = dram_tmp(nc, "s_gbeta", [NT, 16])
    for g in range(3):
        S["qb%d" % g] = dram_tmp(nc, "s_qb%d" % g, [512, NTP], BF16)
        S["kb%d" % g] = dram_tmp(nc, "s_kb%d" % g, [512, NTP], BF16)
        S["vb%d" % g] = dram_tmp(nc, "s_vb%d" % g, [NTP, 512], BF16)
    S["vs"] = dram_tmp(nc, "s_vs", [NS, 3, 512])
    S["yaT"] = dram_tmp(nc, "s_yaT", [1024, NT], BF16)
    S["ybT"] = dram_tmp(nc, "s_ybT", [512, NT], BF16)
    S["bias"] = dram_tmp(nc, "s_bias", [12, 256, 384])
    C.I, C.O, C.S = I, O, S

    A = Arena(nc, 212480)
    C.A = A
    pst = nc.alloc_psum_tensor("psum", [128, 8, 512], F32)
    psa = pst.ap()
    C.banks = Rot([V(Tl(excl=True), psa[:, b, :]) for b in range(8)])

    K = Ctx()
    C.K = K
    K.ident = A.alloc([128], F32)
    K.identb = A.alloc([128], BF16)
    K.umat = A.alloc([128], F32)
    K.maskT = A.alloc([128], F32)
    K.strictT = A.alloc([128], F32)
    K.ones = A.alloc([128], F32)
    K.onesb = A.alloc([128], BF16)
    K.meanm = A.alloc([128], F32)
    K.c128 = A.alloc([128], F32)
    K.qs = A.alloc([12, NS], F32)
    K.ks = A.alloc([12, NS], F32)
    P.dma("sp", K.ident, I["c_ident"])
    P.dma("pool", K.identb, I["c_ident"])
    P.dma("sp", K.umat, I["c_umat"])
    P.dma("sp", K.maskT, I["c_maskT"])
    P.dma("sp", K.strictT, I["c_strictT"])
    memset(P, "pool", K.ones, 1.0)
    memset(P, "pool", K.onesb, 1.0)
    memset(P, "pool", K.meanm, 1.0 / 128.0)
    memset(P, "pool", K.c128, 128.0)
    C.mark0 = A.off

    if "A" in phases:
        phase_a(C)
    P.barrier()
    A.off = C.mark0
    if "B" in phases:
        phase_b(C)
    P.barrier()
    A.off = C.mark0
    if "C" in phases:
        phase_c(C)
    P.barrier()
    A.off = C.mark0
    if "D" in phases:
        phase_d(C)
    P.barrier()
    P.emit()
    return nc


def bcast_rows(v, n):
    return v.raw([[0, 128], [1, n]])


def phase_a(C):
    P, A, I, O, S, K = C.P, C.A, C.I, C.O, C.S, C.K
    banks = C.banks
    xnT = A.alloc([8, NT], BF16)
    gpre = A.alloc([DM], F32)
    convw = A.alloc([24, 4], F32)
    P.dma("sp", gpre, bcast_rows(I["g_pre"], DM))
    P.dma("sp", convw, I["conv_w"].re("(c p) w -> p c w", p=128))

    stT = A.alloc([24, 3, NS], F32)
    mark1 = A.off
    xts = Rot([A.alloc([DM], F32) for _ in range(4)])
    xpre = {}
    sqs = Rot([A.alloc([DM], F32) for _ in range(2)])
    xns = Rot([A.alloc([DM], BF16) for _ in range(2)])
    sts = Rot([A.alloc([4], F32) for _ in range(4)])
    DBG = int(os.environ.get("MK_DBG", "9"))
    for sub in (range(NTP // 128 + 1) if DBG >= 2 else []):
        if sub < NTP // 128:
            np_, src, t0 = 128, I["x_p"][sub * 128:(sub + 1) * 128, :], sub * 128
        else:
            np_, src, t0 = NS, I["x_s"], NTP
        if sub == 0:
            for pf in range(2):
                xpre[pf] = xts.next()
                P.dma("sp", xpre[pf], I["x_p"][pf * 128:(pf + 1) * 128, :])
        xt = xpre.pop(sub)[0:np_]
        nsb = sub + 2
        if nsb <= NTP // 128:
            xpre[nsb] = xts.next()
            if nsb < NTP // 128:
                P.dma("sp", xpre[nsb], I["x_p"][nsb * 128:(nsb + 1) * 128, :])
            else:
                P.dma("sp", xpre[nsb][0:NS], I["x_s"])
        sq = sqs.next()[0:np_]
        xn = xns.next()[0:np_]
        stt_ = sts.next()[0:np_]
        act(P, sq, xt, AF.Square)
        rsum(P, "dve", stt_[:, 0:1], sq)
        rsqrt(P, stt_[:, 2:3], stt_[:, 0:1], EPS, 1.0 / DM)
        stt(P, "dve", xn, xt, stt_[:, 2:3], gpre[0:np_], ALU.mult, ALU.mult)
        bk = banks.next()
        bkb = bk.bitcast(BF16)
        for kc in range(8):
            tr(P, bkb[:, kc * 128:kc * 128 + np_], xn[:, kc * 128:(kc + 1) * 128], K.identb[0:np_, 0:np_])
        src_ps = bkb.re("p (k t) -> p k t", k=8)[:, :, 0:np_]
        cp(P, "act" if sub % 2 else "dve", xnT[:, :, t0:t0 + np_], src_ps)
    stin = A.alloc([3072], F32)
    P.dma("sp", stin[0:48], I["state_conv"].re("b r c -> (b r) c"))
    for c4 in range(6 if DBG >= 3 else 0):
        bk = banks.next()
        for m in range(4):
            c = c4 * 4 + m
            tr(P, bk[:, m * 48:(m + 1) * 48], stin[0:48, c * 128:(c + 1) * 128], K.ident[0:48, 0:48])
        cp(P, "dve", stT[:, c4 * 4:(c4 + 1) * 4, :, :].re("p c r b -> p c b r"),
           bk[:, 0:192].re("p (c b r) -> p c b r", c=4, b=NS))
    P.barrier()
    A.off = mark1

    wbs = Rot([A.alloc([8, 512], BF16) for _ in range(2)])
    w_view = I["w_in"].re("(k p) n -> p k n", p=128)

    WSEQ = [(j * 512, 512) for j in range(6)]
    WSEQ += [(O_ZA, 512), (O_ZA + 512, 512), (O_ZB, 512), (O_GA, 512), (O_GA + 512, 512), (O_GB, 512), (O_GB + 512, 512)]
    WSEQ += [(O_A, 16)]
    for g_ in range(3):
        WSEQ += [(O_QKVB + g_ * 1536, 512), (O_QKVB + g_ * 1536 + 512, 512), (O_QKVB + g_ * 1536 + 1024, 512)]
    wq = {"i": 0, "pend": None}

    def issue_w(i):
        col0, width = WSEQ[i]
        wb = wbs.next()
        P.dma("pool", wb[:, :, 0:width], w_view[:, :, col0:col0 + width], acc=False)
        return wb

    def load_w(col0, width):
        i = wq["i"]
        assert WSEQ[i] == (col0, width), (WSEQ[i], col0, width)
        wb = wq["pend"] if wq["pend"] is not None else issue_w(i)
        wq["pend"] = issue_w(i + 1) if i + 1 < len(WSEQ) else None
        wq["i"] = i + 1
        return wb

    def fm(bank, wb, m, t0, n):
        for kc in range(8):
            mm(P, bank[:, 0:n], wb[:, kc, m * 128:(m + 1) * 128], xnT[:, kc, t0:t0 + n], kc == 0, kc == 7)

    def tm(bank, wb, tok, np_, width=512):
        for kc in range(8):
            mm(P, bank[0:np_, 0:width], tok(kc), wb[:, kc, 0:width], kc == 0, kc == 7)

    TILES = [(ti * 512, 512) for ti in range(NTP // 512)] + [(NTP, NS)]
    evi = [0]

    def evac_eng():
        evi[0] += 1
        return "act" if evi[0] % 2 else "dve"

    osb = Rot([A.alloc([512], F32) for _ in range(3)])
    tmb = Rot([A.alloc([512], F32) for _ in range(2)])
    mark2 = A.off
    stage = Rot([A.alloc([515], F32) for _ in range(3)])
    cvs = Rot([A.alloc([512], F32) for _ in range(2)])
    svs = Rot([A.alloc([512], F32) for _ in range(5)])
    sqq = Rot([A.alloc([512], F32) for _ in range(4)])
    rss = Rot([A.alloc([512], F32) for _ in range(2)])
    pending = []
    DEFER = int(os.environ.get('MK_DEFER', '1'))
    if DBG >= 4:
        P.dma("sp", O["s_conv"][:, 0:2, :], I["state_conv"][:, 1:3, :])

    SUB = os.environ.get("MK_SUB", "conv,simple,ab,att").split(",")
    for j in range(6 if "conv" in SUB else 0):
        wb = load_w(j * 512, 512)
        for m in range(4):
            c = j * 4 + m
            kind = "q" if c < 8 else ("k" if c < 16 else "v")
            dst = S["qT"] if c < 8 else (S["kT"] if c < 16 else S["vT"])
            r0 = (c % 8) * 128
            prevbox = [None]

            def s1a(t0, n):
                bk = banks.next()
                fm(bk, wb, m, t0, n)
                sg = stage.next()
                if n == 512:
                    cp(P, "act", sg[:, 3:515], bk[:, 0:512])
                    if t0 % L == 0:
                        memset(P, "pool", sg[:, 0:3], 0.0)
                    else:
                        cp(P, "pool", sg[:, 0:3], prevbox[0][:, 512:515])
                    prevbox[0] = sg
                    taps = [sg[:, w:w + 512] for w in range(4)]
                else:
                    cp(P, "act", sg[:, 0:n], bk[:, 0:n])
                    taps = [stT[:, c, 0, :], stT[:, c, 1, :], stT[:, c, 2, :], sg[:, 0:n]]
                return taps, t0, n

            def s1b(taps, t0, n):
                cv = cvs.next()[:, 0:n]
                ts(P, "dve", cv, taps[0], convw[:, c, 0:1], ALU.mult)
                for w in range(1, 4):
                    stt(P, "dve", cv, taps[w], convw[:, c, w:w + 1], cv, ALU.mult, ALU.add)
                sv = svs.next()[:, 0:n]
                act(P, sv, cv, AF.Silu)
                if kind == "v":
                    P.dma("sp", dst[r0:r0 + 128, t0:t0 + n], sv)
                else:
                    sq = sqq.next()[:, 0:n]
                    tt(P, "pool", sq, sv, sv, ALU.mult)

                    def stage2(sq=sq, sv=sv, n=n, kind=kind, dst=dst, r0=r0, t0=t0):
                        b2 = banks.next()
                        mm(P, b2[:, 0:n], K.c128 if kind == "q" else K.ones, sq)
                        rs = rss.next()[:, 0:n]
                        yield
                        act(P, rs, b2[:, 0:n], AF.Ln, bias=EPS * (128.0 if kind == "q" else 1.0))
                        yield
                        act(P, rs, rs, AF.Exp, scale=-0.5)
                        yield
                        ob = osb.next()[:, 0:n]
                        tt(P, "pool", ob, sv, rs, ALU.mult)
                        P.dma("sp", dst[r0:r0 + 128, t0:t0 + n], ob)
                    pending.append(stage2())
                    if len(pending) >= DEFER + 2:
                        run_interleaved(pending[0:2])
                        del pending[0:2]
            nxt_info = s1a(*TILES[0])
            for ti in range(len(TILES)):
                info = nxt_info
                if ti + 1 < len(TILES):
                    nxt_info = s1a(*TILES[ti + 1])
                s1b(*info)
        run_interleaved(pending)
        del pending[:]
        for s in range(NSEQ):
            bk = banks.next()
            tm(bk, wb, lambda kc, s=s: xnT[:, kc, s * L + L - 128:s * L + L], 128)
            tb = tmb.next()
            cp(P, evac_eng(), tb, bk)
            P.dma("sp", O["p_conv"][s, :, j * 512:(j + 1) * 512], tb[125:128, :])
        bk = banks.next()
        tm(bk, wb, lambda kc: xnT[:, kc, NTP:NT], NS)
        tb = tmb.next()
        cp(P, evac_eng(), tb[0:NS], bk[0:NS])
        P.dma("sp", O["s_conv"][:, 2, j * 512:(j + 1) * 512], tb[0:NS, :])

    P.barrier()
    A.off = mark2
    def simple_block(col0, dst, r0, func):
        wb = load_w(col0, 512)
        for m in range(4):
            for (t0, n) in TILES:
                bk = banks.next()
                fm(bk, wb, m, t0, n)
                ob = osb.next()[:, 0:n]
                act(P, ob, bk[:, 0:n], func)
                P.dma("sp", dst[r0 + m * 128:r0 + (m + 1) * 128, t0:t0 + n], ob)

    if "simple" in SUB:
        for j in range(2):
            simple_block(O_ZA + j * 512, S["zaT"], j * 512, AF.Silu)
        simple_block(O_ZB, S["zbT"], 0, AF.Silu)
        for j in range(2):
            simple_block(O_GA + j * 512, S["gaT"], j * 512, AF.Sigmoid)
        for j in range(2):
            simple_block(O_GB + j * 512, S["gbT"], j * 512, AF.Sigmoid)

    wb = load_w(O_A, 16)
    dtb = A.alloc([4, 8], F32)
    nega = A.alloc([4, 8], F32)
    for q in range(4):
        P.dma("sp", dtb[:, q, :], bcast_rows(I["dt_bias"], 8))
        P.dma("sp", nega[:, q, :], bcast_rows(I["a_log"], 8))
    act(P, nega, nega, AF.Exp)
    ts(P, "dve", nega, nega, -1.0, ALU.mult)
    abt = Rot([A.alloc([6, 4, 8], F32) for _ in range(2)])
    gbs = Rot([A.alloc([4, 16], F32) for _ in range(2)])
    for (t0, n) in (TILES if "ab" in SUB else []):
        nsub = 4 if n == 512 else 1
        np_ = 128 if n == 512 else NS
        bk = banks.next()
        for sb in range(nsub):
            for kc in range(8):
                mm(P, bk[0:np_, sb * 16:(sb + 1) * 16], xnT[:, kc, t0 + sb * 128:t0 + sb * 128 + np_],
                   wb[:, kc, 0:16], kc == 0, kc == 7)
        pv = bk[0:np_, 0:nsub * 16].re("p (s c) -> p s c", c=16)
        w_ = abt.next()[0:np_, :, 0:nsub, :]
        gb = gbs.next()[0:np_, 0:nsub, :]
        xx, ax, ee, ll = w_[:, 0], w_[:, 1], w_[:, 2], w_[:, 3]
        tt(P, "dve", xx, pv[:, :, 0:8], dtb[0:np_, 0:nsub, :], ALU.add)
        act(P, ax, xx, AF.Abs)
        act(P, ee, ax, AF.Exp, scale=-1.0)
        act(P, ll, ee, AF.Ln, bias=1.0)
        stt(P, "dve", xx, xx, 0.0, ll, ALU.max, ALU.add)
        tt(P, "dve", gb[:, :, 0:8], xx, nega[0:np_, 0:nsub, :], ALU.mult)
        act(P, gb[:, :, 8:16], pv[:, :, 8:16], AF.Sigmoid)
        if n == 512:
            P.dma("sp", S["gbeta"][t0:t0 + 512, :].re("(s p) c -> p s c", p=128), gb)
        else:
            P.dma("sp", S["gbeta"][t0:t0 + NS, :], gb[:, 0, :])

    stgb = Rot([A.alloc([2048], BF16) for _ in range(2)])
    vbb = Rot([A.alloc([512], BF16) for _ in range(3)])
    GS = [int(x) for x in os.environ.get("MK_G", "0,1,2").split(",")]
    for g, dil in (enumerate(DILS) if "att" in SUB else []):
        if g not in GS:
            continue
        lc = L // dil
        spc = 2048 // dil
        ATT = os.environ.get("MK_ATT", "qk,ktm,v").split(",")
        for t, nm in (((0, "qb"), (1, "kb")) if "qk" in ATT else []):
            wb = load_w(O_QKVB + g * 1536 + t * 512, 512)
            dstT = S["%s%d" % (nm, g)]
            for h in range(4):
                for spn in range(NTP // 2048):
                    sgb = stgb.next()
                    for sb in range(4):
                        bk = banks.next()
                        fm(bk, wb, h, spn * 2048 + sb * 512, 512)
                        i0 = sb * 512 // dil
                        cp(P, evac_eng(), sgb.re("p (r i) -> p r i", r=dil)[:, :, i0:i0 + 512 // dil],
                           bk.re("p (i r) -> p r i", r=dil))
                    s_, n_ = spn // 2, spn % 2
                    d = dstT[h * 128:(h + 1) * 128, s_ * L:(s_ + 1) * L].re("p (r i) -> p r i", r=dil)
                    P.dma("sp", d[:, :, n_ * spc:(n_ + 1) * spc], sgb.re("p (r i) -> p r i", r=dil))
                bk = banks.next()
                fm(bk, wb, h, NTP, NS)
                cp(P, evac_eng(), (K.qs if t == 0 else K.ks)[:, g * 4 + h, :], bk[:, 0:NS])
            if t == 1 and "ktm" in ATT:
                for s in range(NSEQ):
                    for r in range(dil):
                        base = s * L + L - 128 * dil + r
                        bk = banks.next()
                        tm(bk, wb, lambda kc, base=base: xnT[:, kc, base:base + 128 * dil:dil], 128)
                        tb = tmb.next()
                        cp(P, evac_eng(), tb, bk)
                        P.dma("sp", O["p_k%d" % g][s, r::dil, :], tb)
                bk = banks.next()
                tm(bk, wb, lambda kc: xnT[:, kc, NTP:NT], NS)
                tb = tmb.next()
                cp(P, evac_eng(), tb[0:NS], bk[0:NS])
                P.dma("sp", O["s_k%d" % g], tb[0:NS])
        if "v" not in ATT:
            continue
        wb = load_w(O_QKVB + g * 1536 + 1024, 512)
        nb = lc // 128
        MKV = os.environ.get("MK_V", "main,pv,samp").split(",")
        for s in range(NSEQ if "main" in MKV else 0):
            for r in range(dil):
                for n in range(nb):
                    base = s * L + n * 128 * dil + r
                    bk = banks.next()
                    tm(bk, wb, lambda kc, base=base: xnT[:, kc, base:base + 128 * dil:dil], 128)
                    vb = vbb.next()
                    cp(P, evac_eng(), vb, bk)
                    row0 = s * L + r * lc + n * 128
                    P.dma("sp", S["vb%d" % g][row0:row0 + 128, :], vb)
                    if n == nb - 1 and "pv" in MKV:
                        tb = tmb.next()
                        cp(P, evac_eng(), tb, bk)
                        P.dma("sp", O["p_v%d" % g][s, r::dil, :], tb)
        if "samp" not in MKV:
            continue
        bk = banks.next()
        tm(bk, wb, lambda kc: xnT[:, kc, NTP:NT], NS)
        tb = tmb.next()
        cp(P, evac_eng(), tb[0:NS], bk[0:NS])
        P.dma("sp", O["s_v%d" % g], tb[0:NS])
        P.dma("sp", S["vs"][:, g, :], tb[0:NS])


def bc(v, dims):
    p = v.ap.ap[0]
    return v.raw([[p[0], p[1]]] + dims)


def gdn_stream(C, T, c, cols, Sst, first_zero, h0=0, NH=8):
    P, K, S = C.P, C.K, C.S
    banks = C.banks
    hs = slice(h0, h0 + NH)
    kT_v = S["kT"].re("(h p) t -> p h t", p=128)[:, hs]
    qT_v = S["qT"].re("(h p) t -> p h t", p=128)[:, hs]
    vT_v = S["vT"].re("(h p) t -> p h t", p=128)[:, hs]
    za_v = S["zaT"].re("(h p) t -> p h t", p=128)[:, hs]
    ya_v = S["yaT"].re("(h p) t -> p h t", p=128)[:, hs]
    ghn = T["ghn"]
    HG = [(hb, min(4, NH - hb)) for hb in range(0, NH, 4)]
    for ci, col0 in enumerate(cols):
        def loads(cj):
            cl = cols[cj]
            kq_ = T["kq"][cj % 2][:, :, :, 0:c]
            P.dma("sp", kq_[:, :, 0, :], kT_v[:, :, cl:cl + c])
            P.dma("sp", kq_[:, :, 1, :], qT_v[:, :, cl:cl + c])
            P.dma("sp", T["vT"][cj % 2][:, :, 0:c], vT_v[:, :, cl:cl + c])
            P.dma("sp", T["gb"][cj % 2][0:c], S["gbeta"][cl:cl + c, :])
            P.dma("sp", T["za"][cj % 2][:, :, 0:c], za_v[:, :, cl:cl + c])
        if ci == 0:
            loads(0)
        if ci + 1 < len(cols):
            loads(ci + 1)
        kq = T["kq"][ci % 2][:, :, :, 0:c]
        vT = T["vT"][ci % 2][:, :, 0:c]
        za = T["za"][ci % 2][:, :, 0:c]
        gb = T["gb"][ci % 2][0:c]
        gg, bb = gb[:, h0:h0 + NH], gb[:, 8 + h0:8 + h0 + NH]
        sm = T["sm"]
        Gcol, GL, kdecs, glast, nbeta = (sm[:, i * NH:(i + 1) * NH] for i in range(5))
        t = [x[0:c, :, 0:c] for x in T["t"]]
        tf = [x[:, :, 0:c] for x in T["t"]]
        td = [x[0:c] for x in T["t"]]
        bk = banks.next()
        mm(P, bk[0:c, 0:NH], K.umat[0:c, 0:c], gg)
        mm(P, bk[:, 8:8 + NH], K.ones[0:c, :], gg)
        cp(P, "dve", Gcol[0:c], bk[0:c, 0:NH])
        cp(P, "dve", GL, bk[:, 8:8 + NH])
        tt(P, "pool", kdecs[0:c], GL[0:c], Gcol[0:c], ALU.subtract)
        act(P, kdecs[0:c], kdecs[0:c], AF.Exp)
        act(P, glast, GL, AF.Exp)
        ts(P, "pool", nbeta[0:c], bb, -1.0, ALU.mult)
        yield
        Ug = t[0]
        tt(P, "dve", Ug, bc(K.umat[0:c, 0:c], [[0, NH], [1, c]]), bc(gg, [[1, NH], [0, c]]), ALU.mult)
        EGb = tf[2]
        dT = t[1]
        for hb, nh in HG:
            bX = banks.next()
            mm(P, bX[:, 0:nh * c].re("p (h i) -> p h i", h=nh), K.ones[0:c, :], Ug[:, hb:hb + nh, :])
            act(P, EGb[:, hb:hb + nh, :], bX[:, 0:nh * c].re("p (h i) -> p h i", h=nh), AF.Exp)
            for h in range(hb, hb + nh):
                stt(P, "dve", dT[:, h, :], bX[0:c, (h - hb) * c:(h - hb + 1) * c], Gcol[0:c, h:h + 1],
                    K.maskT[0:c, 0:c], ALU.subtract, ALU.add)
        decT = t[3]
        act(P, decT, dT, AF.Exp)
        DSb = t[4]
        tt(P, "pool", DSb, decT, bc(K.strictT[0:c, 0:c], [[0, NH], [1, c]]), ALU.mult)
        tt(P, "pool", DSb, DSb, bc(nbeta[0:c], [[1, NH], [0, c]]), ALU.mult)
        kqd = T["kqd"][:, :, :, 0:c]
        tt(P, "pool", kqd, kq, bc(EGb, [[EGb.ap.ap[1][0], NH], [0, 2], [1, c]]), ALU.mult)
        yield
        vtok, kdec = td[5], td[6]
        for src, dst, scale in ((vT, vtok, None), (kq[:, :, 0, :], kdec, True)):
            for hb, nh in HG:
                bk = banks.next()
                for h in range(hb, hb + nh):
                    tr(P, bk[0:c, (h - hb) * 128:(h - hb + 1) * 128], src[:, h, :], K.ident)
                pv = bk[0:c, 0:nh * 128].re("p (h d) -> p h d", h=nh)
                if scale is None:
                    cp(P, "act", dst[:, hb:hb + nh, :], pv)
                else:
                    tt(P, "dve", dst[:, hb:hb + nh, :], pv, bc(kdecs[0:c, hb:hb + nh], [[1, nh], [0, 128]]), ALU.mult)
        yield
        pa, pb = T["pa"][0:c, :, :, 0:c], T["pb"][0:c, :, :, 0:c]
        qkT = t[7]
        for h2 in range(NH // 2):
            bk = banks.next()
            for hh in range(2):
                h = h2 * 2 + hh
                mm(P, bk[0:c, hh * 2 * c:(hh + 1) * 2 * c].re("p (k i) -> p k i", k=2), kq[:, h, 0, :], kq[:, h, :, :])
            pv = bk[0:c, 0:4 * c].re("p (h k i) -> p h k i", h=2, k=2)
            tt(P, "dve", pa[:, h2 * 2:h2 * 2 + 2, 0, :], pv[:, :, 0, :], DSb[:, h2 * 2:h2 * 2 + 2, :], ALU.mult)
            tt(P, "dve", qkT[:, h2 * 2:h2 * 2 + 2, :], pv[:, :, 1, :], decT[:, h2 * 2:h2 * 2 + 2, :], ALU.mult)
        yield
        Pm = t[8]
        if c > 1:
            for hb, nh in HG:
                bk = banks.next()
                for h in range(hb, hb + nh):
                    tr(P, bk[0:c, (h - hb) * c:(h - hb + 1) * c], pa[:, h, 0, :], K.ident[0:c, 0:c])
                cp(P, "act", pa[:, hb:hb + nh, 1, :], bk[0:c, 0:nh * c].re("p (h i) -> p h i", h=nh))
            tt(P, "pool", Pm, pa[:, :, 0, :], bc(K.ident[0:c, 0:c], [[0, NH], [1, c]]), ALU.add)
            yield
            cur, nxt = pa, pb
            for lvl in range(1, 7):
                for h2 in range(NH // 2):
                    bk = banks.next()
                    for hh in range(2):
                        h = h2 * 2 + hh
                        if lvl < 6:
                            mm(P, bk[0:c, hh * 2 * c:hh * 2 * c + c], cur[:, h, 1, :], cur[:, h, 0, :])
                        mm(P, bk[0:c, hh * 2 * c + c:(hh + 1) * 2 * c], cur[:, h, 0, :], cur[:, h, 1, :])
                    pv = bk[0:c, 0:4 * c].re("p (h k i) -> p h k i", h=2, k=2)
                    if lvl < 6:
                        cp(P, "act" if h2 % 2 else "dve", nxt[:, h2 * 2:h2 * 2 + 2, :, :], pv)
                    else:
                        cp(P, "act" if h2 % 2 else "dve", nxt[:, h2 * 2:h2 * 2 + 2, 1, :], pv[:, :, 1, :])
                yield
                for hb, nh in HG:
                    bk = banks.next()
                    for h in range(hb, hb + nh):
                        mm(P, bk[0:c, (h - hb) * c:(h - hb + 1) * c], nxt[:, h, 1, :], Pm[:, h, :])
                    tt(P, "dve", Pm[:, hb:hb + nh, :], Pm[:, hb:hb + nh, :],
                       bk[0:c, 0:nh * c].re("p (h i) -> p h i", h=nh), ALU.add)
                yield
                cur, nxt = nxt, cur
        else:
            memset(P, "pool", Pm, 1.0)
        if ci == 0 and first_zero:
            memset(P, "pool", Sst, 0.0)
        R = td[0]
        for hb, nh in HG:
            bk = banks.next()
            for h in range(hb, hb + nh):
                mm(P, bk[0:c, (h - hb) * 128:(h - hb + 1) * 128], kqd[:, h, 0, :], Sst[:, h, :])
            tt(P, "dve", R[:, hb:hb + nh, :], vtok[:, hb:hb + nh, :],
               bk[0:c, 0:nh * 128].re("p (h d) -> p h d", h=nh), ALU.subtract)
        yield
        vn = td[1]
        for hb, nh in HG:
            bk = banks.next()
            for h in range(hb, hb + nh):
                mm(P, bk[0:c, (h - hb) * 128:(h - hb + 1) * 128], Pm[:, h, :], R[:, h, :])
            tt(P, "dve", vn[:, hb:hb + nh, :], bk[0:c, 0:nh * 128].re("p (h d) -> p h d", h=nh),
               bc(bb[:, hb:hb + nh], [[1, nh], [0, 128]]), ALU.mult)
        yield
        oT = tf[5]
        for hb, nh in HG:
            bk = banks.next()
            for h in range(hb, hb + nh):
                o_ = bk[:, (h - hb) * c:(h - hb + 1) * c]
                mm(P, o_, Sst[:, h, :], kqd[:, h, 1, :], True, False)
                mm(P, o_, vn[:, h, :], qkT[:, h, :], False, True)
            cp(P, "act", oT[:, hb:hb + nh, :], bk[:, 0:nh * c].re("p (h i) -> p h i", h=nh))
        yield
        bks = []
        for hb, nh in HG:
            bk = banks.next()
            bks.append(bk)
            for h in range(hb, hb + nh):
                mm(P, bk[:, (h - hb) * 128:(h - hb + 1) * 128], kdec[:, h, :], vn[:, h, :])
        tt(P, "pool", Sst, Sst, bc(glast, [[1, NH], [0, 128]]), ALU.mult)
        for (hb, nh), bk in zip(HG, bks):
            tt(P, "dve", Sst[:, hb:hb + nh, :], Sst[:, hb:hb + nh, :],
               bk[:, 0:nh * 128].re("p (h d) -> p h d", h=nh), ALU.add)
        yield
        sq = tf[2]
        tt(P, "pool", sq, oT, oT, ALU.mult)
        rs = tf[3]
        for hb, nh in HG:
            bk = banks.next()
            mm(P, bk[:, 0:nh * c].re("p (h i) -> p h i", h=nh), K.meanm, sq[:, hb:hb + nh, :])
            rsqrt(P, rs[:, hb:hb + nh, :], bk[:, 0:nh * c].re("p (h i) -> p h i", h=nh), EPS)
        y1 = tf[4]
        stt(P, "dve", y1, oT, ghn[:, 0:1], rs, ALU.mult, ALU.mult)
        yb = T["yb"][:, :, 0:c]
        tt(P, "pool", yb, y1, za, ALU.mult)
        P.dma("sp", ya_v[:, :, col0:col0 + c], yb)
        yield


def gdn_tiles(C, NH):
    A = C.A
    T = {}
    T["kq"] = [A.alloc([NH, 2, 128], F32) for _ in range(2)]
    T["vT"] = [A.alloc([NH, 128], F32) for _ in range(2)]
    T["gb"] = [A.alloc([16], F32) for _ in range(2)]
    T["za"] = [A.alloc([NH, 128], F32) for _ in range(2)]
    T["sm"] = A.alloc([5 * NH], F32)
    T["t"] = [A.alloc([NH, 128], F32) for _ in range(9)]
    T["kqd"] = A.alloc([NH, 2, 128], F32)
    T["pa"] = A.alloc([NH, 2, 128], F32)
    T["pb"] = A.alloc([NH, 2, 128], F32)
    T["yb"] = A.alloc([NH, 128], BF16)
    T["S"] = A.alloc([NH, 128], F32)
    return T


def run_interleaved(gens, offsets=None):
    gens = list(gens)
    offsets = list(offsets) if offsets else [0] * len(gens)
    rnd = 0
    live = list(range(len(gens)))
    while live:
        nxt_live = []
        for k in live:
            if rnd < offsets[k]:
                nxt_live.append(k)
                continue
            try:
                next(gens[k])
                nxt_live.append(k)
            except StopIteration:
                pass
        live = nxt_live
        rnd += 1


def phase_b(C):
    P, A, I, O, S, K = C.P, C.A, C.I, C.O, C.S, C.K
    ghn = A.alloc([1], F32)
    P.dma("sp", ghn, I["g_head_norm"].re("o d -> d o"))
    NH = int(os.environ.get("MK_NH", "4"))
    NG = 8 // NH
    Ts = [gdn_tiles(C, NH) for _ in range(2 * NG)]
    for T in Ts:
        T["ghn"] = ghn
    MODE = os.environ.get("MK_B", "prompt,sample").split(",")
    NCH = int(os.environ.get("MK_NCH", str(L // 128)))
    if "prompt" in MODE:
        gens = []
        for s in range(NSEQ):
            for hg in range(NG):
                T = Ts[s * NG + hg]
                gens.append(gdn_stream(C, T, 128, [s * L + ch * 128 for ch in range(NCH)], T["S"], True,
                                       hg * NH, NH))
        STG = int(os.environ.get("MK_STG", "6"))
        run_interleaved(gens, [STG * k for k in range(len(gens))])
        for s in range(NSEQ):
            for hg in range(NG):
                P.dma("sp", O["p_gdn"][s, hg * NH:(hg + 1) * NH].re("h k v -> k h v"), Ts[s * NG + hg]["S"])
    if "sample" in MODE:
        def sample_gen(T, bs, hg):
            for b in bs:
                P.dma("sp", T["S"], I["state_gdn"][b, hg * NH:(hg + 1) * NH].re("h k v -> k h v"), acc=False)
                yield from gdn_stream(C, T, 1, [NTP + b], T["S"], False, hg * NH, NH)
                P.dma("sp", O["s_gdn"][b, hg * NH:(hg + 1) * NH].re("h k v -> k h v"), T["S"])
        gens = []
        for par in range(2):
            for hg in range(NG):
                gens.append(sample_gen(Ts[par * NG + hg], range(par, NS, 2), hg))
        run_interleaved(gens)


def phase_c(C):
    P, A, I, O, S, K = C.P, C.A, C.I, C.O, C.S, C.K
    banks = C.banks
    SC = float(128 ** -0.5)
    rel = A.alloc([12], F32)
    P.dma("sp", rel[0:32], I["rel_table"])
    oh = A.alloc([3, 384], F32)
    P.dma("sp", oh[0:32], I["c_oh"].re("g b c -> b g c"))
    ohs = A.alloc([3, 128], F32)
    P.dma("sp", ohs[0:32], I["c_ohs"].re("g b c -> b g c"))
    oh0 = A.alloc([3, 1], F32)
    P.dma("sp", oh0[0:32], I["c_oh0"].re("g b c -> b g c"))
    negm = A.alloc([384], F32)
    P.dma("sp", negm, I["c_negm"])
    bias2 = A.alloc([12, 256], F32)
    biasS = A.alloc([12], F32)
    bias0 = A.alloc([12], F32)
    relb = A.alloc([128], F32)
    vp = Rot([A.alloc([384], F32) for _ in range(2)])
    for g in range(3):
        for h in range(4):
            gh = g * 4 + h
            ts(P, "dve", relb[0:32], K.ones[0:32], rel[0:32, gh:gh + 1], ALU.mult)
            bk = banks.next()
            mm(P, bk[:, 0:384], relb[0:32], oh[0:32, g, :])
            v_ = vp.next()
            tt(P, "dve", v_, bk[:, 0:384], negm, ALU.add)
            P.dma("sp", S["bias"][gh, 0:128, :], v_)
            P.dma("sp", S["bias"][gh, 128:256, :], v_)
        bk = banks.next()
        mm(P, bk[:, 0:4], ohs[0:32, g, :], rel[0:32, g * 4:(g + 1) * 4])
        cp(P, "dve", biasS[:, g * 4:(g + 1) * 4], bk[:, 0:4])
        bk = banks.next()
        mm(P, bk[0:1, 0:4], oh0[0:32, g, :], rel[0:32, g * 4:(g + 1) * 4])
        cp(P, "dve", bias0[0:1, g * 4:(g + 1) * 4], bk[0:1, 0:4])
    P.barrier()
    for gh in range(12):
        base = gh * 256 * 384 + 255
        P.dma("sp", bias2[:, gh, 0:128], S["bias"].raw([[383, 128], [1, 128]], off=base + 128 * 383))
        P.dma("sp", bias2[:, gh, 128:256], S["bias"].raw([[383, 128], [1, 128]], off=base))
    MODE = os.environ.get("MK_C", "prompt,sample").split(",")
    mark = A.off
    if "prompt" in MODE:
        acc2s = Rot([A.alloc([2, L], F32) for _ in range(2)])
        qcs = Rot([A.alloc([L], BF16) for _ in range(3)])
        kcs = Rot([A.alloc([L], BF16) for _ in range(3)])
        vcs = Rot([A.alloc([L // 128, 128], BF16) for _ in range(3)])
        lgs = Rot([A.alloc([256], F32) for _ in range(3)])
        Es = Rot([A.alloc([256], BF16) for _ in range(4)])
        zbt = A.alloc([L], F32)
        ybt = A.alloc([L], BF16)
        items = [(s, h, g, r) for s in range(NSEQ) for h in range(4) for g in range(3) for r in range(DILS[g])]

        def c_loads(s, h, g, r):
            lc = L // DILS[g]
            nb = lc // 128
            qc, kc, vc = qcs.next()[:, 0:lc], kcs.next()[:, 0:lc], vcs.next()[:, 0:nb, :]
            c0 = s * L + r * lc
            P.dma("sp", qc, S["qb%d" % g][h * 128:(h + 1) * 128, c0:c0 + lc], acc=False)
            P.dma("sp", kc, S["kb%d" % g][h * 128:(h + 1) * 128, c0:c0 + lc], acc=False)
            P.dma("sp", vc, S["vb%d" % g][c0:c0 + lc, h * 128:(h + 1) * 128].re("(n p) d -> p n d", p=128),
                  acc=False)
            return qc, kc, vc
        pre = {0: c_loads(*items[0])}
        item_i = [0]
        for s in range(NSEQ):
            for h in range(4):
                acc2 = acc2s.next()
                for g, dil in enumerate(DILS):
                    gh = g * 4 + h
                    lc = L // dil
                    nb = lc // 128
                    for r in range(dil):
                        ii = item_i[0]
                        item_i[0] += 1
                        qc, kc, vc = pre.pop(ii)
                        if ii + 1 < len(items):
                            pre[ii + 1] = c_loads(*items[ii + 1])
                        Eprev = None

                        def logits(n):
                            nq = 256 if n < nb - 1 else 128
                            bk = banks.next()
                            mm(P, bk[:, 0:nq], kc[:, n * 128:(n + 1) * 128], qc[:, n * 128:n * 128 + nq])
                            lg = lgs.next()
                            stt(P, "dve", lg[:, 0:nq], bk[:, 0:nq], SC, bias2[:, gh, 0:nq], ALU.mult, ALU.add)
                            E = Es.next()
                            act(P, E[:, 0:nq], lg[:, 0:nq], AF.Exp)
                            return E
                        Enext = logits(0)
                        for n in range(nb):
                            E = Enext
                            if n + 1 < nb:
                                Enext = logits(n + 1)
                            b2 = banks.next()
                            if n > 0:
                                mm(P, b2[:, 0:128], vc[:, n - 1, :], Eprev[:, 128:256], True, False)
                            mm(P, b2[:, 0:128], vc[:, n, :], E[:, 0:128], n == 0, True)
                            if n > 0:
                                mm(P, b2[:, 128:256], K.onesb, Eprev[:, 128:256], True, False)
                            mm(P, b2[:, 128:256], K.onesb, E[:, 0:128], n == 0, True)
                            Eprev = E
                            lo = r + n * 128 * dil
                            dst = acc2[:, :, lo:lo + 127 * dil + 1:dil]
                            src = b2[:, 0:256].re("p (a q) -> p a q", a=2)
                            if g == 0:
                                cp(P, "act", dst, src)
                            else:
                                tt(P, "dve", dst, dst, src, ALU.add)
                P.dma("sp", zbt, S["zbT"][h * 128:(h + 1) * 128, s * L:(s + 1) * L], acc=False)
                for qd in range(4):
                    sl = slice(qd * 1024, (qd + 1) * 1024)
                    recip(P, acc2[:, 1, sl], acc2[:, 1, sl])
                    tt(P, "pool", acc2[:, 0, sl], acc2[:, 0, sl], acc2[:, 1, sl], ALU.mult)
                    tt(P, "pool", ybt[:, sl], acc2[:, 0, sl], zbt[:, sl], ALU.mult)
                P.dma("sp", S["ybT"][h * 128:(h + 1) * 128, s * L:(s + 1) * L], ybt)
    P.barrier()
    A.off = mark
    if "sample" in MODE:
        Kcs = [Rot([A.alloc([512], F32) for _ in range(2)]) for g in range(3)]
        Vcs = [Rot([A.alloc([512], F32) for _ in range(2)]) for g in range(3)]
        KcT = Rot([A.alloc([4, 128], F32) for _ in range(2)])
        vsb = Rot([A.alloc([3, 512], F32) for _ in range(2)])
        zbs = A.alloc([4, NS], F32)
        ybs = A.alloc([4, NS], BF16)
        sw = Rot([A.alloc([64], F32) for _ in range(2)])
        P.dma("sp", zbs, S["zbT"].re("(h p) t -> p h t", p=128)[:, :, NTP:NT])
        for b in range(NS):
            kk = [Kcs[g].next() for g in range(3)]
            vv = [Vcs[g].next() for g in range(3)]
            for g, dil in enumerate(DILS):
                P.dma("sp", kk[g], I["ck%d" % g][b, 0::dil, :], acc=False)
                P.dma("sp", vv[g], I["cv%d" % g][b, 0::dil, :], acc=False)
            vs_ = vsb.next()
            P.dma("sp", vs_[0:1], S["vs"][b:b + 1], acc=False)
            w_ = sw.next()
            bL = banks.next()
            for g in range(3):
                bk = banks.next()
                for h in range(4):
                    tr(P, bk[:, h * 128:(h + 1) * 128], kk[g][:, h * 128:(h + 1) * 128], K.ident)
                kt = KcT.next()
                cp(P, "act", kt, bk.re("p (h k) -> p h k", h=4))
                for h in range(4):
                    gh = g * 4 + h
                    mm(P, bL[:, gh:gh + 1], kt[:, h, :], K.qs[:, gh, b:b + 1])
            for gh in range(12):
                mm(P, bL[0:1, 16 + gh:17 + gh], K.ks[:, gh, b:b + 1], K.qs[:, gh, b:b + 1])
            lgS, ES, lg0, E0 = w_[:, 0:12], w_[:, 12:24], w_[0:1, 24:36], w_[0:1, 36:48]
            stt(P, "dve", lgS, bL[:, 0:12], SC, biasS, ALU.mult, ALU.add)
            act(P, ES, lgS, AF.Exp)
            stt(P, "dve", lg0, bL[0:1, 16:28], SC, bias0[0:1], ALU.mult, ALU.add)
            act(P, E0, lg0, AF.Exp)
            bO = banks.next()
            for h in range(4):
                for g in range(3):
                    gh = g * 4 + h
                    mm(P, bO[:, h:h + 1], vv[g][:, h * 128:(h + 1) * 128], ES[:, gh:gh + 1], g == 0, False)
                    mm(P, bO[:, h:h + 1], vs_[0:1, g, h * 128:(h + 1) * 128], E0[0:1, gh:gh + 1], False, g == 2)
                for g in range(3):
                    gh = g * 4 + h
                    mm(P, bO[:, 4 + h:5 + h], K.ones, ES[:, gh:gh + 1], g == 0, False)
                    mm(P, bO[:, 4 + h:5 + h], K.ones[0:1, :], E0[0:1, gh:gh + 1], False, g == 2)
            ob = w_[:, 48:56]
            cp(P, "dve", ob, bO[:, 0:8])
            recip(P, ob[:, 4:8], ob[:, 4:8])
            tt(P, "pool", ob[:, 0:4], ob[:, 0:4], ob[:, 4:8], ALU.mult)
            tt(P, "pool", ybs[:, :, b], ob[:, 0:4], zbs[:, :, b], ALU.mult)
        P.dma("sp", S["ybT"].re("(h p) t -> p h t", p=128)[:, :, NTP:NT], ybs)


def phase_d(C):
    P, A, I, O, S, K = C.P, C.A, C.I, C.O, C.S, C.K
    banks = C.banks
    wpa = A.alloc([8, DM], BF16)
    wpb = A.alloc([4, DM], BF16)
    wo = A.alloc([8, DM], BF16)
    gpost = A.alloc([DM], F32)
    P.dma("pool", wpa, I["w_proj_a"].re("(k p) n -> p k n", p=128))
    P.dma("pool", wpb, I["w_proj_b"].re("(k p) n -> p k n", p=128))
    P.dma("pool", wo, I["w_out"].re("(k p) n -> p k n", p=128))
    P.dma("sp", gpost, bcast_rows(I["g_post"], DM))
    yas = Rot([A.alloc([8, 512], BF16) for _ in range(2)])
    ybs_ = Rot([A.alloc([4, 512], BF16) for _ in range(2)])
    gas = Rot([A.alloc([8, 512], F32) for _ in range(int(os.environ.get('MK_GB', '2')))])
    gbs_ = Rot([A.alloc([8, 512], F32) for _ in range(int(os.environ.get('MK_GB', '2')))])
    xts = Rot([A.alloc([4, DM], F32) for _ in range(2)])
    hTs = Rot([A.alloc([8, 512], BF16) for _ in range(2)])
    t1s = Rot([A.alloc([512], F32) for _ in range(2)])
    t2s = Rot([A.alloc([512], F32) for _ in range(2)])
    junk = A.alloc([DM], F32)
    ysbs = Rot([A.alloc([DM], F32) for _ in range(2)])
    sts = Rot([A.alloc([4], F32) for _ in range(4)])
    fmv = lambda nm: S[nm].re("(k p) t -> p k t", p=128)
    TILES = [(ti * 512, 512) for ti in range(NTP // 512)] + [(NTP, NS)]
    def d_loads(t0, n):
        ya, yb, ga, gb_, xt = yas.next(), ybs_.next(), gas.next(), gbs_.next(), xts.next()
        P.dma("sp", ya[:, :, 0:n], fmv("yaT")[:, :, t0:t0 + n], acc=False)
        P.dma("sp", yb[:, :, 0:n], fmv("ybT")[:, :, t0:t0 + n], acc=False)
        P.dma("sp", ga[:, :, 0:n], fmv("gaT")[:, :, t0:t0 + n], acc=False)
        P.dma("sp", gb_[:, :, 0:n], fmv("gbT")[:, :, t0:t0 + n], acc=False)
        if n == 512:
            P.dma("sp", xt, I["x_p"][t0:t0 + 512, :].re("(s p) d -> p s d", p=128), acc=False)
        else:
            P.dma("sp", xt[0:NS, 0, :], I["x_s"], acc=False)
        return ya, yb, ga, gb_, xt
    nxt_ld = d_loads(*TILES[0])
    for ti, (t0, n) in enumerate(TILES):
        nsub = 4 if n == 512 else 1
        np_ = 128 if n == 512 else NS
        ya, yb, ga, gb_, xt = nxt_ld
        hT = hTs.next()
        if ti + 1 < len(TILES):
            nxt_ld = d_loads(*TILES[ti + 1])
        for e in range(8):
            pA, pB = banks.next(), banks.next()
            for k in range(8):
                mm(P, pA[:, 0:n], wpa[:, k, e * 128:(e + 1) * 128], ya[:, k, 0:n], k == 0, k == 7)
            for k in range(4):
                mm(P, pB[:, 0:n], wpb[:, k, e * 128:(e + 1) * 128], yb[:, k, 0:n], k == 0, k == 3)
            t1, t2 = t1s.next()[:, 0:n], t2s.next()[:, 0:n]
            tt(P, "dve", t1, ga[:, e, 0:n], pA[:, 0:n], ALU.mult)
            tt(P, "dve", t2, gb_[:, e, 0:n], pB[:, 0:n], ALU.mult)
            tt(P, "pool", hT[:, e, 0:n], t1, t2, ALU.add)
        for sb in range(nsub):
            bks = [banks.next(), banks.next()]
            for half in range(2):
                for k in range(8):
                    mm(P, bks[half][0:np_, :], hT[:, k, sb * 128:sb * 128 + np_], wo[:, k, half * 512:(half + 1) * 512],
                       k == 0, k == 7)
            for half in range(2):
                act(P, junk[0:np_, half * 512:(half + 1) * 512], bks[half][0:np_, :], AF.Square)
            st_ = sts.next()[0:np_]
            rsum(P, "dve", st_[:, 0:1], junk[0:np_])
            rsqrt(P, st_[:, 1:2], st_[:, 0:1], EPS, 1.0 / DM)
            ysb = ysbs.next()[0:np_]
            for half in range(2):
                hs = slice(half * 512, (half + 1) * 512)
                stt(P, "dve", ysb[:, hs], bks[half][0:np_, :], st_[:, 1:2], gpost[0:np_, hs], ALU.mult, ALU.mult)
            tt(P, "pool", ysb, ysb, xt[0:np_, sb, :], ALU.add)
            if n == 512:
                P.dma("sp", O["y_p"][t0 + sb * 128:t0 + (sb + 1) * 128, :], ysb)
            else:
                P.dma("sp", O["y_s"], ysb)


def rel_bucket_np(dist):
    import math
    max_exact = 16
    d = np.maximum(dist, 1).astype(np.float32)
    large = max_exact + (np.log(d / max_exact) / math.log(2048 / max_exact) * (32 - max_exact)).astype(np.int32)
    large = np.minimum(large, 31)
    return np.where(dist < max_exact, dist, large)


def host_consts():
    c = {}
    c["c_ident"] = np.eye(128, dtype=np.float32)
    k = np.arange(128)
    c["c_umat"] = (k[:, None] <= k[None, :]).astype(np.float32)
    c["c_maskT"] = np.where(k[None, :] >= k[:, None], 0.0, NEG).astype(np.float32)
    c["c_strictT"] = (k[None, :] > k[:, None]).astype(np.float32)
    oh = np.zeros((3, 32, 384), np.float32)
    ohs = np.zeros((3, 32, 128), np.float32)
    oh0 = np.zeros((3, 32, 1), np.float32)
    for g, dil in enumerate(DILS):
        bk = rel_bucket_np(np.arange(129, dtype=np.int32) * dil)
        for j in range(129):
            oh[g, bk[j], 127 + j] = 1.0
        for i in range(128):
            ohs[g, bk[128 - i], i] = 1.0
        oh0[g, bk[0], 0] = 1.0
    c["c_oh"] = oh
    negm = np.full((128, 384), NEG, np.float32)
    negm[:, 127:256] = 0.0
    c["c_negm"] = negm
    c["c_ohs"] = ohs
    c["c_oh0"] = oh0
    return c


_NC_CACHE = {}


def kernel(x_prompt, x_sample, state_gdn, state_conv, cache_k_w128, cache_v_w128,
           cache_k_w512, cache_v_w512, cache_k_w2048, cache_v_w2048, rel_table,
           g_pre, w_in, conv_w, a_log, dt_bias, g_head_norm, w_proj_a, w_proj_b, w_out, g_post):
    phases = os.environ.get("MK_PHASES", "ABCD")
    ncores = int(os.environ.get("MK_CORES", str(NCORES)))
    if phases not in _NC_CACHE:
        _NC_CACHE[phases] = build(phases)
    nc = _NC_CACHE[phases]
    f = lambda a: np.ascontiguousarray(np.asarray(a, dtype=np.float32))
    consts = host_consts()
    caches = ((cache_k_w128, cache_v_w128), (cache_k_w512, cache_v_w512), (cache_k_w2048, cache_v_w2048))
    in_maps = []
    for c in range(ncores):
        m = {}
        m["x_p"] = f(x_prompt[NSEQ * c:NSEQ * (c + 1)]).reshape(NTP, DM)
        m["x_s"] = f(x_sample[NS * c:NS * (c + 1)]).reshape(NS, DM)
        m["state_gdn"] = f(state_gdn[0, NS * c:NS * (c + 1)])
        m["state_conv"] = f(state_conv[0, NS * c:NS * (c + 1)])
        for g, (ck, cv) in enumerate(caches):
            m["ck%d" % g] = f(ck[0, NS * c:NS * (c + 1)]).reshape(NS, -1, 512)
            m["cv%d" % g] = f(cv[0, NS * c:NS * (c + 1)]).reshape(NS, -1, 512)
        m["rel_table"] = f(rel_table)
        m["g_pre"] = f(g_pre)
        m["w_in"] = f(w_in[0])
        m["conv_w"] = f(conv_w[0])
        m["a_log"] = f(a_log)
        m["dt_bias"] = f(dt_bias)
        m["g_head_norm"] = f(g_head_norm)
        m["w_proj_a"] = f(w_proj_a[0])
        m["w_proj_b"] = f(w_proj_b[0])
        m["w_out"] = f(w_out[0])
        m["g_post"] = f(g_post)
        m.update(consts)
        in_maps.append(m)
    if os.environ.get("MK_TRACE"):
        res = run_bass_kernel_spmd(nc, in_maps, core_ids=list(range(ncores)), trace=True)
        print("EXEC_TIME_NS", phases, res.exec_time_ns)
    else:
        res = run_bass_kernel_spmd(nc, in_maps, core_ids=list(range(ncores)))
    R = res.results
    cat = lambda k: np.concatenate([np.asarray(r[k]) for r in R], axis=0)
    B = NSEQ * ncores
    SB = NS * ncores
    outs = [cat("y_p").reshape(B, L, DM), cat("y_s").reshape(SB, 1, DM),
            cat("p_gdn").reshape(1, B, 8, 128, 128), cat("p_conv").reshape(1, B, 3, 3072)]
    for g, w in enumerate((128, 512, 2048)):
        outs.append(cat("p_k%d" % g).reshape(1, B, w, 4, 128))
        outs.append(cat("p_v%d" % g).reshape(1, B, w, 4, 128))
    outs.append(cat("s_gdn").reshape(1, SB, 8, 128, 128))
    outs.append(cat("s_conv").reshape(1, SB, 3, 3072))
    for g in range(3):
        outs.append(cat("s_k%d" % g).reshape(1, SB, 1, 4, 128))
        outs.append(cat("s_v%d" % g).reshape(1, SB, 1, 4, 128))
    return tuple(o.astype(np.float32) for o in outs)
```

```python
import os
from contextlib import ExitStack
import numpy as np
import concourse.bass as bass
import concourse.mybir as mybir
from concourse.bass_utils import run_bass_kernel_spmd

F32 = mybir.dt.float32
BF16 = mybir.dt.bfloat16
U8 = mybir.dt.uint8
AF = mybir.ActivationFunctionType
ALU = mybir.AluOpType
AX = mybir.AxisListType

NCORES = 8
L = 4096
NSEQ = 2
NS = 16
NTP = NSEQ * L
NT = NTP + NS
DM = 1024
INW = 11280
EPS = 1e-6
NEG = -1e30
DILS = (1, 4, 16)
O_ZA, O_A, O_QKVB, O_ZB, O_GA, O_GB = 3072, 4096, 4112, 8720, 9232, 10256


class Tl:
    __slots__ = ("w", "rd", "excl")

    def __init__(self, excl=False):
        self.w = {}
        self.rd = {}
        self.excl = excl


class V:
    __slots__ = ("t", "ap")

    def __init__(self, t, ap):
        self.t = t
        self.ap = ap

    def __getitem__(self, k):
        return V(self.t, self.ap[k])

    def re(self, s, **kw):
        return V(self.t, self.ap.rearrange(s, **kw))

    def raw(self, dims, off=0):
        return V(self.t, bass.AP(tensor=self.ap.tensor, offset=self.ap.offset + off, ap=dims))

    def bitcast(self, dt):
        return V(self.t, self.ap.bitcast(dt))


NDS = 12


class Prog:
    ENG = ("sp", "pe", "act", "dve", "pool")

    def __init__(self, nc):
        self.nc = nc
        self.streams = {e: [] for e in self.ENG}
        self.count = {e: 0 for e in self.ENG}
        self.known = {e: {} for e in self.ENG}
        self.dma_n = {"sp": 0, "pool": 0, "act": 0}
        self.latest = {}

    def _resolve(self, eng, deps):
        kn = self.known[eng]
        waits = []
        for k, v in deps.items():
            if k == "pe" and eng == "pe":
                continue
            if kn.get(k, 0) >= v:
                continue
            kn[k] = v
            waits.append((k, v))
        return waits

    @staticmethod
    def _deps(reads, writes, acc):
        deps = {}

        def add(d):
            for k, v in d.items():
                if deps.get(k, 0) < v:
                    deps[k] = v
        for t in reads:
            if t is not None:
                add(t.w)
                if t.excl:
                    add(t.rd)
        for t in writes:
            if t is not None:
                if not acc:
                    add(t.w)
                add(t.rd)
        return deps

    def _commit(self, tok, reads, writes, acc):
        k, v = tok
        self.latest[k] = max(self.latest.get(k, 0), v)
        for t in writes:
            if t is None:
                continue
            if acc:
                t.w[k] = max(t.w.get(k, 0), v)
            else:
                t.w = {k: v}
                t.rd = {}
        for t in reads:
            if t is None or t in writes:
                continue
            t.rd[k] = max(t.rd.get(k, 0), v)

    def op(self, eng, fn, reads=(), writes=(), acc=False):
        reads = [r.t if isinstance(r, V) else r for r in reads]
        writes = [r.t if isinstance(r, V) else r for r in writes]
        deps = self._deps(reads, writes, acc)
        waits = self._resolve(eng, deps)
        self.count[eng] += 1
        tok = (eng, self.count[eng])
        self.streams[eng].append((waits, fn, (eng, 1)))
        self._commit(tok, reads, writes, acc)

    def dma(self, q, out, in_, acc=True):
        if os.environ.get("MK_NOST") and out.t is None and out.ap.tensor.name.startswith("s_"):
            return
        n = self.dma_n[q]
        self.dma_n[q] += 1
        i = n % NDS
        val = 16 * (n // NDS + 1)
        key = ("d", q, i)
        reads = [in_.t]
        writes = [out.t]
        deps = self._deps(reads, writes, acc)
        if n >= NDS:
            deps[key] = max(deps.get(key, 0), val - 16)
        waits = self._resolve(q, deps)
        oa, ia = out.ap, in_.ap
        self.streams[q].append((waits, lambda e: e.dma_start(out=oa, in_=ia), (key, 16)))
        self._commit((key, val), reads, writes, acc)

    def barrier(self):
        for e in self.ENG:
            waits = self._resolve(e, dict(self.latest))
            if waits:
                self.streams[e].append((waits, None, None))

    def emit(self):
        nc = self.nc
        keys = list(self.ENG[1:]) + [("d", q, i) for q in ("sp", "pool", "act") for i in range(NDS)]
        with ExitStack() as st:
            st.enter_context(nc.allow_non_contiguous_dma(reason="small strided sample-path transfers"))
            sems = {}
            for k in keys:
                nm = k if isinstance(k, str) else "d%s%d" % (k[1], k[2])
                sems[k] = st.enter_context(nc.semaphore("s_" + nm))
            block = st.enter_context(nc.Block())
            decos = {"sp": block.sync, "pe": block.tensor, "act": block.scalar,
                     "dve": block.vector, "pool": block.gpsimd}
            for eng in self.ENG:
                stream = self.streams[eng]

                def body(e, stream=stream):
                    for waits, fn, inc in stream:
                        for k, v in waits:
                            e.wait_ge(sems[k], v)
                        if fn is not None:
                            fn(e).then_inc(sems[inc[0]], inc[1])
                decos[eng](body)


class Arena:
    def __init__(self, nc, nbytes):
        self.t = nc.alloc_sbuf_tensor("arena", [128, nbytes], U8)
        self.ap = self.t.ap()
        self.n = nbytes
        self.off = 0

    def alloc(self, free_shape, dt, parts=128):
        esz = 4 if dt == F32 else 2
        ne = int(np.prod(free_shape))
        nb = (ne * esz + 31) // 32 * 32
        assert self.off + nb <= self.n, "SBUF arena overflow %d + %d > %d" % (self.off, nb, self.n)
        a = self.ap[0:parts, self.off:self.off + ne * esz].bitcast(dt)
        self.off += nb
        if len(free_shape) == 2:
            a = a.rearrange("p (a b) -> p a b", a=free_shape[0])
        elif len(free_shape) == 3:
            a = a.rearrange("p (a b c) -> p a b c", a=free_shape[0], b=free_shape[1])
        return V(Tl(), a)


class Ctx:
    pass


def dram_in(nc, name, shape, dt=F32):
    return V(None, nc.dram_tensor(name, list(shape), dt, kind="ExternalInput").ap())


def dram_out(nc, name, shape, dt=F32):
    return V(None, nc.dram_tensor(name, list(shape), dt, kind="ExternalOutput").ap())


def dram_tmp(nc, name, shape, dt=F32):
    return V(None, nc.dram_tensor(name, list(shape), dt, kind="Internal").ap())


def mm(P, out, lhsT, rhs, start=True, stop=True):
    o, l, r = out.ap, lhsT.ap, rhs.ap
    P.op("pe", lambda e: e.matmul(o, l, r, start=start, stop=stop), [lhsT, rhs], [out])


def tr(P, out, in_, ident):
    o, i, d = out.ap, in_.ap, ident.ap
    P.op("pe", lambda e: e.transpose(o, i, d), [in_, ident], [out])


def act(P, out, in_, func, bias=0.0, scale=1.0, eng="act"):
    o, i = out.ap, in_.ap
    rd = [in_]
    b = bias
    if isinstance(bias, V):
        rd.append(bias)
        b = bias.ap
    s = scale
    if isinstance(scale, V):
        rd.append(scale)
        s = scale.ap
    P.op("act", lambda e: e.activation(o, i, func, bias=b, scale=s), rd, [out])


def cp(P, eng, out, in_):
    o, i = out.ap, in_.ap
    if eng == "act":
        P.op("act", lambda e: e.copy(o, i), [in_], [out])
    else:
        P.op(eng, lambda e: e.tensor_copy(o, i), [in_], [out])


def tt(P, eng, out, in0, in1, op):
    o, a, b = out.ap, in0.ap, in1.ap
    P.op(eng, lambda e: e.tensor_tensor(o, a, b, op), [in0, in1], [out])


def ts(P, eng, out, in0, s1, op0, s2=None, op1=None):
    o, a = out.ap, in0.ap
    rd = [in0]
    x1 = s1
    if isinstance(s1, V):
        rd.append(s1)
        x1 = s1.ap
    x2 = s2
    if isinstance(s2, V):
        rd.append(s2)
        x2 = s2.ap
    if op1 is None:
        P.op(eng, lambda e: e.tensor_scalar(o, a, x1, None, op0), rd, [out])
    else:
        P.op(eng, lambda e: e.tensor_scalar(o, a, x1, x2, op0, op1), rd, [out])


def stt(P, eng, out, in0, scalar, in1, op0, op1):
    o, a, b = out.ap, in0.ap, in1.ap
    rd = [in0, in1]
    s = scalar
    if isinstance(scalar, V):
        rd.append(scalar)
        s = scalar.ap
    P.op(eng, lambda e: e.scalar_tensor_tensor(o, a, s, b, op0, op1), rd, [out])


def memset(P, eng, out, val):
    o = out.ap
    P.op(eng, lambda e: e.memset(o, val), [], [out])


def rsum(P, eng, out, in_):
    o, i = out.ap, in_.ap
    P.op(eng, lambda e: e.reduce_sum(o, i, AX.X), [in_], [out])


def recip(P, out, in_):
    o, i = out.ap, in_.ap
    P.op("dve", lambda e: e.reciprocal(o, i), [in_], [out])


def rsqrt(P, out, in_, eps, scale=1.0):
    act(P, out, in_, AF.Ln, bias=eps, scale=scale)
    act(P, out, out, AF.Exp, scale=-0.5)


class Rot:
    def __init__(self, items):
        self.items = items
        self.i = 0

    def next(self):
        v = self.items[self.i % len(self.items)]
        self.i += 1
        return v


def build(phases="ABCD"):
    nc = bass.Bass("TRN2", target_bir_lowering=False)
    P = Prog(nc)
    C = Ctx()
    C.nc, C.P = nc, P
    I = {}
    I["x_p"] = dram_in(nc, "x_p", [NTP, DM])
    I["x_s"] = dram_in(nc, "x_s", [NS, DM])
    I["state_gdn"] = dram_in(nc, "state_gdn", [NS, 8, 128, 128])
    I["state_conv"] = dram_in(nc, "state_conv", [NS, 3, 3072])
    for g, w in enumerate((128, 512, 2048)):
        I["ck%d" % g] = dram_in(nc, "ck%d" % g, [NS, w, 512])
        I["cv%d" % g] = dram_in(nc, "cv%d" % g, [NS, w, 512])
    I["rel_table"] = dram_in(nc, "rel_table", [32, 12])
    I["g_pre"] = dram_in(nc, "g_pre", [1, DM])
    I["w_in"] = dram_in(nc, "w_in", [DM, INW])
    I["conv_w"] = dram_in(nc, "conv_w", [3072, 4])
    I["a_log"] = dram_in(nc, "a_log", [1, 8])
    I["dt_bias"] = dram_in(nc, "dt_bias", [1, 8])
    I["g_head_norm"] = dram_in(nc, "g_head_norm", [1, 128])
    I["w_proj_a"] = dram_in(nc, "w_proj_a", [1024, DM])
    I["w_proj_b"] = dram_in(nc, "w_proj_b", [512, DM])
    I["w_out"] = dram_in(nc, "w_out", [DM, DM])
    I["g_post"] = dram_in(nc, "g_post", [1, DM])
    I["c_ident"] = dram_in(nc, "c_ident", [128, 128])
    I["c_umat"] = dram_in(nc, "c_umat", [128, 128])
    I["c_maskT"] = dram_in(nc, "c_maskT", [128, 128])
    I["c_strictT"] = dram_in(nc, "c_strictT", [128, 128])
    I["c_oh"] = dram_in(nc, "c_oh", [3, 32, 384])
    I["c_negm"] = dram_in(nc, "c_negm", [128, 384])
    I["c_ohs"] = dram_in(nc, "c_ohs", [3, 32, 128])
    I["c_oh0"] = dram_in(nc, "c_oh0", [3, 32, 1])
    O = {}
    O["y_p"] = dram_out(nc, "y_p", [NTP, DM])
    O["y_s"] = dram_out(nc, "y_s", [NS, DM])
    O["p_gdn"] = dram_out(nc, "p_gdn", [NSEQ, 8, 128, 128])
    O["p_conv"] = dram_out(nc, "p_conv", [NSEQ, 3, 3072])
    for g, w in enumerate((128, 512, 2048)):
        O["p_k%d" % g] = dram_out(nc, "p_k%d" % g, [NSEQ, w, 512])
        O["p_v%d" % g] = dram_out(nc, "p_v%d" % g, [NSEQ, w, 512])
        O["s_k%d" % g] = dram_out(nc, "s_k%d" % g, [NS, 512])
        O["s_v%d" % g] = dram_out(nc, "s_v%d" % g, [NS, 512])
    O["s_gdn"] = dram_out(nc, "s_gdn", [NS, 8, 128, 128])
    O["s_conv"] = dram_out(nc, "s_conv", [NS, 3, 3072])
    S = {}
    S["qT"] = dram_tmp(nc, "s_qT", [1024, NT])
    S["kT"] = dram_tmp(nc, "s_kT", [1024, NT])
    S["vT"] = dram_tmp(nc, "s_vT", [1024, NT])
    S["zaT"] = dram_tmp(nc, "s_zaT", [1024, NT])
    S["zbT"] = dram_tmp(nc, "s_zbT", [512, NT])
    S["gaT"] = dram_tmp(nc, "s_gaT", [1024, NT])
    S["gbT"] = dram_tmp(nc, "s_gbT", [1024, NT])
    S["gbeta"] = dram_tmp(nc, "s_gbeta", [NT, 16])
    for g in range(3):
        S["qb%d" % g] = dram_tmp(nc, "s_qb%d" % g, [512, NTP], BF16)
        S["kb%d" % g] = dram_tmp(nc, "s_kb%d" % g, [512, NTP], BF16)
        S["vb%d" % g] = dram_tmp(nc, "s_vb%d" % g, [NTP, 512], BF16)
    S["vs"] = dram_tmp(nc, "s_vs", [NS, 3, 512])
    S["yaT"] = dram_tmp(nc, "s_yaT", [1024, NT], BF16)
    S["ybT"] = dram_tmp(nc, "s_ybT", [512, NT], BF16)
    S["bias"] = dram_tmp(nc, "s_bias", [12, 256, 384])
    C.I, C.O, C.S = I, O, S

    A = Arena(nc, 212480)
    C.A = A
    pst = nc.alloc_psum_tensor("psum", [128, 8, 512], F32)
    psa = pst.ap()
    C.banks = Rot([V(Tl(excl=True), psa[:, b, :]) for b in range(8)])

    K = Ctx()
    C.K = K
    K.ident = A.alloc([128], F32)
    K.identb = A.alloc([128], BF16)
    K.umat = A.alloc([128], F32)
    K.maskT = A.alloc([128], F32)
    K.strictT = A.alloc([128], F32)
    K.ones = A.alloc([128], F32)
    K.onesb = A.alloc([128], BF16)
    K.meanm = A.alloc([128], F32)
    K.c128 = A.alloc([128], F32)
    K.qs = A.alloc([12, NS], F32)
    K.ks = A.alloc([12, NS], F32)
    P.dma("sp", K.ident, I["c_ident"])
    P.dma("pool", K.identb, I["c_ident"])
    P.dma("sp", K.umat, I["c_umat"])
    P.dma("sp", K.maskT, I["c_maskT"])
    P.dma("sp", K.strictT, I["c_strictT"])
    memset(P, "pool", K.ones, 1.0)
    memset(P, "pool", K.onesb, 1.0)
    memset(P, "pool", K.meanm, 1.0 / 128.0)
    memset(P, "pool", K.c128, 128.0)
    C.mark0 = A.off

    if "A" in phases:
        phase_a(C)
    P.barrier()
    A.off = C.mark0
    if "B" in phases:
        phase_b(C)
    P.barrier()
    A.off = C.mark0
    if "C" in phases:
        phase_c(C)
    P.barrier()
    A.off = C.mark0
    if "D" in phases:
        phase_d(C)
    P.barrier()
    P.emit()
    return nc


def bcast_rows(v, n):
    return v.raw([[0, 128], [1, n]])


def phase_a(C):
    P, A, I, O, S, K = C.P, C.A, C.I, C.O, C.S, C.K
    banks = C.banks
    xnT = A.alloc([8, NT], BF16)
    gpre = A.alloc([DM], F32)
    convw = A.alloc([24, 4], F32)
    P.dma("sp", gpre, bcast_rows(I["g_pre"], DM))
    P.dma("sp", convw, I["conv_w"].re("(c p) w -> p c w", p=128))

    stT = A.alloc([24, 3, NS], F32)
    mark1 = A.off
    xts = Rot([A.alloc([DM], F32) for _ in range(4)])
    xpre = {}
    sqs = Rot([A.alloc([DM], F32) for _ in range(2)])
    xns = Rot([A.alloc([DM], BF16) for _ in range(2)])
    sts = Rot([A.alloc([4], F32) for _ in range(4)])
    DBG = int(os.environ.get("MK_DBG", "9"))
    for sub in (range(NTP // 128 + 1) if DBG >= 2 else []):
        if sub < NTP // 128:
            np_, src, t0 = 128, I["x_p"][sub * 128:(sub + 1) * 128, :], sub * 128
        else:
            np_, src, t0 = NS, I["x_s"], NTP
        if sub == 0:
            for pf in range(2):
                xpre[pf] = xts.next()
                P.dma("sp", xpre[pf], I["x_p"][pf * 128:(pf + 1) * 128, :])
        xt = xpre.pop(sub)[0:np_]
        nsb = sub + 2
        if nsb <= NTP // 128:
            xpre[nsb] = xts.next()
            if nsb < NTP // 128:
                P.dma("sp", xpre[nsb], I["x_p"][nsb * 128:(nsb + 1) * 128, :])
            else:
                P.dma("sp", xpre[nsb][0:NS], I["x_s"])
        sq = sqs.next()[0:np_]
        xn = xns.next()[0:np_]
        stt_ = sts.next()[0:np_]
        act(P, sq, xt, AF.Square)
        rsum(P, "dve", stt_[:, 0:1], sq)
        rsqrt(P, stt_[:, 2:3], stt_[:, 0:1], EPS, 1.0 / DM)
        stt(P, "dve", xn, xt, stt_[:, 2:3], gpre[0:np_], ALU.mult, ALU.mult)
        bk = banks.next()
        bkb = bk.bitcast(BF16)
        for kc in range(8):
            tr(P, bkb[:, kc * 128:kc * 128 + np_], xn[:, kc * 128:(kc + 1) * 128], K.identb[0:np_, 0:np_])
        src_ps = bkb.re("p (k t) -> p k t", k=8)[:, :, 0:np_]
        cp(P, "act" if sub % 2 else "dve", xnT[:, :, t0:t0 + np_], src_ps)
    stin = A.alloc([3072], F32)
    P.dma("sp", stin[0:48], I["state_conv"].re("b r c -> (b r) c"))
    for c4 in range(6 if DBG >= 3 else 0):
        bk = banks.next()
        for m in range(4):
            c = c4 * 4 + m
            tr(P, bk[:, m * 48:(m + 1) * 48], stin[0:48, c * 128:(c + 1) * 128], K.ident[0:48, 0:48])
        cp(P, "dve", stT[:, c4 * 4:(c4 + 1) * 4, :, :].re("p c r b -> p c b r"),
           bk[:, 0:192].re("p (c b r) -> p c b r", c=4, b=NS))
    P.barrier()
    A.off = mark1

    wbs = Rot([A.alloc([8, 512], BF16) for _ in range(2)])
    w_view = I["w_in"].re("(k p) n -> p k n", p=128)

    WSEQ = [(j * 512, 512) for j in range(6)]
    WSEQ += [(O_ZA, 512), (O_ZA + 512, 512), (O_ZB, 512), (O_GA, 512), (O_GA + 512, 512), (O_GB, 512), (O_GB + 512, 512)]
    WSEQ += [(O_A, 16)]
    for g_ in range(3):
        WSEQ += [(O_QKVB + g_ * 1536, 512), (O_QKVB + g_ * 1536 + 512, 512), (O_QKVB + g_ * 1536 + 1024, 512)]
    wq = {"i": 0, "pend": None}

    def issue_w(i):
        col0, width = WSEQ[i]
        wb = wbs.next()
        P.dma("pool", wb[:, :, 0:width], w_view[:, :, col0:col0 + width], acc=False)
        return wb

    def load_w(col0, width):
        i = wq["i"]
        assert WSEQ[i] == (col0, width), (WSEQ[i], col0, width)
        wb = wq["pend"] if wq["pend"] is not None else issue_w(i)
        wq["pend"] = issue_w(i + 1) if i + 1 < len(WSEQ) else None
        wq["i"] = i + 1
        return wb

    def fm(bank, wb, m, t0, n):
        for kc in range(8):
            mm(P, bank[:, 0:n], wb[:, kc, m * 128:(m + 1) * 128], xnT[:, kc, t0:t0 + n], kc == 0, kc == 7)

    def tm(bank, wb, tok, np_, width=512):
        for kc in range(8):
            mm(P, bank[0:np_, 0:width], tok(kc), wb[:, kc, 0:width], kc == 0, kc == 7)

    TILES = [(ti * 512, 512) for ti in range(NTP // 512)] + [(NTP, NS)]
    evi = [0]

    def evac_eng():
        evi[0] += 1
        return "act" if evi[0] % 2 else "dve"

    osb = Rot([A.alloc([512], F32) for _ in range(3)])
    tmb = Rot([A.alloc([512], F32) for _ in range(2)])
    mark2 = A.off
    stage = Rot([A.alloc([515], F32) for _ in range(3)])
    cvs = Rot([A.alloc([512], F32) for _ in range(2)])
    svs = Rot([A.alloc([512], F32) for _ in range(5)])
    sqq = Rot([A.alloc([512], F32) for _ in range(4)])
    rss = Rot([A.alloc([512], F32) for _ in range(2)])
    pending = []
    DEFER = int(os.environ.get('MK_DEFER', '1'))
    if DBG >= 4:
        P.dma("sp", O["s_conv"][:, 0:2, :], I["state_conv"][:, 1:3, :])

    SUB = os.environ.get("MK_SUB", "conv,simple,ab,att").split(",")
    for j in range(6 if "conv" in SUB else 0):
        wb = load_w(j * 512, 512)
        for m in range(4):
            c = j * 4 + m
            kind = "q" if c < 8 else ("k" if c < 16 else "v")
            dst = S["qT"] if c < 8 else (S["kT"] if c < 16 else S["vT"])
            r0 = (c % 8) * 128
            prevbox = [None]

            def s1a(t0, n):
                bk = banks.next()
                fm(bk, wb, m, t0, n)
                sg = stage.next()
                if n == 512:
                    cp(P, "act", sg[:, 3:515], bk[:, 0:512])
                    if t0 % L == 0:
                        memset(P, "pool", sg[:, 0:3], 0.0)
                    else:
                        cp(P, "pool", sg[:, 0:3], prevbox[0][:, 512:515])
                    prevbox[0] = sg
                    taps = [sg[:, w:w + 512] for w in range(4)]
                else:
                    cp(P, "act", sg[:, 0:n], bk[:, 0:n])
                    taps = [stT[:, c, 0, :], stT[:, c, 1, :], stT[:, c, 2, :], sg[:, 0:n]]
                return taps, t0, n

            def s1b(taps, t0, n):
                cv = cvs.next()[:, 0:n]
                ts(P, "dve", cv, taps[0], convw[:, c, 0:1], ALU.mult)
                for w in range(1, 4):
                    stt(P, "dve", cv, taps[w], convw[:, c, w:w + 1], cv, ALU.mult, ALU.add)
                sv = svs.next()[:, 0:n]
                act(P, sv, cv, AF.Silu)
                if kind == "v":
                    P.dma("sp", dst[r0:r0 + 128, t0:t0 + n], sv)
                else:
                    sq = sqq.next()[:, 0:n]
                    tt(P, "pool", sq, sv, sv, ALU.mult)

                    def stage2(sq=sq, sv=sv, n=n, kind=kind, dst=dst, r0=r0, t0=t0):
                        b2 = banks.next()
                        mm(P, b2[:, 0:n], K.c128 if kind == "q" else K.ones, sq)
                        rs = rss.next()[:, 0:n]
                        yield
                        act(P, rs, b2[:, 0:n], AF.Ln, bias=EPS * (128.0 if kind == "q" else 1.0))
                        yield
                        act(P, rs, rs, AF.Exp, scale=-0.5)
                        yield
                        ob = osb.next()[:, 0:n]
                        tt(P, "pool", ob, sv, rs, ALU.mult)
                        P.dma("sp", dst[r0:r0 + 128, t0:t0 + n], ob)
                    pending.append(stage2())
                    if len(pending) >= DEFER + 2:
                        run_interleaved(pending[0:2])
                        del pending[0:2]
            nxt_info = s1a(*TILES[0])
            for ti in range(len(TILES)):
                info = nxt_info
                if ti + 1 < len(TILES):
                    nxt_info = s1a(*TILES[ti + 1])
                s1b(*info)
        run_interleaved(pending)
        del pending[:]
        for s in range(NSEQ):
            bk = banks.next()
            tm(bk, wb, lambda kc, s=s: xnT[:, kc, s * L + L - 128:s * L + L], 128)
            tb = tmb.next()
            cp(P, evac_eng(), tb, bk)
            P.dma("sp", O["p_conv"][s, :, j * 512:(j + 1) * 512], tb[125:128, :])
        bk = banks.next()
        tm(bk, wb, lambda kc: xnT[:, kc, NTP:NT], NS)
        tb = tmb.next()
        cp(P, evac_eng(), tb[0:NS], bk[0:NS])
        P.dma("sp", O["s_conv"][:, 2, j * 512:(j + 1) * 512], tb[0:NS, :])

    P.barrier()
    A.off = mark2
    def simple_block(col0, dst, r0, func):
        wb = load_w(col0, 512)
        for m in range(4):
            for (t0, n) in TILES:
                bk = banks.next()
                fm(bk, wb, m, t0, n)
                ob = osb.next()[:, 0:n]
                act(P, ob, bk[:, 0:n], func)
                P.dma("sp", dst[r0 + m * 128:r0 + (m + 1) * 128, t0:t0 + n], ob)

    if "simple" in SUB:
        for j in range(2):
            simple_block(O_ZA + j * 512, S["zaT"], j * 512, AF.Silu)
        simple_block(O_ZB, S["zbT"], 0, AF.Silu)
        for j in range(2):
            simple_block(O_GA + j * 512, S["gaT"], j * 512, AF.Sigmoid)
        for j in range(2):
            simple_block(O_GB + j * 512, S["gbT"], j * 512, AF.Sigmoid)

    wb = load_w(O_A, 16)
    dtb = A.alloc([4, 8], F32)
    nega = A.alloc([4, 8], F32)
    for q in range(4):
        P.dma("sp", dtb[:, q, :], bcast_rows(I["dt_bias"], 8))
        P.dma("sp", nega[:, q, :], bcast_rows(I["a_log"], 8))
    act(P, nega, nega, AF.Exp)
    ts(P, "dve", nega, nega, -1.0, ALU.mult)
    abt = Rot([A.alloc([6, 4, 8], F32) for _ in range(2)])
    gbs = Rot([A.alloc([4, 16], F32) for _ in range(2)])
    for (t0, n) in (TILES if "ab" in SUB else []):
        nsub = 4 if n == 512 else 1
        np_ = 128 if n == 512 else NS
        bk = banks.next()
        for sb in range(nsub):
            for kc in range(8):
                mm(P, bk[0:np_, sb * 16:(sb + 1) * 16], xnT[:, kc, t0 + sb * 128:t0 + sb * 128 + np_],
                   wb[:, kc, 0:16], kc == 0, kc == 7)
        pv = bk[0:np_, 0:nsub * 16].re("p (s c) -> p s c", c=16)
        w_ = abt.next()[0:np_, :, 0:nsub, :]
        gb = gbs.next()[0:np_, 0:nsub, :]
        xx, ax, ee, ll = w_[:, 0], w_[:, 1], w_[:, 2], w_[:, 3]
        tt(P, "dve", xx, pv[:, :, 0:8], dtb[0:np_, 0:nsub, :], ALU.add)
        act(P, ax, xx, AF.Abs)
        act(P, ee, ax, AF.Exp, scale=-1.0)
        act(P, ll, ee, AF.Ln, bias=1.0)
        stt(P, "dve", xx, xx, 0.0, ll, ALU.max, ALU.add)
        tt(P, "dve", gb[:, :, 0:8], xx, nega[0:np_, 0:nsub, :], ALU.mult)
        act(P, gb[:, :, 8:16], pv[:, :, 8:16], AF.Sigmoid)
        if n == 512:
            P.dma("sp", S["gbeta"][t0:t0 + 512, :].re("(s p) c -> p s c", p=128), gb)
        else:
            P.dma("sp", S["gbeta"][t0:t0 + NS, :], gb[:, 0, :])

    stgb = Rot([A.alloc([2048], BF16) for _ in range(2)])
    vbb = Rot([A.alloc([512], BF16) for _ in range(3)])
    GS = [int(x) for x in os.environ.get("MK_G", "0,1,2").split(",")]
    for g, dil in (enumerate(DILS) if "att" in SUB else []):
        if g not in GS:
            continue
        lc = L // dil
        spc = 2048 // dil
        ATT = os.environ.get("MK_ATT", "qk,ktm,v").split(",")
        for t, nm in (((0, "qb"), (1, "kb")) if "qk" in ATT else []):
            wb = load_w(O_QKVB + g * 1536 + t * 512, 512)
            dstT = S["%s%d" % (nm, g)]
            for h in range(4):
                for spn in range(NTP // 2048):
                    sgb = stgb.next()
                    for sb in range(4):
                        bk = banks.next()
                        fm(bk, wb, h, spn * 2048 + sb * 512, 512)
                        i0 = sb * 512 // dil
                        cp(P, evac_eng(), sgb.re("p (r i) -> p r i", r=dil)[:, :, i0:i0 + 512 // dil],
                           bk.re("p (i r) -> p r i", r=dil))
                    s_, n_ = spn // 2, spn % 2
                    d = dstT[h * 128:(h + 1) * 128, s_ * L:(s_ + 1) * L].re("p (r i) -> p r i", r=dil)
                    P.dma("sp", d[:, :, n_ * spc:(n_ + 1) * spc], sgb.re("p (r i) -> p r i", r=dil))
                bk = banks.next()
                fm(bk, wb, h, NTP, NS)
                cp(P, evac_eng(), (K.qs if t == 0 else K.ks)[:, g * 4 + h, :], bk[:, 0:NS])
            if t == 1 and "ktm" in ATT:
                for s in range(NSEQ):
                    for r in range(dil):
                        base = s * L + L - 128 * dil + r
                        bk = banks.next()
                        tm(bk, wb, lambda kc, base=base: xnT[:, kc, base:base + 128 * dil:dil], 128)
                        tb = tmb.next()
                        cp(P, evac_eng(), tb, bk)
                        P.dma("sp", O["p_k%d" % g][s, r::dil, :], tb)
                bk = banks.next()
                tm(bk, wb, lambda kc: xnT[:, kc, NTP:NT], NS)
                tb = tmb.next()
                cp(P, evac_eng(), tb[0:NS], bk[0:NS])
                P.dma("sp", O["s_k%d" % g], tb[0:NS])
        if "v" not in ATT:
            continue
        wb = load_w(O_QKVB + g * 1536 + 1024, 512)
        nb = lc // 128
        MKV = os.environ.get("MK_V", "main,pv,samp").split(",")
        for s in range(NSEQ if "main" in MKV else 0):
            for r in range(dil):
                for n in range(nb):
                    base = s * L + n * 128 * dil + r
                    bk = banks.next()
                    tm(bk, wb, lambda kc, base=base: xnT[:, kc, base:base + 128 * dil:dil], 128)
                    vb = vbb.next()
                    cp(P, evac_eng(), vb, bk)
                    row0 = s * L + r * lc + n * 128
                    P.dma("sp", S["vb%d" % g][row0:row0 + 128, :], vb)
                    if n == nb - 1 and "pv" in MKV:
                        tb = tmb.next()
                        cp(P, evac_eng(), tb, bk)
                        P.dma("sp", O["p_v%d" % g][s, r::dil, :], tb)
        if "samp" not in MKV:
            continue
        bk = banks.next()
        tm(bk, wb, lambda kc: xnT[:, kc, NTP:NT], NS)
        tb = tmb.next()
        cp(P, evac_eng(), tb[0:NS], bk[0:NS])
        P.dma("sp", O["s_v%d" % g], tb[0:NS])
        P.dma("sp", S["vs"][:, g, :], tb[0:NS])


def bc(v, dims):
    p = v.ap.ap[0]
    return v.raw([[p[0], p[1]]] + dims)


def gdn_stream(C, T, c, cols, Sst, first_zero, h0=0, NH=8, S_list=None, s_load=None, s_store=None):
    P, K, S = C.P, C.K, C.S
    banks = C.banks
    hs = slice(h0, h0 + NH)
    kT_v = S["kT"].re("(h p) t -> p h t", p=128)[:, hs]
    qT_v = S["qT"].re("(h p) t -> p h t", p=128)[:, hs]
    vT_v = S["vT"].re("(h p) t -> p h t", p=128)[:, hs]
    za_v = S["zaT"].re("(h p) t -> p h t", p=128)[:, hs]
    ya_v = S["yaT"].re("(h p) t -> p h t", p=128)[:, hs]
    ghn = T["ghn"]
    HG = [(hb, min(4, NH - hb)) for hb in range(0, NH, 4)]
    for ci, col0 in enumerate(cols):
        def loads(cj):
            cl = cols[cj]
            kq_ = T["kq"][cj % 2][:, :, :, 0:c]
            P.dma("sp", kq_[:, :, 0, :], kT_v[:, :, cl:cl + c])
            P.dma("sp", kq_[:, :, 1, :], qT_v[:, :, cl:cl + c])
            P.dma("sp", T["vT"][cj % 2][:, :, 0:c], vT_v[:, :, cl:cl + c])
            P.dma("sp", T["gb"][cj % 2][0:c], S["gbeta"][cl:cl + c, :])
            P.dma("sp", T["za"][cj % 2][:, :, 0:c], za_v[:, :, cl:cl + c])
        if ci == 0:
            loads(0)
            if s_load:
                s_load(0, S_list[0])
        if ci + 1 < len(cols):
            loads(ci + 1)
            if s_load:
                s_load(ci + 1, S_list[(ci + 1) % 2])
        if S_list is not None:
            Sst = S_list[ci % 2]
        kq = T["kq"][ci % 2][:, :, :, 0:c]
        vT = T["vT"][ci % 2][:, :, 0:c]
        za = T["za"][ci % 2][:, :, 0:c]
        gb = T["gb"][ci % 2][0:c]
        gg, bb = gb[:, h0:h0 + NH], gb[:, 8 + h0:8 + h0 + NH]
        sm = T["sm"]
        Gcol, GL, kdecs, glast, nbeta = (sm[:, i * NH:(i + 1) * NH] for i in range(5))
        t = [x[0:c, :, 0:c] for x in T["t"]]
        tf = [x[:, :, 0:c] for x in T["t"]]
        td = [x[0:c] for x in T["t"]]
        bk = banks.next()
        mm(P, bk[0:c, 0:NH], K.umat[0:c, 0:c], gg)
        mm(P, bk[:, 8:8 + NH], K.ones[0:c, :], gg)
        cp(P, "dve", Gcol[0:c], bk[0:c, 0:NH])
        cp(P, "dve", GL, bk[:, 8:8 + NH])
        tt(P, "pool", kdecs[0:c], GL[0:c], Gcol[0:c], ALU.subtract)
        act(P, kdecs[0:c], kdecs[0:c], AF.Exp)
        act(P, glast, GL, AF.Exp)
        ts(P, "pool", nbeta[0:c], bb, -1.0, ALU.mult)
        yield
        Ug = t[0]
        tt(P, "dve", Ug, bc(K.umat[0:c, 0:c], [[0, NH], [1, c]]), bc(gg, [[1, NH], [0, c]]), ALU.mult)
        EGb = tf[2]
        dT = t[1]
        for hb, nh in HG:
            bX = banks.next()
            mm(P, bX[:, 0:nh * c].re("p (h i) -> p h i", h=nh), K.ones[0:c, :], Ug[:, hb:hb + nh, :])
            act(P, EGb[:, hb:hb + nh, :], bX[:, 0:nh * c].re("p (h i) -> p h i", h=nh), AF.Exp)
            for h in range(hb, hb + nh):
                stt(P, "dve", dT[:, h, :], bX[0:c, (h - hb) * c:(h - hb + 1) * c], Gcol[0:c, h:h + 1],
                    K.maskT[0:c, 0:c], ALU.subtract, ALU.add)
        decT = t[3]
        act(P, decT, dT, AF.Exp)
        DSb = t[4]
        tt(P, "pool", DSb, decT, bc(K.strictT[0:c, 0:c], [[0, NH], [1, c]]), ALU.mult)
        tt(P, "pool", DSb, DSb, bc(nbeta[0:c], [[1, NH], [0, c]]), ALU.mult)
        kqd = T["kqd"][:, :, :, 0:c]
        tt(P, "pool", kqd, kq, bc(EGb, [[EGb.ap.ap[1][0], NH], [0, 2], [1, c]]), ALU.mult)
        yield
        vtok, kdec = td[5], td[6]
        for src, dst, scale in ((vT, vtok, None), (kq[:, :, 0, :], kdec, True)):
            for hb, nh in HG:
                bk = banks.next()
                for h in range(hb, hb + nh):
                    tr(P, bk[0:c, (h - hb) * 128:(h - hb + 1) * 128], src[:, h, :], K.ident)
                pv = bk[0:c, 0:nh * 128].re("p (h d) -> p h d", h=nh)
                if scale is None:
                    cp(P, "act", dst[:, hb:hb + nh, :], pv)
                else:
                    tt(P, "dve", dst[:, hb:hb + nh, :], pv, bc(kdecs[0:c, hb:hb + nh], [[1, nh], [0, 128]]), ALU.mult)
        yield
        pa, pb = T["pa"][0:c, :, :, 0:c], T["pb"][0:c, :, :, 0:c]
        qkT = t[7]
        for h2 in range(NH // 2):
            bk = banks.next()
            for hh in range(2):
                h = h2 * 2 + hh
                mm(P, bk[0:c, hh * 2 * c:(hh + 1) * 2 * c].re("p (k i) -> p k i", k=2), kq[:, h, 0, :], kq[:, h, :, :])
            pv = bk[0:c, 0:4 * c].re("p (h k i) -> p h k i", h=2, k=2)
            tt(P, "dve", pa[:, h2 * 2:h2 * 2 + 2, 0, :], pv[:, :, 0, :], DSb[:, h2 * 2:h2 * 2 + 2, :], ALU.mult)
            tt(P, "dve", qkT[:, h2 * 2:h2 * 2 + 2, :], pv[:, :, 1, :], decT[:, h2 * 2:h2 * 2 + 2, :], ALU.mult)
        yield
        Pm = t[8]
        if c > 1:
            for hb, nh in HG:
                bk = banks.next()
                for h in range(hb, hb + nh):
                    tr(P, bk[0:c, (h - hb) * c:(h - hb + 1) * c], pa[:, h, 0, :], K.ident[0:c, 0:c])
                cp(P, "act", pa[:, hb:hb + nh, 1, :], bk[0:c, 0:nh * c].re("p (h i) -> p h i", h=nh))
            tt(P, "pool", Pm, pa[:, :, 0, :], bc(K.ident[0:c, 0:c], [[0, NH], [1, c]]), ALU.add)
            yield
            cur, nxt = pa, pb
            for lvl in range(1, 7):
                for h2 in range(NH // 2):
                    bk = banks.next()
                    for hh in range(2):
                        h = h2 * 2 + hh
                        if lvl < 6:
                            mm(P, bk[0:c, hh * 2 * c:hh * 2 * c + c], cur[:, h, 1, :], cur[:, h, 0, :])
                        mm(P, bk[0:c, hh * 2 * c + c:(hh + 1) * 2 * c], cur[:, h, 0, :], cur[:, h, 1, :])
                    pv = bk[0:c, 0:4 * c].re("p (h k i) -> p h k i", h=2, k=2)
                    if lvl < 6:
                        cp(P, "act" if h2 % 2 else "dve", nxt[:, h2 * 2:h2 * 2 + 2, :, :], pv)
                    else:
                        cp(P, "act" if h2 % 2 else "dve", nxt[:, h2 * 2:h2 * 2 + 2, 1, :], pv[:, :, 1, :])
                yield
                for hb, nh in HG:
                    bk = banks.next()
                    for h in range(hb, hb + nh):
                        mm(P, bk[0:c, (h - hb) * c:(h - hb + 1) * c], nxt[:, h, 1, :], Pm[:, h, :])
                    tt(P, "dve", Pm[:, hb:hb + nh, :], Pm[:, hb:hb + nh, :],
                       bk[0:c, 0:nh * c].re("p (h i) -> p h i", h=nh), ALU.add)
                yield
                cur, nxt = nxt, cur
        else:
            memset(P, "pool", Pm, 1.0)
        if ci == 0 and first_zero:
            memset(P, "pool", Sst, 0.0)
        R = td[0]
        for hb, nh in HG:
            bk = banks.next()
            for h in range(hb, hb + nh):
                mm(P, bk[0:c, (h - hb) * 128:(h - hb + 1) * 128], kqd[:, h, 0, :], Sst[:, h, :])
            tt(P, "dve", R[:, hb:hb + nh, :], vtok[:, hb:hb + nh, :],
               bk[0:c, 0:nh * 128].re("p (h d) -> p h d", h=nh), ALU.subtract)
        yield
        vn = td[1]
        for hb, nh in HG:
            bk = banks.next()
            for h in range(hb, hb + nh):
                mm(P, bk[0:c, (h - hb) * 128:(h - hb + 1) * 128], Pm[:, h, :], R[:, h, :])
            tt(P, "dve", vn[:, hb:hb + nh, :], bk[0:c, 0:nh * 128].re("p (h d) -> p h d", h=nh),
               bc(bb[:, hb:hb + nh], [[1, nh], [0, 128]]), ALU.mult)
        yield
        oT = tf[5]
        for hb, nh in HG:
            bk = banks.next()
            for h in range(hb, hb + nh):
                o_ = bk[:, (h - hb) * c:(h - hb + 1) * c]
                mm(P, o_, Sst[:, h, :], kqd[:, h, 1, :], True, False)
                mm(P, o_, vn[:, h, :], qkT[:, h, :], False, True)
            cp(P, "act", oT[:, hb:hb + nh, :], bk[:, 0:nh * c].re("p (h i) -> p h i", h=nh))
        yield
        bks = []
        for hb, nh in HG:
            bk = banks.next()
            bks.append(bk)
            for h in range(hb, hb + nh):
                mm(P, bk[:, (h - hb) * 128:(h - hb + 1) * 128], kdec[:, h, :], vn[:, h, :])
        tt(P, "pool", Sst, Sst, bc(glast, [[1, NH], [0, 128]]), ALU.mult)
        for (hb, nh), bk in zip(HG, bks):
            tt(P, "dve", Sst[:, hb:hb + nh, :], Sst[:, hb:hb + nh, :],
               bk[:, 0:nh * 128].re("p (h d) -> p h d", h=nh), ALU.add)
        yield
        sq = tf[2]
        tt(P, "pool", sq, oT, oT, ALU.mult)
        rs = tf[3]
        for hb, nh in HG:
            bk = banks.next()
            mm(P, bk[:, 0:nh * c].re("p (h i) -> p h i", h=nh), K.meanm, sq[:, hb:hb + nh, :])
            rsqrt(P, rs[:, hb:hb + nh, :], bk[:, 0:nh * c].re("p (h i) -> p h i", h=nh), EPS)
        y1 = tf[4]
        stt(P, "dve", y1, oT, ghn[:, 0:1], rs, ALU.mult, ALU.mult)
        yb = T["yb"][:, :, 0:c]
        tt(P, "pool", yb, y1, za, ALU.mult)
        P.dma("sp", ya_v[:, :, col0:col0 + c], yb)
        if s_store:
            s_store(ci, Sst)
        yield


def gdn_tiles(C, NH, cw=128):
    A = C.A
    T = {}
    T["kq"] = [A.alloc([NH, 2, cw], F32) for _ in range(2)]
    T["vT"] = [A.alloc([NH, cw], F32) for _ in range(2)]
    T["gb"] = [A.alloc([16], F32) for _ in range(2)]
    T["za"] = [A.alloc([NH, cw], F32) for _ in range(2)]
    T["sm"] = A.alloc([5 * NH], F32)
    T["t"] = [A.alloc([NH, 128], F32) for _ in range(9)]
    T["kqd"] = A.alloc([NH, 2, cw], F32)
    T["pa"] = A.alloc([NH, 2, cw], F32)
    T["pb"] = A.alloc([NH, 2, cw], F32)
    T["yb"] = A.alloc([NH, max(cw, 2)], BF16)
    T["S"] = A.alloc([NH, 128], F32)
    return T


def run_interleaved(gens, offsets=None):
    gens = list(gens)
    offsets = list(offsets) if offsets else [0] * len(gens)
    rnd = 0
    live = list(range(len(gens)))
    while live:
        nxt_live = []
        for k in live:
            if rnd < offsets[k]:
                nxt_live.append(k)
                continue
            try:
                next(gens[k])
                nxt_live.append(k)
            except StopIteration:
                pass
        live = nxt_live
        rnd += 1


def phase_b(C):
    P, A, I, O, S, K = C.P, C.A, C.I, C.O, C.S, C.K
    ghn = A.alloc([1], F32)
    P.dma("sp", ghn, I["g_head_norm"].re("o d -> d o"))
    NH = int(os.environ.get("MK_NH", "4"))
    NG = 8 // NH
    markb = A.off
    Ts = [gdn_tiles(C, NH) for _ in range(2 * NG)]
    for T in Ts:
        T["ghn"] = ghn
    MODE = os.environ.get("MK_B", "prompt,sample").split(",")
    NCH = int(os.environ.get("MK_NCH", str(L // 128)))
    if "prompt" in MODE:
        gens = []
        for s in range(NSEQ):
            for hg in range(NG):
                T = Ts[s * NG + hg]
                gens.append(gdn_stream(C, T, 128, [s * L + ch * 128 for ch in range(NCH)], T["S"], True,
                                       hg * NH, NH))
        STG = int(os.environ.get("MK_STG", "6"))
        run_interleaved(gens, [STG * k for k in range(len(gens))])
        for s in range(NSEQ):
            for hg in range(NG):
                P.dma("sp", O["p_gdn"][s, hg * NH:(hg + 1) * NH].re("h k v -> k h v"), Ts[s * NG + hg]["S"])
    if "sample" in MODE:
        def sample_gen(T, bs, hg):
            for b in bs:
                P.dma("sp", T["S"], I["state_gdn"][b, hg * NH:(hg + 1) * NH].re("h k v -> k h v"), acc=False)
                yield from gdn_stream(C, T, 1, [NTP + b], T["S"], False, hg * NH, NH)
                P.dma("sp", O["s_gdn"][b, hg * NH:(hg + 1) * NH].re("h k v -> k h v"), T["S"])
        P.barrier()
        A.off = markb
        NSTR = 8 // NG
        Tss = [gdn_tiles(C, NH, 1) for _ in range(NSTR * NG)]
        for T in Tss:
            T["ghn"] = ghn
        gens = []
        for par in range(NSTR):
            for hg in range(NG):
                T = Tss[par * NG + hg]
                bs = list(range(par, NS, NSTR))
                S2 = [T["S"], A.alloc([NH, 128], F32)]
                hsl = slice(hg * NH, (hg + 1) * NH)

                def s_load(ci, tile, bs=bs, hsl=hsl):
                    P.dma("sp", tile, I["state_gdn"][bs[ci], hsl].re("h k v -> k h v"), acc=False)

                def s_store(ci, tile, bs=bs, hsl=hsl):
                    P.dma("sp", O["s_gdn"][bs[ci], hsl].re("h k v -> k h v"), tile)
                gens.append(gdn_stream(C, T, 1, [NTP + b for b in bs], None, False, hg * NH, NH,
                                       S_list=S2, s_load=s_load, s_store=s_store))
        run_interleaved(gens)


def phase_c(C):
    P, A, I, O, S, K = C.P, C.A, C.I, C.O, C.S, C.K
    banks = C.banks
    SC = float(128 ** -0.5)
    rel = A.alloc([12], F32)
    P.dma("sp", rel[0:32], I["rel_table"])
    oh = A.alloc([3, 384], F32)
    P.dma("sp", oh[0:32], I["c_oh"].re("g b c -> b g c"))
    ohs = A.alloc([3, 128], F32)
    P.dma("sp", ohs[0:32], I["c_ohs"].re("g b c -> b g c"))
    oh0 = A.alloc([3, 1], F32)
    P.dma("sp", oh0[0:32], I["c_oh0"].re("g b c -> b g c"))
    negm = A.alloc([384], F32)
    P.dma("sp", negm, I["c_negm"])
    bias2 = A.alloc([12, 256], F32)
    biasS = A.alloc([12], F32)
    bias0 = A.alloc([12], F32)
    relb = A.alloc([128], F32)
    vp = Rot([A.alloc([384], F32) for _ in range(2)])
    for g in range(3):
        for h in range(4):
            gh = g * 4 + h
            ts(P, "dve", relb[0:32], K.ones[0:32], rel[0:32, gh:gh + 1], ALU.mult)
            bk = banks.next()
            mm(P, bk[:, 0:384], relb[0:32], oh[0:32, g, :])
            v_ = vp.next()
            tt(P, "dve", v_, bk[:, 0:384], negm, ALU.add)
            P.dma("sp", S["bias"][gh, 0:128, :], v_)
            P.dma("sp", S["bias"][gh, 128:256, :], v_)
        bk = banks.next()
        mm(P, bk[:, 0:4], ohs[0:32, g, :], rel[0:32, g * 4:(g + 1) * 4])
        cp(P, "dve", biasS[:, g * 4:(g + 1) * 4], bk[:, 0:4])
        bk = banks.next()
        mm(P, bk[0:1, 0:4], oh0[0:32, g, :], rel[0:32, g * 4:(g + 1) * 4])
        cp(P, "dve", bias0[0:1, g * 4:(g + 1) * 4], bk[0:1, 0:4])
    P.barrier()
    for gh in range(12):
        base = gh * 256 * 384 + 255
        P.dma("sp", bias2[:, gh, 0:128], S["bias"].raw([[383, 128], [1, 128]], off=base + 128 * 383))
        P.dma("sp", bias2[:, gh, 128:256], S["bias"].raw([[383, 128], [1, 128]], off=base))
    MODE = os.environ.get("MK_C", "prompt,sample").split(",")
    mark = A.off
    if "prompt" in MODE:
        acc2s = Rot([A.alloc([2, L], F32) for _ in range(2)])
        qcs = Rot([A.alloc([L], BF16) for _ in range(3)])
        kcs = Rot([A.alloc([L], BF16) for _ in range(3)])
        vcs = Rot([A.alloc([L // 128, 128], BF16) for _ in range(3)])
        lgs = Rot([A.alloc([256], F32) for _ in range(3)])
        Es = Rot([A.alloc([256], BF16) for _ in range(4)])
        zbt = A.alloc([L], F32)
        ybt = A.alloc([L], BF16)
        evs = Rot([A.alloc([2, 128], F32) for _ in range(3)])
        items = [(s, h, g, r) for s in range(NSEQ) for h in range(4) for g in range(3) for r in range(DILS[g])]

        def c_loads(s, h, g, r):
            lc = L // DILS[g]
            nb = lc // 128
            qc, kc, vc = qcs.next()[:, 0:lc], kcs.next()[:, 0:lc], vcs.next()[:, 0:nb, :]
            c0 = s * L + r * lc
            P.dma("sp", qc, S["qb%d" % g][h * 128:(h + 1) * 128, c0:c0 + lc], acc=False)
            P.dma("sp", kc, S["kb%d" % g][h * 128:(h + 1) * 128, c0:c0 + lc], acc=False)
            P.dma("sp", vc, S["vb%d" % g][c0:c0 + lc, h * 128:(h + 1) * 128].re("(n p) d -> p n d", p=128),
                  acc=False)
            return qc, kc, vc
        pre = {0: c_loads(*items[0])}
        item_i = [0]
        for s in range(NSEQ):
            for h in range(4):
                acc2 = acc2s.next()
                for g, dil in enumerate(DILS):
                    gh = g * 4 + h
                    lc = L // dil
                    nb = lc // 128
                    for r in range(dil):
                        ii = item_i[0]
                        item_i[0] += 1
                        qc, kc, vc = pre.pop(ii)
                        if ii + 1 < len(items):
                            pre[ii + 1] = c_loads(*items[ii + 1])
                        Eprev = None

                        def logits(n):
                            nq = 256 if n < nb - 1 else 128
                            bk = banks.next()
                            mm(P, bk[:, 0:nq], kc[:, n * 128:(n + 1) * 128], qc[:, n * 128:n * 128 + nq])
                            lg = lgs.next()
                            stt(P, "dve", lg[:, 0:nq], bk[:, 0:nq], SC, bias2[:, gh, 0:nq], ALU.mult, ALU.add)
                            E = Es.next()
                            act(P, E[:, 0:nq], lg[:, 0:nq], AF.Exp)
                            return E
                        Enext = logits(0)
                        for n in range(nb):
                            E = Enext
                            if n + 1 < nb:
                                Enext = logits(n + 1)
                            b2 = banks.next()
                            if n > 0:
                                mm(P, b2[:, 0:128], vc[:, n - 1, :], Eprev[:, 128:256], True, False)
                            mm(P, b2[:, 0:128], vc[:, n, :], E[:, 0:128], n == 0, True)
                            if n > 0:
                                mm(P, b2[:, 128:256], K.onesb, Eprev[:, 128:256], True, False)
                            mm(P, b2[:, 128:256], K.onesb, E[:, 0:128], n == 0, True)
                            Eprev = E
                            lo = r + n * 128 * dil
                            dst = acc2[:, :, lo:lo + 127 * dil + 1:dil]
                            src = b2[:, 0:256].re("p (a q) -> p a q", a=2)
                            if g == 0:
                                cp(P, "act", dst, src)
                            else:
                                ev = evs.next()
                                cp(P, "act", ev, src)
                                tt(P, "pool", dst, dst, ev, ALU.add)
                P.dma("sp", zbt, S["zbT"][h * 128:(h + 1) * 128, s * L:(s + 1) * L], acc=False)
                for qd in range(4):
                    sl = slice(qd * 1024, (qd + 1) * 1024)
                    recip(P, acc2[:, 1, sl], acc2[:, 1, sl])
                    tt(P, "pool", acc2[:, 0, sl], acc2[:, 0, sl], acc2[:, 1, sl], ALU.mult)
                    tt(P, "pool", ybt[:, sl], acc2[:, 0, sl], zbt[:, sl], ALU.mult)
                P.dma("sp", S["ybT"][h * 128:(h + 1) * 128, s * L:(s + 1) * L], ybt)
    P.barrier()
    A.off = mark
    if "sample" in MODE:
        Kcs = [Rot([A.alloc([512], F32) for _ in range(2)]) for g in range(3)]
        Vcs = [Rot([A.alloc([512], F32) for _ in range(2)]) for g in range(3)]
        KcT = Rot([A.alloc([4, 128], F32) for _ in range(2)])
        vsb = Rot([A.alloc([3, 512], F32) for _ in range(2)])
        zbs = A.alloc([4, NS], F32)
        ybs = A.alloc([4, NS], BF16)
        sw = Rot([A.alloc([64], F32) for _ in range(2)])
        P.dma("sp", zbs, S["zbT"].re("(h p) t -> p h t", p=128)[:, :, NTP:NT])
        def cs_loads(b):
            kk = [Kcs[g].next() for g in range(3)]
            vv = [Vcs[g].next() for g in range(3)]
            for g, dil in enumerate(DILS):
                P.dma("sp", kk[g], I["ck%d" % g][b, 0::dil, :], acc=False)
                P.dma("sp", vv[g], I["cv%d" % g][b, 0::dil, :], acc=False)
            vs_ = vsb.next()
            P.dma("sp", vs_[0:1], S["vs"][b:b + 1], acc=False)
            return kk, vv, vs_
        nxt_cs = cs_loads(0)
        for b in range(NS):
            kk, vv, vs_ = nxt_cs
            if b + 1 < NS:
                nxt_cs = cs_loads(b + 1)
            w_ = sw.next()
            bL = banks.next()
            for g in range(3):
                bk = banks.next()
                for h in range(4):
                    tr(P, bk[:, h * 128:(h + 1) * 128], kk[g][:, h * 128:(h + 1) * 128], K.ident)
                kt = KcT.next()
                cp(P, "act", kt, bk.re("p (h k) -> p h k", h=4))
                for h in range(4):
                    gh = g * 4 + h
                    mm(P, bL[:, gh:gh + 1], kt[:, h, :], K.qs[:, gh, b:b + 1])
            for gh in range(12):
                mm(P, bL[0:1, 16 + gh:17 + gh], K.ks[:, gh, b:b + 1], K.qs[:, gh, b:b + 1])
            lgS, ES, lg0, E0 = w_[:, 0:12], w_[:, 12:24], w_[0:1, 24:36], w_[0:1, 36:48]
            stt(P, "dve", lgS, bL[:, 0:12], SC, biasS, ALU.mult, ALU.add)
            act(P, ES, lgS, AF.Exp)
            stt(P, "dve", lg0, bL[0:1, 16:28], SC, bias0[0:1], ALU.mult, ALU.add)
            act(P, E0, lg0, AF.Exp)
            bO = banks.next()
            for h in range(4):
                for g in range(3):
                    gh = g * 4 + h
                    mm(P, bO[:, h:h + 1], vv[g][:, h * 128:(h + 1) * 128], ES[:, gh:gh + 1], g == 0, False)
                    mm(P, bO[:, h:h + 1], vs_[0:1, g, h * 128:(h + 1) * 128], E0[0:1, gh:gh + 1], False, g == 2)
                for g in range(3):
                    gh = g * 4 + h
                    mm(P, bO[:, 4 + h:5 + h], K.ones, ES[:, gh:gh + 1], g == 0, False)
                    mm(P, bO[:, 4 + h:5 + h], K.ones[0:1, :], E0[0:1, gh:gh + 1], False, g == 2)
            ob = w_[:, 48:56]
            cp(P, "dve", ob, bO[:, 0:8])
            recip(P, ob[:, 4:8], ob[:, 4:8])
            tt(P, "pool", ob[:, 0:4], ob[:, 0:4], ob[:, 4:8], ALU.mult)
            tt(P, "pool", ybs[:, :, b], ob[:, 0:4], zbs[:, :, b], ALU.mult)
        P.dma("sp", S["ybT"].re("(h p) t -> p h t", p=128)[:, :, NTP:NT], ybs)


def phase_d(C):
    P, A, I, O, S, K = C.P, C.A, C.I, C.O, C.S, C.K
    banks = C.banks
    wpa = A.alloc([8, DM], BF16)
    wpb = A.alloc([4, DM], BF16)
    wo = A.alloc([8, DM], BF16)
    gpost = A.alloc([DM], F32)
    P.dma("pool", wpa, I["w_proj_a"].re("(k p) n -> p k n", p=128))
    P.dma("pool", wpb, I["w_proj_b"].re("(k p) n -> p k n", p=128))
    P.dma("pool", wo, I["w_out"].re("(k p) n -> p k n", p=128))
    P.dma("sp", gpost, bcast_rows(I["g_post"], DM))
    yas = Rot([A.alloc([8, 512], BF16) for _ in range(2)])
    ybs_ = Rot([A.alloc([4, 512], BF16) for _ in range(2)])
    gas = Rot([A.alloc([8, 512], F32) for _ in range(int(os.environ.get('MK_GB', '2')))])
    gbs_ = Rot([A.alloc([8, 512], F32) for _ in range(int(os.environ.get('MK_GB', '2')))])
    xts = Rot([A.alloc([4, DM], F32) for _ in range(2)])
    hTs = Rot([A.alloc([8, 512], BF16) for _ in range(2)])
    t1s = Rot([A.alloc([512], F32) for _ in range(2)])
    t2s = Rot([A.alloc([512], F32) for _ in range(2)])
    junk = A.alloc([DM], F32)
    ysbs = Rot([A.alloc([DM], F32) for _ in range(2)])
    sts = Rot([A.alloc([4], F32) for _ in range(4)])
    fmv = lambda nm: S[nm].re("(k p) t -> p k t", p=128)
    TILES = [(ti * 512, 512) for ti in range(NTP // 512)] + [(NTP, NS)]
    def d_loads(t0, n):
        ya, yb, ga, gb_, xt = yas.next(), ybs_.next(), gas.next(), gbs_.next(), xts.next()
        P.dma("sp", ya[:, :, 0:n], fmv("yaT")[:, :, t0:t0 + n], acc=False)
        P.dma("sp", yb[:, :, 0:n], fmv("ybT")[:, :, t0:t0 + n], acc=False)
        P.dma("sp", ga[:, :, 0:n], fmv("gaT")[:, :, t0:t0 + n], acc=False)
        P.dma("sp", gb_[:, :, 0:n], fmv("gbT")[:, :, t0:t0 + n], acc=False)
        if n == 512:
            P.dma("sp", xt, I["x_p"][t0:t0 + 512, :].re("(s p) d -> p s d", p=128), acc=False)
        else:
            P.dma("sp", xt[0:NS, 0, :], I["x_s"], acc=False)
        return ya, yb, ga, gb_, xt
    nxt_ld = d_loads(*TILES[0])
    for ti, (t0, n) in enumerate(TILES):
        nsub = 4 if n == 512 else 1
        np_ = 128 if n == 512 else NS
        ya, yb, ga, gb_, xt = nxt_ld
        hT = hTs.next()
        if ti + 1 < len(TILES):
            nxt_ld = d_loads(*TILES[ti + 1])
        for e in range(8):
            pA, pB = banks.next(), banks.next()
            for k in range(8):
                mm(P, pA[:, 0:n], wpa[:, k, e * 128:(e + 1) * 128], ya[:, k, 0:n], k == 0, k == 7)
            for k in range(4):
                mm(P, pB[:, 0:n], wpb[:, k, e * 128:(e + 1) * 128], yb[:, k, 0:n], k == 0, k == 3)
            t1, t2 = t1s.next()[:, 0:n], t2s.next()[:, 0:n]
            tt(P, "dve", t1, ga[:, e, 0:n], pA[:, 0:n], ALU.mult)
            tt(P, "dve", t2, gb_[:, e, 0:n], pB[:, 0:n], ALU.mult)
            tt(P, "pool", hT[:, e, 0:n], t1, t2, ALU.add)
        for sb in range(nsub):
            bks = [banks.next(), banks.next()]
            for half in range(2):
                for k in range(8):
                    mm(P, bks[half][0:np_, :], hT[:, k, sb * 128:sb * 128 + np_], wo[:, k, half * 512:(half + 1) * 512],
                       k == 0, k == 7)
            for half in range(2):
                act(P, junk[0:np_, half * 512:(half + 1) * 512], bks[half][0:np_, :], AF.Square)
            st_ = sts.next()[0:np_]
            rsum(P, "dve", st_[:, 0:1], junk[0:np_])
            rsqrt(P, st_[:, 1:2], st_[:, 0:1], EPS, 1.0 / DM)
            ysb = ysbs.next()[0:np_]
            for half in range(2):
                hs = slice(half * 512, (half + 1) * 512)
                stt(P, "dve", ysb[:, hs], bks[half][0:np_, :], st_[:, 1:2], gpost[0:np_, hs], ALU.mult, ALU.mult)
            tt(P, "pool", ysb, ysb, xt[0:np_, sb, :], ALU.add)
            if n == 512:
                P.dma("sp", O["y_p"][t0 + sb * 128:t0 + (sb + 1) * 128, :], ysb)
            else:
                P.dma("sp", O["y_s"], ysb)


def rel_bucket_np(dist):
    import math
    max_exact = 16
    d = np.maximum(dist, 1).astype(np.float32)
    large = max_exact + (np.log(d / max_exact) / math.log(2048 / max_exact) * (32 - max_exact)).astype(np.int32)
    large = np.minimum(large, 31)
    return np.where(dist < max_exact, dist, large)


def host_consts():
    c = {}
    c["c_ident"] = np.eye(128, dtype=np.float32)
    k = np.arange(128)
    c["c_umat"] = (k[:, None] <= k[None, :]).astype(np.float32)
    c["c_maskT"] = np.where(k[None, :] >= k[:, None], 0.0, NEG).astype(np.float32)
    c["c_strictT"] = (k[None, :] > k[:, None]).astype(np.float32)
    oh = np.zeros((3, 32, 384), np.float32)
    ohs = np.zeros((3, 32, 128), np.float32)
    oh0 = np.zeros((3, 32, 1), np.float32)
    for g, dil in enumerate(DILS):
        bk = rel_bucket_np(np.arange(129, dtype=np.int32) * dil)
        for j in range(129):
            oh[g, bk[j], 127 + j] = 1.0
        for i in range(128):
            ohs[g, bk[128 - i], i] = 1.0
        oh0[g, bk[0], 0] = 1.0
    c["c_oh"] = oh
    negm = np.full((128, 384), NEG, np.float32)
    negm[:, 127:256] = 0.0
    c["c_negm"] = negm
    c["c_ohs"] = ohs
    c["c_oh0"] = oh0
    return c


_NC_CACHE = {}


def kernel(x_prompt, x_sample, state_gdn, state_conv, cache_k_w128, cache_v_w128,
           cache_k_w512, cache_v_w512, cache_k_w2048, cache_v_w2048, rel_table,
           g_pre, w_in, conv_w, a_log, dt_bias, g_head_norm, w_proj_a, w_proj_b, w_out, g_post):
    phases = os.environ.get("MK_PHASES", "ABCD")
    ncores = int(os.environ.get("MK_CORES", str(NCORES)))
    if phases not in _NC_CACHE:
        _NC_CACHE[phases] = build(phases)
    nc = _NC_CACHE[phases]
    f = lambda a: np.ascontiguousarray(np.asarray(a, dtype=np.float32))
    consts = host_consts()
    caches = ((cache_k_w128, cache_v_w128), (cache_k_w512, cache_v_w512), (cache_k_w2048, cache_v_w2048))
    in_maps = []
    for c in range(ncores):
        m = {}
        m["x_p"] = f(x_prompt[NSEQ * c:NSEQ * (c + 1)]).reshape(NTP, DM)
        m["x_s"] = f(x_sample[NS * c:NS * (c + 1)]).reshape(NS, DM)
        m["state_gdn"] = f(state_gdn[0, NS * c:NS * (c + 1)])
        m["state_conv"] = f(state_conv[0, NS * c:NS * (c + 1)])
        for g, (ck, cv) in enumerate(caches):
            m["ck%d" % g] = f(ck[0, NS * c:NS * (c + 1)]).reshape(NS, -1, 512)
            m["cv%d" % g] = f(cv[0, NS * c:NS * (c + 1)]).reshape(NS, -1, 512)
        m["rel_table"] = f(rel_table)
        m["g_pre"] = f(g_pre)
        m["w_in"] = f(w_in[0])
        m["conv_w"] = f(conv_w[0])
        m["a_log"] = f(a_log)
        m["dt_bias"] = f(dt_bias)
        m["g_head_norm"] = f(g_head_norm)
        m["w_proj_a"] = f(w_proj_a[0])
        m["w_proj_b"] = f(w_proj_b[0])
        m["w_out"] = f(w_out[0])
        m["g_post"] = f(g_post)
        m.update(consts)
        in_maps.append(m)
    if os.environ.get("MK_TRACE"):
        res = run_bass_kernel_spmd(nc, in_maps, core_ids=list(range(ncores)), trace=True)
        print("EXEC_TIME_NS", phases, res.exec_time_ns)
    else:
        res = run_bass_kernel_spmd(nc, in_maps, core_ids=list(range(ncores)))
    R = res.results
    cat = lambda k: np.concatenate([np.asarray(r[k]) for r in R], axis=0)
    B = NSEQ * ncores
    SB = NS * ncores
    outs = [cat("y_p").reshape(B, L, DM), cat("y_s").reshape(SB, 1, DM),
            cat("p_gdn").reshape(1, B, 8, 128, 128), cat("p_conv").reshape(1, B, 3, 3072)]
    for g, w in enumerate((128, 512, 2048)):
        outs.append(cat("p_k%d" % g).reshape(1, B, w, 4, 128))
        outs.append(cat("p_v%d" % g).reshape(1, B, w, 4, 128))
    outs.append(cat("s_gdn").reshape(1, SB, 8, 128, 128))
    outs.append(cat("s_conv").reshape(1, SB, 3, 3072))
    for g in range(3):
        outs.append(cat("s_k%d" % g).reshape(1, SB, 1, 4, 128))
        outs.append(cat("s_v%d" % g).reshape(1, SB, 1, 4, 128))
    return tuple(o.astype(np.float32) for o in outs)
```

```python
import os
from contextlib import ExitStack
import numpy as np
import concourse.bass as bass
import concourse.mybir as mybir
from concourse.bass_utils import run_bass_kernel_spmd

F32 = mybir.dt.float32
BF16 = mybir.dt.bfloat16
U8 = mybir.dt.uint8
AF = mybir.ActivationFunctionType
ALU = mybir.AluOpType
AX = mybir.AxisListType

NCORES = 8
L = 4096
NSEQ = 2
NS = 16
NTP = NSEQ * L
NT = NTP + NS
DM = 1024
INW = 11280
EPS = 1e-6
NEG = -1e30
DILS = (1, 4, 16)
O_ZA, O_A, O_QKVB, O_ZB, O_GA, O_GB = 3072, 4096, 4112, 8720, 9232, 10256


class Tl:
    __slots__ = ("w", "rd", "excl")

    def __init__(self, excl=False):
        self.w = {}
        self.rd = {}
        self.excl = excl


class V:
    __slots__ = ("t", "ap")

    def __init__(self, t, ap):
        self.t = t
        self.ap = ap

    def __getitem__(self, k):
        return V(self.t, self.ap[k])

    def re(self, s, **kw):
        return V(self.t, self.ap.rearrange(s, **kw))

    def raw(self, dims, off=0):
        return V(self.t, bass.AP(tensor=self.ap.tensor, offset=self.ap.offset + off, ap=dims))

    def bitcast(self, dt):
        return V(self.t, self.ap.bitcast(dt))


NDS = 12


class Prog:
    ENG = ("sp", "pe", "act", "dve", "pool")

    def __init__(self, nc):
        self.nc = nc
        self.streams = {e: [] for e in self.ENG}
        self.count = {e: 0 for e in self.ENG}
        self.known = {e: {} for e in self.ENG}
        self.dma_n = {"sp": 0, "pool": 0, "act": 0}
        self.latest = {}

    def _resolve(self, eng, deps):
        kn = self.known[eng]
        waits = []
        for k, v in deps.items():
            if k == "pe" and eng == "pe":
                continue
            if kn.get(k, 0) >= v:
                continue
            kn[k] = v
            waits.append((k, v))
        return waits

    @staticmethod
    def _deps(reads, writes, acc):
        deps = {}

        def add(d):
            for k, v in d.items():
                if deps.get(k, 0) < v:
                    deps[k] = v
        for t in reads:
            if t is not None:
                add(t.w)
                if t.excl:
                    add(t.rd)
        for t in writes:
            if t is not None:
                if not acc:
                    add(t.w)
                add(t.rd)
        return deps

    def _commit(self, tok, reads, writes, acc):
        k, v = tok
        self.latest[k] = max(self.latest.get(k, 0), v)
        for t in writes:
            if t is None:
                continue
            if acc:
                t.w[k] = max(t.w.get(k, 0), v)
            else:
                t.w = {k: v}
                t.rd = {}
        for t in reads:
            if t is None or t in writes:
                continue
            t.rd[k] = max(t.rd.get(k, 0), v)

    def op(self, eng, fn, reads=(), writes=(), acc=False):
        reads = [r.t if isinstance(r, V) else r for r in reads]
        writes = [r.t if isinstance(r, V) else r for r in writes]
        deps = self._deps(reads, writes, acc)
        waits = self._resolve(eng, deps)
        self.count[eng] += 1
        tok = (eng, self.count[eng])
        self.streams[eng].append((waits, fn, (eng, 1)))
        self._commit(tok, reads, writes, acc)

    def dma(self, q, out, in_, acc=True):
        if os.environ.get("MK_NOST") and out.t is None and out.ap.tensor.name.startswith("s_"):
            return
        n = self.dma_n[q]
        self.dma_n[q] += 1
        i = n % NDS
        val = 16 * (n // NDS + 1)
        key = ("d", q, i)
        reads = [in_.t]
        writes = [out.t]
        deps = self._deps(reads, writes, acc)
        if n >= NDS:
            deps[key] = max(deps.get(key, 0), val - 16)
        waits = self._resolve(q, deps)
        oa, ia = out.ap, in_.ap
        self.streams[q].append((waits, lambda e: e.dma_start(out=oa, in_=ia), (key, 16)))
        self._commit((key, val), reads, writes, acc)

    def barrier(self):
        for e in self.ENG:
            waits = self._resolve(e, dict(self.latest))
            if waits:
                self.streams[e].append((waits, None, None))

    def emit(self):
        nc = self.nc
        keys = list(self.ENG[1:]) + [("d", q, i) for q in ("sp", "pool", "act") for i in range(NDS)]
        with ExitStack() as st:
            st.enter_context(nc.allow_non_contiguous_dma(reason="small strided sample-path transfers"))
            sems = {}
            for k in keys:
                nm = k if isinstance(k, str) else "d%s%d" % (k[1], k[2])
                sems[k] = st.enter_context(nc.semaphore("s_" + nm))
            block = st.enter_context(nc.Block())
            decos = {"sp": block.sync, "pe": block.tensor, "act": block.scalar,
                     "dve": block.vector, "pool": block.gpsimd}
            for eng in self.ENG:
                stream = self.streams[eng]

                def body(e, stream=stream):
                    for waits, fn, inc in stream:
                        for k, v in waits:
                            e.wait_ge(sems[k], v)
                        if fn is not None:
                            fn(e).then_inc(sems[inc[0]], inc[1])
                decos[eng](body)


class Arena:
    def __init__(self, nc, nbytes):
        self.t = nc.alloc_sbuf_tensor("arena", [128, nbytes], U8)
        self.ap = self.t.ap()
        self.n = nbytes
        self.off = 0

    def alloc(self, free_shape, dt, parts=128):
        esz = 4 if dt == F32 else 2
        ne = int(np.prod(free_shape))
        nb = (ne * esz + 31) // 32 * 32
        assert self.off + nb <= self.n, "SBUF arena overflow %d + %d > %d" % (self.off, nb, self.n)
        a = self.ap[0:parts, self.off:self.off + ne * esz].bitcast(dt)
        self.off += nb
        if len(free_shape) == 2:
            a = a.rearrange("p (a b) -> p a b", a=free_shape[0])
        elif len(free_shape) == 3:
            a = a.rearrange("p (a b c) -> p a b c", a=free_shape[0], b=free_shape[1])
        return V(Tl(), a)


class Ctx:
    pass


def dram_in(nc, name, shape, dt=F32):
    return V(None, nc.dram_tensor(name, list(shape), dt, kind="ExternalInput").ap())


def dram_out(nc, name, shape, dt=F32):
    return V(None, nc.dram_tensor(name, list(shape), dt, kind="ExternalOutput").ap())


def dram_tmp(nc, name, shape, dt=F32):
    return V(None, nc.dram_tensor(name, list(shape), dt, kind="Internal").ap())


def mm(P, out, lhsT, rhs, start=True, stop=True):
    o, l, r = out.ap, lhsT.ap, rhs.ap
    P.op("pe", lambda e: e.matmul(o, l, r, start=start, stop=stop), [lhsT, rhs], [out])


def tr(P, out, in_, ident):
    o, i, d = out.ap, in_.ap, ident.ap
    P.op("pe", lambda e: e.transpose(o, i, d), [in_, ident], [out])


def act(P, out, in_, func, bias=0.0, scale=1.0, eng="act"):
    o, i = out.ap, in_.ap
    rd = [in_]
    b = bias
    if isinstance(bias, V):
        rd.append(bias)
        b = bias.ap
    s = scale
    if isinstance(scale, V):
        rd.append(scale)
        s = scale.ap
    P.op("act", lambda e: e.activation(o, i, func, bias=b, scale=s), rd, [out])


def cp(P, eng, out, in_):
    o, i = out.ap, in_.ap
    if eng == "act":
        P.op("act", lambda e: e.copy(o, i), [in_], [out])
    else:
        P.op(eng, lambda e: e.tensor_copy(o, i), [in_], [out])


def tt(P, eng, out, in0, in1, op):
    o, a, b = out.ap, in0.ap, in1.ap
    P.op(eng, lambda e: e.tensor_tensor(o, a, b, op), [in0, in1], [out])


def ts(P, eng, out, in0, s1, op0, s2=None, op1=None):
    o, a = out.ap, in0.ap
    rd = [in0]
    x1 = s1
    if isinstance(s1, V):
        rd.append(s1)
        x1 = s1.ap
    x2 = s2
    if isinstance(s2, V):
        rd.append(s2)
        x2 = s2.ap
    if op1 is None:
        P.op(eng, lambda e: e.tensor_scalar(o, a, x1, None, op0), rd, [out])
    else:
        P.op(eng, lambda e: e.tensor_scalar(o, a, x1, x2, op0, op1), rd, [out])


def stt(P, eng, out, in0, scalar, in1, op0, op1):
    o, a, b = out.ap, in0.ap, in1.ap
    rd = [in0, in1]
    s = scalar
    if isinstance(scalar, V):
        rd.append(scalar)
        s = scalar.ap
    P.op(eng, lambda e: e.scalar_tensor_tensor(o, a, s, b, op0, op1), rd, [out])


def memset(P, eng, out, val):
    o = out.ap
    P.op(eng, lambda e: e.memset(o, val), [], [out])


def rsum(P, eng, out, in_):
    o, i = out.ap, in_.ap
    P.op(eng, lambda e: e.reduce_sum(o, i, AX.X), [in_], [out])


def recip(P, out, in_):
    o, i = out.ap, in_.ap
    P.op("dve", lambda e: e.reciprocal(o, i), [in_], [out])


def rsqrt(P, out, in_, eps, scale=1.0):
    act(P, out, in_, AF.Ln, bias=eps, scale=scale)
    act(P, out, out, AF.Exp, scale=-0.5)


class Rot:
    def __init__(self, items):
        self.items = items
        self.i = 0

    def next(self):
        v = self.items[self.i % len(self.items)]
        self.i += 1
        return v


def build(phases="ABCD"):
    nc = bass.Bass("TRN2", target_bir_lowering=False)
    P = Prog(nc)
    C = Ctx()
    C.nc, C.P = nc, P
    I = {}
    I["x_p"] = dram_in(nc, "x_p", [NTP, DM])
    I["x_s"] = dram_in(nc, "x_s", [NS, DM])
    I["state_gdn"] = dram_in(nc, "state_gdn", [NS, 8, 128, 128])
    I["state_conv"] = dram_in(nc, "state_conv", [NS, 3, 3072])
    for g, w in enumerate((128, 512, 2048)):
        I["ck%d" % g] = dram_in(nc, "ck%d" % g, [NS, w, 512])
        I["cv%d" % g] = dram_in(nc, "cv%d" % g, [NS, w, 512])
    I["rel_table"] = dram_in(nc, "rel_table", [32, 12])
    I["g_pre"] = dram_in(nc, "g_pre", [1, DM])
    I["w_in"] = dram_in(nc, "w_in", [DM, INW])
    I["conv_w"] = dram_in(nc, "conv_w", [3072, 4])
    I["a_log"] = dram_in(nc, "a_log", [1, 8])
    I["dt_bias"] = dram_in(nc, "dt_bias", [1, 8])
    I["g_head_norm"] = dram_in(nc, "g_head_norm", [1, 128])
    I["w_proj_a"] = dram_in(nc, "w_proj_a", [1024, DM])
    I["w_proj_b"] = dram_in(nc, "w_proj_b", [512, DM])
    I["w_out"] = dram_in(nc, "w_out", [DM, DM])
    I["g_post"] = dram_in(nc, "g_post", [1, DM])
    I["c_ident"] = dram_in(nc, "c_ident", [128, 128])
    I["c_umat"] = dram_in(nc, "c_umat", [128, 128])
    I["c_maskT"] = dram_in(nc, "c_maskT", [128, 128])
    I["c_strictT"] = dram_in(nc, "c_strictT", [128, 128])
    I["c_oh"] = dram_in(nc, "c_oh", [3, 32, 384])
    I["c_negm"] = dram_in(nc, "c_negm", [128, 384])
    I["c_ohs"] = dram_in(nc, "c_ohs", [3, 32, 128])
    I["c_oh0"] = dram_in(nc, "c_oh0", [3, 32, 1])
    O = {}
    O["y_p"] = dram_out(nc, "y_p", [NTP, DM])
    O["y_s"] = dram_out(nc, "y_s", [NS, DM])
    O["p_gdn"] = dram_out(nc, "p_gdn", [NSEQ, 8, 128, 128])
    O["p_conv"] = dram_out(nc, "p_conv", [NSEQ, 3, 3072])
    for g, w in enumerate((128, 512, 2048)):
        O["p_k%d" % g] = dram_out(nc, "p_k%d" % g, [NSEQ, w, 512])
        O["p_v%d" % g] = dram_out(nc, "p_v%d" % g, [NSEQ, w, 512])
        O["s_k%d" % g] = dram_out(nc, "s_k%d" % g, [NS, 512])
        O["s_v%d" % g] = dram_out(nc, "s_v%d" % g, [NS, 512])
    O["s_gdn"] = dram_out(nc, "s_gdn", [NS, 8, 128, 128])
    O["s_conv"] = dram_out(nc, "s_conv", [NS, 3, 3072])
    S = {}
    S["qT"] = dram_tmp(nc, "s_qT", [1024, NT])
    S["kT"] = dram_tmp(nc, "s_kT", [1024, NT])
    S["vT"] = dram_tmp(nc, "s_vT", [1024, NT])
    S["zaT"] = dram_tmp(nc, "s_zaT", [1024, NT])
    S["zbT"] = dram_tmp(nc, "s_zbT", [512, NT])
    S["gaT"] = dram_tmp(nc, "s_gaT", [1024, NT])
    S["gbT"] = dram_tmp(nc, "s_gbT", [1024, NT])
    S["gbeta"] = dram_tmp(nc, "s_gbeta", [NT, 16])
    for g in range(3):
        S["qb%d" % g] = dram_tmp(nc, "s_qb%d" % g, [512, NTP], BF16)
        S["kb%d" % g] = dram_tmp(nc, "s_kb%d" % g, [512, NTP], BF16)
        S["vb%d" % g] = dram_tmp(nc, "s_vb%d" % g, [NTP, 512], BF16)
    S["vs"] = dram_tmp(nc, "s_vs", [NS, 3, 512])
    S["yaT"] = dram_tmp(nc, "s_yaT", [1024, NT], BF16)
    S["ybT"] = dram_tmp(nc, "s_ybT", [512, NT], BF16)
    S["bias"] = dram_tmp(nc, "s_bias", [12, 256, 384])
    C.I, C.O, C.S = I, O, S

    A = Arena(nc, 212480)
    C.A = A
    pst = nc.alloc_psum_tensor("psum", [128, 8, 512], F32)
    psa = pst.ap()
    C.banks = Rot([V(Tl(excl=True), psa[:, b, :]) for b in range(8)])

    K = Ctx()
    C.K = K
    K.ident = A.alloc([128], F32)
    K.identb = A.alloc([128], BF16)
    K.umat = A.alloc([128], F32)
    K.maskT = A.alloc([128], F32)
    K.strictT = A.alloc([128], F32)
    K.ones = A.alloc([128], F32)
    K.onesb = A.alloc([128], BF16)
    K.meanm = A.alloc([128], F32)
    K.c128 = A.alloc([128], F32)
    K.qs = A.alloc([12, NS], F32)
    K.ks = A.alloc([12, NS], F32)
    P.dma("sp", K.ident, I["c_ident"])
    P.dma("pool", K.identb, I["c_ident"])
    P.dma("sp", K.umat, I["c_umat"])
    P.dma("sp", K.maskT, I["c_maskT"])
    P.dma("sp", K.strictT, I["c_strictT"])
    memset(P, "pool", K.ones, 1.0)
    memset(P, "pool", K.onesb, 1.0)
    memset(P, "pool", K.meanm, 1.0 / 128.0)
    memset(P, "pool", K.c128, 128.0)
    C.mark0 = A.off

    if "A" in phases:
        phase_a(C)
    P.barrier()
    A.off = C.mark0
    if "B" in phases:
        phase_b(C)
    P.barrier()
    A.off = C.mark0
    if "C" in phases:
        phase_c(C)
    P.barrier()
    A.off = C.mark0
    if "D" in phases:
        phase_d(C)
    P.barrier()
    P.emit()
    return nc


def bcast_rows(v, n):
    return v.raw([[0, 128], [1, n]])


def phase_a(C):
    P, A, I, O, S, K = C.P, C.A, C.I, C.O, C.S, C.K
    banks = C.banks
    xnT = A.alloc([8, NT], BF16)
    gpre = A.alloc([DM], F32)
    convw = A.alloc([24, 4], F32)
    P.dma("sp", gpre, bcast_rows(I["g_pre"], DM))
    P.dma("sp", convw, I["conv_w"].re("(c p) w -> p c w", p=128))

    stT = A.alloc([24, 3, NS], F32)
    mark1 = A.off
    xts = Rot([A.alloc([DM], F32) for _ in range(4)])
    xpre = {}
    sqs = Rot([A.alloc([DM], F32) for _ in range(2)])
    xns = Rot([A.alloc([DM], BF16) for _ in range(2)])
    sts = Rot([A.alloc([4], F32) for _ in range(4)])
    DBG = int(os.environ.get("MK_DBG", "9"))
    for sub in (range(NTP // 128 + 1) if DBG >= 2 else []):
        if sub < NTP // 128:
            np_, src, t0 = 128, I["x_p"][sub * 128:(sub + 1) * 128, :], sub * 128
        else:
            np_, src, t0 = NS, I["x_s"], NTP
        if sub == 0:
            for pf in range(2):
                xpre[pf] = xts.next()
                P.dma("sp", xpre[pf], I["x_p"][pf * 128:(pf + 1) * 128, :])
        xt = xpre.pop(sub)[0:np_]
        nsb = sub + 2
        if nsb <= NTP // 128:
            xpre[nsb] = xts.next()
            if nsb < NTP // 128:
                P.dma("sp", xpre[nsb], I["x_p"][nsb * 128:(nsb + 1) * 128, :])
            else:
                P.dma("sp", xpre[nsb][0:NS], I["x_s"])
        sq = sqs.next()[0:np_]
        xn = xns.next()[0:np_]
        stt_ = sts.next()[0:np_]
        act(P, sq, xt, AF.Square)
        rsum(P, "dve", stt_[:, 0:1], sq)
        rsqrt(P, stt_[:, 2:3], stt_[:, 0:1], EPS, 1.0 / DM)
        stt(P, "dve", xn, xt, stt_[:, 2:3], gpre[0:np_], ALU.mult, ALU.mult)
        bk = banks.next()
        bkb = bk.bitcast(BF16)
        for kc in range(8):
            tr(P, bkb[:, kc * 128:kc * 128 + np_], xn[:, kc * 128:(kc + 1) * 128], K.identb[0:np_, 0:np_])
        src_ps = bkb.re("p (k t) -> p k t", k=8)[:, :, 0:np_]
        cp(P, "act" if sub % 2 else "dve", xnT[:, :, t0:t0 + np_], src_ps)
    stin = A.alloc([3072], F32)
    P.dma("sp", stin[0:48], I["state_conv"].re("b r c -> (b r) c"))
    for c4 in range(6 if DBG >= 3 else 0):
        bk = banks.next()
        for m in range(4):
            c = c4 * 4 + m
            tr(P, bk[:, m * 48:(m + 1) * 48], stin[0:48, c * 128:(c + 1) * 128], K.ident[0:48, 0:48])
        cp(P, "dve", stT[:, c4 * 4:(c4 + 1) * 4, :, :].re("p c r b -> p c b r"),
           bk[:, 0:192].re("p (c b r) -> p c b r", c=4, b=NS))
    P.barrier()
    A.off = mark1

    wbs = Rot([A.alloc([8, 512], BF16) for _ in range(2)])
    w_view = I["w_in"].re("(k p) n -> p k n", p=128)

    WSEQ = [(j * 512, 512) for j in range(6)]
    WSEQ += [(O_ZA, 512), (O_ZA + 512, 512), (O_ZB, 512), (O_GA, 512), (O_GA + 512, 512), (O_GB, 512), (O_GB + 512, 512)]
    WSEQ += [(O_A, 16)]
    for g_ in range(3):
        WSEQ += [(O_QKVB + g_ * 1536, 512), (O_QKVB + g_ * 1536 + 512, 512), (O_QKVB + g_ * 1536 + 1024, 512)]
    wq = {"i": 0, "pend": None}

    def issue_w(i):
        col0, width = WSEQ[i]
        wb = wbs.next()
        P.dma("pool", wb[:, :, 0:width], w_view[:, :, col0:col0 + width], acc=False)
        return wb

    def load_w(col0, width):
        i = wq["i"]
        assert WSEQ[i] == (col0, width), (WSEQ[i], col0, width)
        wb = wq["pend"] if wq["pend"] is not None else issue_w(i)
        wq["pend"] = issue_w(i + 1) if i + 1 < len(WSEQ) else None
        wq["i"] = i + 1
        return wb

    def fm(bank, wb, m, t0, n):
        for kc in range(8):
            mm(P, bank[:, 0:n], wb[:, kc, m * 128:(m + 1) * 128], xnT[:, kc, t0:t0 + n], kc == 0, kc == 7)

    def tm(bank, wb, tok, np_, width=512):
        for kc in range(8):
            mm(P, bank[0:np_, 0:width], tok(kc), wb[:, kc, 0:width], kc == 0, kc == 7)

    TILES = [(ti * 512, 512) for ti in range(NTP // 512)] + [(NTP, NS)]
    evi = [0]

    def evac_eng():
        evi[0] += 1
        return "act" if evi[0] % 2 else "dve"

    osb = Rot([A.alloc([512], F32) for _ in range(3)])
    tmb = Rot([A.alloc([512], F32) for _ in range(2)])
    mark2 = A.off
    stage = Rot([A.alloc([515], F32) for _ in range(3)])
    cvs = Rot([A.alloc([512], F32) for _ in range(2)])
    svs = Rot([A.alloc([512], F32) for _ in range(5)])
    sqq = Rot([A.alloc([512], F32) for _ in range(4)])
    rss = Rot([A.alloc([512], F32) for _ in range(2)])
    pending = []
    DEFER = int(os.environ.get('MK_DEFER', '1'))
    if DBG >= 4:
        P.dma("sp", O["s_conv"][:, 0:2, :], I["state_conv"][:, 1:3, :])

    SUB = os.environ.get("MK_SUB", "conv,simple,ab,att").split(",")
    for j in range(6 if "conv" in SUB else 0):
        wb = load_w(j * 512, 512)
        for m in range(4):
            c = j * 4 + m
            kind = "q" if c < 8 else ("k" if c < 16 else "v")
            dst = S["qT"] if c < 8 else (S["kT"] if c < 16 else S["vT"])
            r0 = (c % 8) * 128
            prevbox = [None]

            def s1a(t0, n):
                bk = banks.next()
                fm(bk, wb, m, t0, n)
                sg = stage.next()
                if n == 512:
                    cp(P, "act", sg[:, 3:515], bk[:, 0:512])
                    if t0 % L == 0:
                        memset(P, "pool", sg[:, 0:3], 0.0)
                    else:
                        cp(P, "pool", sg[:, 0:3], prevbox[0][:, 512:515])
                    prevbox[0] = sg
                    taps = [sg[:, w:w + 512] for w in range(4)]
                else:
                    cp(P, "act", sg[:, 0:n], bk[:, 0:n])
                    taps = [stT[:, c, 0, :], stT[:, c, 1, :], stT[:, c, 2, :], sg[:, 0:n]]
                return taps, t0, n

            def s1b(taps, t0, n):
                cv = cvs.next()[:, 0:n]
                ts(P, "dve", cv, taps[0], convw[:, c, 0:1], ALU.mult)
                for w in range(1, 4):
                    stt(P, "dve", cv, taps[w], convw[:, c, w:w + 1], cv, ALU.mult, ALU.add)
                sv = svs.next()[:, 0:n]
                act(P, sv, cv, AF.Silu)
                if kind == "v":
                    P.dma("sp", dst[r0:r0 + 128, t0:t0 + n], sv)
                else:
                    sq = sqq.next()[:, 0:n]
                    tt(P, "pool", sq, sv, sv, ALU.mult)

                    def stage2(sq=sq, sv=sv, n=n, kind=kind, dst=dst, r0=r0, t0=t0):
                        b2 = banks.next()
                        mm(P, b2[:, 0:n], K.c128 if kind == "q" else K.ones, sq)
                        rs = rss.next()[:, 0:n]
                        yield
                        act(P, rs, b2[:, 0:n], AF.Ln, bias=EPS * (128.0 if kind == "q" else 1.0))
                        yield
                        act(P, rs, rs, AF.Exp, scale=-0.5)
                        yield
                        ob = osb.next()[:, 0:n]
                        tt(P, "pool", ob, sv, rs, ALU.mult)
                        P.dma("sp", dst[r0:r0 + 128, t0:t0 + n], ob)
                    pending.append(stage2())
                    if len(pending) >= DEFER + 2:
                        run_interleaved(pending[0:2])
                        del pending[0:2]
            nxt_info = s1a(*TILES[0])
            for ti in range(len(TILES)):
                info = nxt_info
                if ti + 1 < len(TILES):
                    nxt_info = s1a(*TILES[ti + 1])
                s1b(*info)
        run_interleaved(pending)
        del pending[:]
        for s in range(NSEQ):
            bk = banks.next()
            tm(bk, wb, lambda kc, s=s: xnT[:, kc, s * L + L - 128:s * L + L], 128)
            tb = tmb.next()
            cp(P, evac_eng(), tb, bk)
            P.dma("sp", O["p_conv"][s, :, j * 512:(j + 1) * 512], tb[125:128, :])
        bk = banks.next()
        tm(bk, wb, lambda kc: xnT[:, kc, NTP:NT], NS)
        tb = tmb.next()
        cp(P, evac_eng(), tb[0:NS], bk[0:NS])
        P.dma("sp", O["s_conv"][:, 2, j * 512:(j + 1) * 512], tb[0:NS, :])

    P.barrier()
    A.off = mark2
    def simple_block(col0, dst, r0, func):
        wb = load_w(col0, 512)
        for m in range(4):
            for (t0, n) in TILES:
                bk = banks.next()
                fm(bk, wb, m, t0, n)
                ob = osb.next()[:, 0:n]
                act(P, ob, bk[:, 0:n], func)
                P.dma("sp", dst[r0 + m * 128:r0 + (m + 1) * 128, t0:t0 + n], ob)

    if "simple" in SUB:
        for j in range(2):
            simple_block(O_ZA + j * 512, S["zaT"], j * 512, AF.Silu)
        simple_block(O_ZB, S["zbT"], 0, AF.Silu)
        for j in range(2):
            simple_block(O_GA + j * 512, S["gaT"], j * 512, AF.Sigmoid)
        for j in range(2):
            simple_block(O_GB + j * 512, S["gbT"], j * 512, AF.Sigmoid)

    wb = load_w(O_A, 16)
    dtb = A.alloc([4, 8], F32)
    nega = A.alloc([4, 8], F32)
    for q in range(4):
        P.dma("sp", dtb[:, q, :], bcast_rows(I["dt_bias"], 8))
        P.dma("sp", nega[:, q, :], bcast_rows(I["a_log"], 8))
    act(P, nega, nega, AF.Exp)
    ts(P, "dve", nega, nega, -1.0, ALU.mult)
    abt = Rot([A.alloc([6, 4, 8], F32) for _ in range(2)])
    gbs = Rot([A.alloc([4, 16], F32) for _ in range(2)])
    for (t0, n) in (TILES if "ab" in SUB else []):
        nsub = 4 if n == 512 else 1
        np_ = 128 if n == 512 else NS
        bk = banks.next()
        for sb in range(nsub):
            for kc in range(8):
                mm(P, bk[0:np_, sb * 16:(sb + 1) * 16], xnT[:, kc, t0 + sb * 128:t0 + sb * 128 + np_],
                   wb[:, kc, 0:16], kc == 0, kc == 7)
        pv = bk[0:np_, 0:nsub * 16].re("p (s c) -> p s c", c=16)
        w_ = abt.next()[0:np_, :, 0:nsub, :]
        gb = gbs.next()[0:np_, 0:nsub, :]
        xx, ax, ee, ll = w_[:, 0], w_[:, 1], w_[:, 2], w_[:, 3]
        tt(P, "dve", xx, pv[:, :, 0:8], dtb[0:np_, 0:nsub, :], ALU.add)
        act(P, ax, xx, AF.Abs)
        act(P, ee, ax, AF.Exp, scale=-1.0)
        act(P, ll, ee, AF.Ln, bias=1.0)
        stt(P, "dve", xx, xx, 0.0, ll, ALU.max, ALU.add)
        tt(P, "dve", gb[:, :, 0:8], xx, nega[0:np_, 0:nsub, :], ALU.mult)
        act(P, gb[:, :, 8:16], pv[:, :, 8:16], AF.Sigmoid)
        if n == 512:
            P.dma("sp", S["gbeta"][t0:t0 + 512, :].re("(s p) c -> p s c", p=128), gb)
        else:
            P.dma("sp", S["gbeta"][t0:t0 + NS, :], gb[:, 0, :])

    stgb = Rot([A.alloc([2048], BF16) for _ in range(2)])
    vbb = Rot([A.alloc([512], BF16) for _ in range(3)])
    GS = [int(x) for x in os.environ.get("MK_G", "0,1,2").split(",")]
    for g, dil in (enumerate(DILS) if "att" in SUB else []):
        if g not in GS:
            continue
        lc = L // dil
        spc = 2048 // dil
        ATT = os.environ.get("MK_ATT", "qk,ktm,v").split(",")
        for t, nm in (((0, "qb"), (1, "kb")) if "qk" in ATT else []):
            wb = load_w(O_QKVB + g * 1536 + t * 512, 512)
            dstT = S["%s%d" % (nm, g)]
            for h in range(4):
                for spn in range(NTP // 2048):
                    sgb = stgb.next()
                    for sb in range(4):
                        bk = banks.next()
                        fm(bk, wb, h, spn * 2048 + sb * 512, 512)
                        i0 = sb * 512 // dil
                        cp(P, evac_eng(), sgb.re("p (r i) -> p r i", r=dil)[:, :, i0:i0 + 512 // dil],
                           bk.re("p (i r) -> p r i", r=dil))
                    s_, n_ = spn // 2, spn % 2
                    d = dstT[h * 128:(h + 1) * 128, s_ * L:(s_ + 1) * L].re("p (r i) -> p r i", r=dil)
                    P.dma("sp", d[:, :, n_ * spc:(n_ + 1) * spc], sgb.re("p (r i) -> p r i", r=dil))
                bk = banks.next()
                fm(bk, wb, h, NTP, NS)
                cp(P, evac_eng(), (K.qs if t == 0 else K.ks)[:, g * 4 + h, :], bk[:, 0:NS])
            if t == 1 and "ktm" in ATT:
                for s in range(NSEQ):
                    for r in range(dil):
                        base = s * L + L - 128 * dil + r
                        bk = banks.next()
                        tm(bk, wb, lambda kc, base=base: xnT[:, kc, base:base + 128 * dil:dil], 128)
                        tb = tmb.next()
                        cp(P, evac_eng(), tb, bk)
                        P.dma("sp", O["p_k%d" % g][s, r::dil, :], tb)
                bk = banks.next()
                tm(bk, wb, lambda kc: xnT[:, kc, NTP:NT], NS)
                tb = tmb.next()
                cp(P, evac_eng(), tb[0:NS], bk[0:NS])
                P.dma("sp", O["s_k%d" % g], tb[0:NS])
        if "v" not in ATT:
            continue
        wb = load_w(O_QKVB + g * 1536 + 1024, 512)
        nb = lc // 128
        MKV = os.environ.get("MK_V", "main,pv,samp").split(",")
        for s in range(NSEQ if "main" in MKV else 0):
            for r in range(dil):
                for n in range(nb):
                    base = s * L + n * 128 * dil + r
                    bk = banks.next()
                    tm(bk, wb, lambda kc, base=base: xnT[:, kc, base:base + 128 * dil:dil], 128)
                    vb = vbb.next()
                    cp(P, evac_eng(), vb, bk)
                    row0 = s * L + r * lc + n * 128
                    P.dma("sp", S["vb%d" % g][row0:row0 + 128, :], vb)
                    if n == nb - 1 and "pv" in MKV:
                        tb = tmb.next()
                        cp(P, evac_eng(), tb, bk)
                        P.dma("sp", O["p_v%d" % g][s, r::dil, :], tb)
        if "samp" not in MKV:
            continue
        bk = banks.next()
        tm(bk, wb, lambda kc: xnT[:, kc, NTP:NT], NS)
        tb = tmb.next()
        cp(P, evac_eng(), tb[0:NS], bk[0:NS])
        P.dma("sp", O["s_v%d" % g], tb[0:NS])
        P.dma("sp", S["vs"][:, g, :], tb[0:NS])


def bc(v, dims):
    p = v.ap.ap[0]
    return v.raw([[p[0], p[1]]] + dims)


def gdn_stream(C, T, c, cols, Sst, first_zero, h0=0, NH=8, S_list=None, s_load=None, s_store=None):
    P, K, S = C.P, C.K, C.S
    banks = C.banks
    hs = slice(h0, h0 + NH)
    kT_v = S["kT"].re("(h p) t -> p h t", p=128)[:, hs]
    qT_v = S["qT"].re("(h p) t -> p h t", p=128)[:, hs]
    vT_v = S["vT"].re("(h p) t -> p h t", p=128)[:, hs]
    za_v = S["zaT"].re("(h p) t -> p h t", p=128)[:, hs]
    ya_v = S["yaT"].re("(h p) t -> p h t", p=128)[:, hs]
    ghn = T["ghn"]
    HG = [(hb, min(4, NH - hb)) for hb in range(0, NH, 4)]
    for ci, col0 in enumerate(cols):
        def loads(cj):
            cl = cols[cj]
            kq_ = T["kq"][cj % 2][:, :, :, 0:c]
            P.dma("sp", kq_[:, :, 0, :], kT_v[:, :, cl:cl + c])
            P.dma("sp", kq_[:, :, 1, :], qT_v[:, :, cl:cl + c])
            P.dma("sp", T["vT"][cj % 2][:, :, 0:c], vT_v[:, :, cl:cl + c])
            P.dma("sp", T["gb"][cj % 2][0:c], S["gbeta"][cl:cl + c, :])
            P.dma("sp", T["za"][cj % 2][:, :, 0:c], za_v[:, :, cl:cl + c])
        if ci == 0:
            loads(0)
            if s_load:
                s_load(0, S_list[0])
        if ci + 1 < len(cols):
            loads(ci + 1)
            if s_load:
                s_load(ci + 1, S_list[(ci + 1) % 2])
        if S_list is not None:
            Sst = S_list[ci % 2]
        kq = T["kq"][ci % 2][:, :, :, 0:c]
        vT = T["vT"][ci % 2][:, :, 0:c]
        za = T["za"][ci % 2][:, :, 0:c]
        gb = T["gb"][ci % 2][0:c]
        gg, bb = gb[:, h0:h0 + NH], gb[:, 8 + h0:8 + h0 + NH]
        sm = T["sm"]
        Gcol, GL, kdecs, glast, nbeta = (sm[:, i * NH:(i + 1) * NH] for i in range(5))
        t = [x[0:c, :, 0:c] for x in T["t"]]
        tf = [x[:, :, 0:c] for x in T["t"]]
        td = [x[0:c] for x in T["t"]]
        bk = banks.next()
        mm(P, bk[0:c, 0:NH], K.umat[0:c, 0:c], gg)
        mm(P, bk[:, 8:8 + NH], K.ones[0:c, :], gg)
        cp(P, "dve", Gcol[0:c], bk[0:c, 0:NH])
        cp(P, "dve", GL, bk[:, 8:8 + NH])
        tt(P, "pool", kdecs[0:c], GL[0:c], Gcol[0:c], ALU.subtract)
        act(P, kdecs[0:c], kdecs[0:c], AF.Exp)
        act(P, glast, GL, AF.Exp)
        ts(P, "pool", nbeta[0:c], bb, -1.0, ALU.mult)
        yield
        Ug = t[0]
        tt(P, "dve", Ug, bc(K.umat[0:c, 0:c], [[0, NH], [1, c]]), bc(gg, [[1, NH], [0, c]]), ALU.mult)
        EGb = tf[2]
        dT = t[1]
        for hb, nh in HG:
            bX = banks.next()
            mm(P, bX[:, 0:nh * c].re("p (h i) -> p h i", h=nh), K.ones[0:c, :], Ug[:, hb:hb + nh, :])
            act(P, EGb[:, hb:hb + nh, :], bX[:, 0:nh * c].re("p (h i) -> p h i", h=nh), AF.Exp)
            for h in range(hb, hb + nh):
                stt(P, "dve", dT[:, h, :], bX[0:c, (h - hb) * c:(h - hb + 1) * c], Gcol[0:c, h:h + 1],
                    K.maskT[0:c, 0:c], ALU.subtract, ALU.add)
        decT = t[3]
        act(P, decT, dT, AF.Exp)
        DSb = t[4]
        tt(P, "pool", DSb, decT, bc(K.strictT[0:c, 0:c], [[0, NH], [1, c]]), ALU.mult)
        tt(P, "pool", DSb, DSb, bc(nbeta[0:c], [[1, NH], [0, c]]), ALU.mult)
        kqd = T["kqd"][:, :, :, 0:c]
        tt(P, "pool", kqd, kq, bc(EGb, [[EGb.ap.ap[1][0], NH], [0, 2], [1, c]]), ALU.mult)
        yield
        vtok, kdec = td[5], td[6]
        for src, dst, scale in ((vT, vtok, None), (kq[:, :, 0, :], kdec, True)):
            for hb, nh in HG:
                bk = banks.next()
                for h in range(hb, hb + nh):
                    tr(P, bk[0:c, (h - hb) * 128:(h - hb + 1) * 128], src[:, h, :], K.ident)
                pv = bk[0:c, 0:nh * 128].re("p (h d) -> p h d", h=nh)
                if scale is None:
                    cp(P, "act", dst[:, hb:hb + nh, :], pv)
                else:
                    tt(P, "dve", dst[:, hb:hb + nh, :], pv, bc(kdecs[0:c, hb:hb + nh], [[1, nh], [0, 128]]), ALU.mult)
        yield
        pa, pb = T["pa"][0:c, :, :, 0:c], T["pb"][0:c, :, :, 0:c]
        qkT = t[7]
        for h2 in range(NH // 2):
            bk = banks.next()
            for hh in range(2):
                h = h2 * 2 + hh
                mm(P, bk[0:c, hh * 2 * c:(hh + 1) * 2 * c].re("p (k i) -> p k i", k=2), kq[:, h, 0, :], kq[:, h, :, :])
            pv = bk[0:c, 0:4 * c].re("p (h k i) -> p h k i", h=2, k=2)
            tt(P, "dve", pa[:, h2 * 2:h2 * 2 + 2, 0, :], pv[:, :, 0, :], DSb[:, h2 * 2:h2 * 2 + 2, :], ALU.mult)
            tt(P, "dve", qkT[:, h2 * 2:h2 * 2 + 2, :], pv[:, :, 1, :], decT[:, h2 * 2:h2 * 2 + 2, :], ALU.mult)
        yield
        Pm = t[8]
        if c > 1:
            for hb, nh in HG:
                bk = banks.next()
                for h in range(hb, hb + nh):
                    tr(P, bk[0:c, (h - hb) * c:(h - hb + 1) * c], pa[:, h, 0, :], K.ident[0:c, 0:c])
                cp(P, "act", pa[:, hb:hb + nh, 1, :], bk[0:c, 0:nh * c].re("p (h i) -> p h i", h=nh))
            tt(P, "pool", Pm, pa[:, :, 0, :], bc(K.ident[0:c, 0:c], [[0, NH], [1, c]]), ALU.add)
            yield
            cur, nxt = pa, pb
            for lvl in range(1, 7):
                for h2 in range(NH // 2):
                    bk = banks.next()
                    for hh in range(2):
                        h = h2 * 2 + hh
                        if lvl < 6:
                            mm(P, bk[0:c, hh * 2 * c:hh * 2 * c + c], cur[:, h, 1, :], cur[:, h, 0, :])
                        mm(P, bk[0:c, hh * 2 * c + c:(hh + 1) * 2 * c], cur[:, h, 0, :], cur[:, h, 1, :])
                    pv = bk[0:c, 0:4 * c].re("p (h k i) -> p h k i", h=2, k=2)
                    if lvl < 6:
                        cp(P, "act" if h2 % 2 else "dve", nxt[:, h2 * 2:h2 * 2 + 2, :, :], pv)
                    else:
                        cp(P, "act" if h2 % 2 else "dve", nxt[:, h2 * 2:h2 * 2 + 2, 1, :], pv[:, :, 1, :])
                yield
                for hb, nh in HG:
                    bk = banks.next()
                    for h in range(hb, hb + nh):
                        mm(P, bk[0:c, (h - hb) * c:(h - hb + 1) * c], nxt[:, h, 1, :], Pm[:, h, :])
                    tt(P, "dve", Pm[:, hb:hb + nh, :], Pm[:, hb:hb + nh, :],
                       bk[0:c, 0:nh * c].re("p (h i) -> p h i", h=nh), ALU.add)
                yield
                cur, nxt = nxt, cur
        else:
            memset(P, "pool", Pm, 1.0)
        if ci == 0 and first_zero:
            memset(P, "pool", Sst, 0.0)
        R = td[0]
        for hb, nh in HG:
            bk = banks.next()
            for h in range(hb, hb + nh):
                mm(P, bk[0:c, (h - hb) * 128:(h - hb + 1) * 128], kqd[:, h, 0, :], Sst[:, h, :])
            tt(P, "dve", R[:, hb:hb + nh, :], vtok[:, hb:hb + nh, :],
               bk[0:c, 0:nh * 128].re("p (h d) -> p h d", h=nh), ALU.subtract)
        yield
        vn = td[1]
        for hb, nh in HG:
            bk = banks.next()
            for h in range(hb, hb + nh):
                mm(P, bk[0:c, (h - hb) * 128:(h - hb + 1) * 128], Pm[:, h, :], R[:, h, :])
            tt(P, "dve", vn[:, hb:hb + nh, :], bk[0:c, 0:nh * 128].re("p (h d) -> p h d", h=nh),
               bc(bb[:, hb:hb + nh], [[1, nh], [0, 128]]), ALU.mult)
        yield
        oT = tf[5]
        for hb, nh in HG:
            bk = banks.next()
            for h in range(hb, hb + nh):
                o_ = bk[:, (h - hb) * c:(h - hb + 1) * c]
                mm(P, o_, Sst[:, h, :], kqd[:, h, 1, :], True, False)
                mm(P, o_, vn[:, h, :], qkT[:, h, :], False, True)
            cp(P, "act", oT[:, hb:hb + nh, :], bk[:, 0:nh * c].re("p (h i) -> p h i", h=nh))
        yield
        bks = []
        for hb, nh in HG:
            bk = banks.next()
            bks.append(bk)
            for h in range(hb, hb + nh):
                mm(P, bk[:, (h - hb) * 128:(h - hb + 1) * 128], kdec[:, h, :], vn[:, h, :])
        tt(P, "pool", Sst, Sst, bc(glast, [[1, NH], [0, 128]]), ALU.mult)
        for (hb, nh), bk in zip(HG, bks):
            tt(P, "dve", Sst[:, hb:hb + nh, :], Sst[:, hb:hb + nh, :],
               bk[:, 0:nh * 128].re("p (h d) -> p h d", h=nh), ALU.add)
        yield
        sq = tf[2]
        tt(P, "pool", sq, oT, oT, ALU.mult)
        rs = tf[3]
        for hb, nh in HG:
            bk = banks.next()
            mm(P, bk[:, 0:nh * c].re("p (h i) -> p h i", h=nh), K.meanm, sq[:, hb:hb + nh, :])
            rsqrt(P, rs[:, hb:hb + nh, :], bk[:, 0:nh * c].re("p (h i) -> p h i", h=nh), EPS)
        y1 = tf[4]
        stt(P, "dve", y1, oT, ghn[:, 0:1], rs, ALU.mult, ALU.mult)
        yb = T["yb"][:, :, 0:c]
        tt(P, "pool", yb, y1, za, ALU.mult)
        P.dma("sp", ya_v[:, :, col0:col0 + c], yb)
        if s_store:
            s_store(ci, Sst)
        yield


def gdn_tiles(C, NH, cw=128):
    A = C.A
    T = {}
    T["kq"] = [A.alloc([NH, 2, cw], F32) for _ in range(2)]
    T["vT"] = [A.alloc([NH, cw], F32) for _ in range(2)]
    T["gb"] = [A.alloc([16], F32) for _ in range(2)]
    T["za"] = [A.alloc([NH, cw], F32) for _ in range(2)]
    T["sm"] = A.alloc([5 * NH], F32)
    T["t"] = [A.alloc([NH, 128], F32) for _ in range(9)]
    T["kqd"] = A.alloc([NH, 2, cw], F32)
    T["pa"] = A.alloc([NH, 2, cw], F32)
    T["pb"] = A.alloc([NH, 2, cw], F32)
    T["yb"] = A.alloc([NH, max(cw, 2)], BF16)
    T["S"] = A.alloc([NH, 128], F32)
    return T


def run_interleaved(gens, offsets=None):
    gens = list(gens)
    offsets = list(offsets) if offsets else [0] * len(gens)
    rnd = 0
    live = list(range(len(gens)))
    while live:
        nxt_live = []
        for k in live:
            if rnd < offsets[k]:
                nxt_live.append(k)
                continue
            try:
                next(gens[k])
                nxt_live.append(k)
            except StopIteration:
                pass
        live = nxt_live
        rnd += 1


def phase_b(C):
    P, A, I, O, S, K = C.P, C.A, C.I, C.O, C.S, C.K
    ghn = A.alloc([1], F32)
    P.dma("sp", ghn, I["g_head_norm"].re("o d -> d o"))
    NH = int(os.environ.get("MK_NH", "4"))
    NG = 8 // NH
    markb = A.off
    Ts = [gdn_tiles(C, NH) for _ in range(2 * NG)]
    for T in Ts:
        T["ghn"] = ghn
    MODE = os.environ.get("MK_B", "prompt,sample").split(",")
    NCH = int(os.environ.get("MK_NCH", str(L // 128)))
    if "prompt" in MODE:
        gens = []
        for s in range(NSEQ):
            for hg in range(NG):
                T = Ts[s * NG + hg]
                gens.append(gdn_stream(C, T, 128, [s * L + ch * 128 for ch in range(NCH)], T["S"], True,
                                       hg * NH, NH))
        STG = int(os.environ.get("MK_STG", "6"))
        run_interleaved(gens, [STG * k for k in range(len(gens))])
        for s in range(NSEQ):
            for hg in range(NG):
                P.dma("sp", O["p_gdn"][s, hg * NH:(hg + 1) * NH].re("h k v -> k h v"), Ts[s * NG + hg]["S"])
    if "sample" in MODE:
        def sample_gen(T, bs, hg):
            for b in bs:
                P.dma("sp", T["S"], I["state_gdn"][b, hg * NH:(hg + 1) * NH].re("h k v -> k h v"), acc=False)
                yield from gdn_stream(C, T, 1, [NTP + b], T["S"], False, hg * NH, NH)
                P.dma("sp", O["s_gdn"][b, hg * NH:(hg + 1) * NH].re("h k v -> k h v"), T["S"])
        P.barrier()
        A.off = markb
        NSTR = 8 // NG
        Tss = [gdn_tiles(C, NH, 1) for _ in range(NSTR * NG)]
        for T in Tss:
            T["ghn"] = ghn
        gens = []
        for par in range(NSTR):
            for hg in range(NG):
                T = Tss[par * NG + hg]
                bs = list(range(par, NS, NSTR))
                S2 = [T["S"], A.alloc([NH, 128], F32)]
                hsl = slice(hg * NH, (hg + 1) * NH)

                def s_load(ci, tile, bs=bs, hsl=hsl):
                    P.dma("sp", tile, I["state_gdn"][bs[ci], hsl].re("h k v -> k h v"), acc=False)

                def s_store(ci, tile, bs=bs, hsl=hsl):
                    P.dma("sp", O["s_gdn"][bs[ci], hsl].re("h k v -> k h v"), tile)
                gens.append(gdn_stream(C, T, 1, [NTP + b for b in bs], None, False, hg * NH, NH,
                                       S_list=S2, s_load=s_load, s_store=s_store))
        run_interleaved(gens)


def phase_c(C):
    P, A, I, O, S, K = C.P, C.A, C.I, C.O, C.S, C.K
    banks = C.banks
    SC = float(128 ** -0.5)
    rel = A.alloc([12], F32)
    P.dma("sp", rel[0:32], I["rel_table"])
    oh = A.alloc([3, 384], F32)
    P.dma("sp", oh[0:32], I["c_oh"].re("g b c -> b g c"))
    ohs = A.alloc([3, 128], F32)
    P.dma("sp", ohs[0:32], I["c_ohs"].re("g b c -> b g c"))
    oh0 = A.alloc([3, 1], F32)
    P.dma("sp", oh0[0:32], I["c_oh0"].re("g b c -> b g c"))
    negm = A.alloc([384], F32)
    P.dma("sp", negm, I["c_negm"])
    bias2 = A.alloc([12, 256], F32)
    biasS = A.alloc([12], F32)
    bias0 = A.alloc([12], F32)
    relb = A.alloc([128], F32)
    vp = Rot([A.alloc([384], F32) for _ in range(2)])
    for g in range(3):
        for h in range(4):
            gh = g * 4 + h
            ts(P, "dve", relb[0:32], K.ones[0:32], rel[0:32, gh:gh + 1], ALU.mult)
            bk = banks.next()
            mm(P, bk[:, 0:384], relb[0:32], oh[0:32, g, :])
            v_ = vp.next()
            tt(P, "dve", v_, bk[:, 0:384], negm, ALU.add)
            P.dma("sp", S["bias"][gh, 0:128, :], v_)
            P.dma("sp", S["bias"][gh, 128:256, :], v_)
        bk = banks.next()
        mm(P, bk[:, 0:4], ohs[0:32, g, :], rel[0:32, g * 4:(g + 1) * 4])
        cp(P, "dve", biasS[:, g * 4:(g + 1) * 4], bk[:, 0:4])
        bk = banks.next()
        mm(P, bk[0:1, 0:4], oh0[0:32, g, :], rel[0:32, g * 4:(g + 1) * 4])
        cp(P, "dve", bias0[0:1, g * 4:(g + 1) * 4], bk[0:1, 0:4])
    P.barrier()
    for gh in range(12):
        base = gh * 256 * 384 + 255
        P.dma("sp", bias2[:, gh, 0:128], S["bias"].raw([[383, 128], [1, 128]], off=base + 128 * 383))
        P.dma("sp", bias2[:, gh, 128:256], S["bias"].raw([[383, 128], [1, 128]], off=base))
    MODE = os.environ.get("MK_C", "prompt,sample").split(",")
    mark = A.off
    if "prompt" in MODE:
        acc2s = Rot([A.alloc([2, L], F32) for _ in range(2)])
        qcs = Rot([A.alloc([L], BF16) for _ in range(3)])
        kcs = Rot([A.alloc([L], BF16) for _ in range(3)])
        vcs = Rot([A.alloc([L // 128, 128], BF16) for _ in range(3)])
        lgs = Rot([A.alloc([256], F32) for _ in range(3)])
        Es = Rot([A.alloc([256], BF16) for _ in range(4)])
        zbt = A.alloc([L], F32)
        ybt = A.alloc([L], BF16)
        evs = Rot([A.alloc([2, 128], F32) for _ in range(3)])
        items = [(s, h, g, r) for s in range(NSEQ) for h in range(4) for g in range(3) for r in range(DILS[g])]

        def c_loads(s, h, g, r):
            lc = L // DILS[g]
            nb = lc // 128
            qc, kc, vc = qcs.next()[:, 0:lc], kcs.next()[:, 0:lc], vcs.next()[:, 0:nb, :]
            c0 = s * L + r * lc
            P.dma("sp", qc, S["qb%d" % g][h * 128:(h + 1) * 128, c0:c0 + lc], acc=False)
            P.dma("sp", kc, S["kb%d" % g][h * 128:(h + 1) * 128, c0:c0 + lc], acc=False)
            P.dma("sp", vc, S["vb%d" % g][c0:c0 + lc, h * 128:(h + 1) * 128].re("(n p) d -> p n d", p=128),
                  acc=False)
            return qc, kc, vc
        pre = {0: c_loads(*items[0])}
        item_i = [0]
        for s in range(NSEQ):
            for h in range(4):
                acc2 = acc2s.next()
                for g, dil in enumerate(DILS):
                    gh = g * 4 + h
                    lc = L // dil
                    nb = lc // 128
                    for r in range(dil):
                        ii = item_i[0]
                        item_i[0] += 1
                        qc, kc, vc = pre.pop(ii)
                        if ii + 1 < len(items):
                            pre[ii + 1] = c_loads(*items[ii + 1])
                        Eprev = None

                        def logits(n):
                            nq = 256 if n < nb - 1 else 128
                            bk = banks.next()
                            mm(P, bk[:, 0:nq], kc[:, n * 128:(n + 1) * 128], qc[:, n * 128:n * 128 + nq])
                            lg = lgs.next()
                            stt(P, "dve", lg[:, 0:nq], bk[:, 0:nq], SC, bias2[:, gh, 0:nq], ALU.mult, ALU.add)
                            E = Es.next()
                            act(P, E[:, 0:nq], lg[:, 0:nq], AF.Exp)
                            return E
                        Enext = logits(0)
                        for n in range(nb):
                            E = Enext
                            if n + 1 < nb:
                                Enext = logits(n + 1)
                            b2 = banks.next()
                            if n > 0:
                                mm(P, b2[:, 0:128], vc[:, n - 1, :], Eprev[:, 128:256], True, False)
                            mm(P, b2[:, 0:128], vc[:, n, :], E[:, 0:128], n == 0, True)
                            if n > 0:
                                mm(P, b2[:, 128:256], K.onesb, Eprev[:, 128:256], True, False)
                            mm(P, b2[:, 128:256], K.onesb, E[:, 0:128], n == 0, True)
                            Eprev = E
                            lo = r + n * 128 * dil
                            dst = acc2[:, :, lo:lo + 127 * dil + 1:dil]
                            src = b2[:, 0:256].re("p (a q) -> p a q", a=2)
                            if g == 0:
                                cp(P, "act", dst, src)
                            else:
                                ev = evs.next()
                                cp(P, "act", ev, src)
                                tt(P, "pool", dst, dst, ev, ALU.add)
                P.dma("sp", zbt, S["zbT"][h * 128:(h + 1) * 128, s * L:(s + 1) * L], acc=False)
                for qd in range(4):
                    sl = slice(qd * 1024, (qd + 1) * 1024)
                    recip(P, acc2[:, 1, sl], acc2[:, 1, sl])
                    tt(P, "pool", acc2[:, 0, sl], acc2[:, 0, sl], acc2[:, 1, sl], ALU.mult)
                    tt(P, "pool", ybt[:, sl], acc2[:, 0, sl], zbt[:, sl], ALU.mult)
                P.dma("sp", S["ybT"][h * 128:(h + 1) * 128, s * L:(s + 1) * L], ybt)
    P.barrier()
    A.off = mark
    if "sample" in MODE:
        Kcs = [Rot([A.alloc([512], F32) for _ in range(2)]) for g in range(3)]
        Vcs = [Rot([A.alloc([512], F32) for _ in range(2)]) for g in range(3)]
        KcT = Rot([A.alloc([4, 128], F32) for _ in range(2)])
        vsb = Rot([A.alloc([3, 512], F32) for _ in range(2)])
        zbs = A.alloc([4, NS], F32)
        ybs = A.alloc([4, NS], BF16)
        sw = Rot([A.alloc([64], F32) for _ in range(2)])
        P.dma("sp", zbs, S["zbT"].re("(h p) t -> p h t", p=128)[:, :, NTP:NT])
        def cs_loads(b):
            kk = [Kcs[g].next() for g in range(3)]
            vv = [Vcs[g].next() for g in range(3)]
            for g, dil in enumerate(DILS):
                P.dma("sp", kk[g], I["ck%d" % g][b, 0::dil, :], acc=False)
                P.dma("sp", vv[g], I["cv%d" % g][b, 0::dil, :], acc=False)
            vs_ = vsb.next()
            P.dma("sp", vs_[0:1], S["vs"][b:b + 1], acc=False)
            return kk, vv, vs_
        nxt_cs = cs_loads(0)
        for b in range(NS):
            kk, vv, vs_ = nxt_cs
            if b + 1 < NS:
                nxt_cs = cs_loads(b + 1)
            w_ = sw.next()
            bL = banks.next()
            for g in range(3):
                bk = banks.next()
                for h in range(4):
                    tr(P, bk[:, h * 128:(h + 1) * 128], kk[g][:, h * 128:(h + 1) * 128], K.ident)
                kt = KcT.next()
                cp(P, "act", kt, bk.re("p (h k) -> p h k", h=4))
                for h in range(4):
                    gh = g * 4 + h
                    mm(P, bL[:, gh:gh + 1], kt[:, h, :], K.qs[:, gh, b:b + 1])
            for gh in range(12):
                mm(P, bL[0:1, 16 + gh:17 + gh], K.ks[:, gh, b:b + 1], K.qs[:, gh, b:b + 1])
            lgS, ES, lg0, E0 = w_[:, 0:12], w_[:, 12:24], w_[0:1, 24:36], w_[0:1, 36:48]
            stt(P, "dve", lgS, bL[:, 0:12], SC, biasS, ALU.mult, ALU.add)
            act(P, ES, lgS, AF.Exp)
            stt(P, "dve", lg0, bL[0:1, 16:28], SC, bias0[0:1], ALU.mult, ALU.add)
            act(P, E0, lg0, AF.Exp)
            bO = banks.next()
            for h in range(4):
                for g in range(3):
                    gh = g * 4 + h
                    mm(P, bO[:, h:h + 1], vv[g][:, h * 128:(h + 1) * 128], ES[:, gh:gh + 1], g == 0, False)
                    mm(P, bO[:, h:h + 1], vs_[0:1, g, h * 128:(h + 1) * 128], E0[0:1, gh:gh + 1], False, g == 2)
                for g in range(3):
                    gh = g * 4 + h
                    mm(P, bO[:, 4 + h:5 + h], K.ones, ES[:, gh:gh + 1], g == 0, False)
                    mm(P, bO[:, 4 + h:5 + h], K.ones[0:1, :], E0[0:1, gh:gh + 1], False, g == 2)
            ob = w_[:, 48:56]
            cp(P, "dve", ob, bO[:, 0:8])
            recip(P, ob[:, 4:8], ob[:, 4:8])
            tt(P, "pool", ob[:, 0:4], ob[:, 0:4], ob[:, 4:8], ALU.mult)
            tt(P, "pool", ybs[:, :, b], ob[:, 0:4], zbs[:, :, b], ALU.mult)
        P.dma("sp", S["ybT"].re("(h p) t -> p h t", p=128)[:, :, NTP:NT], ybs)


def phase_d(C):
    P, A, I, O, S, K = C.P, C.A, C.I, C.O, C.S, C.K
    banks = C.banks
    wpa = A.alloc([8, DM], BF16)
    wpb = A.alloc([4, DM], BF16)
    wo = A.alloc([8, DM], BF16)
    gpost = A.alloc([DM], F32)
    P.dma("pool", wpa, I["w_proj_a"].re("(k p) n -> p k n", p=128))
    P.dma("pool", wpb, I["w_proj_b"].re("(k p) n -> p k n", p=128))
    P.dma("pool", wo, I["w_out"].re("(k p) n -> p k n", p=128))
    P.dma("sp", gpost, bcast_rows(I["g_post"], DM))
    yas = Rot([A.alloc([8, 512], BF16) for _ in range(2)])
    ybs_ = Rot([A.alloc([4, 512], BF16) for _ in range(2)])
    gas = Rot([A.alloc([8, 512], F32) for _ in range(int(os.environ.get('MK_GB', '2')))])
    gbs_ = Rot([A.alloc([8, 512], F32) for _ in range(int(os.environ.get('MK_GB', '2')))])
    xts = Rot([A.alloc([4, DM], F32) for _ in range(2)])
    hTs = Rot([A.alloc([8, 512], BF16) for _ in range(2)])
    t1s = Rot([A.alloc([512], F32) for _ in range(2)])
    t2s = Rot([A.alloc([512], F32) for _ in range(2)])
    junk = A.alloc([DM], F32)
    ysbs = Rot([A.alloc([DM], F32) for _ in range(2)])
    sts = Rot([A.alloc([4], F32) for _ in range(4)])
    fmv = lambda nm: S[nm].re("(k p) t -> p k t", p=128)
    TILES = [(ti * 512, 512) for ti in range(NTP // 512)] + [(NTP, NS)]
    def d_loads(t0, n):
        ya, yb, ga, gb_, xt = yas.next(), ybs_.next(), gas.next(), gbs_.next(), xts.next()
        P.dma("sp", ya[:, :, 0:n], fmv("yaT")[:, :, t0:t0 + n], acc=False)
        P.dma("sp", yb[:, :, 0:n], fmv("ybT")[:, :, t0:t0 + n], acc=False)
        P.dma("sp", ga[:, :, 0:n], fmv("gaT")[:, :, t0:t0 + n], acc=False)
        P.dma("sp", gb_[:, :, 0:n], fmv("gbT")[:, :, t0:t0 + n], acc=False)
        if n == 512:
            P.dma("sp", xt, I["x_p"][t0:t0 + 512, :].re("(s p) d -> p s d", p=128), acc=False)
        else:
            P.dma("sp", xt[0:NS, 0, :], I["x_s"], acc=False)
        return ya, yb, ga, gb_, xt
    nxt_ld = d_loads(*TILES[0])
    for ti, (t0, n) in enumerate(TILES):
        nsub = 4 if n == 512 else 1
        np_ = 128 if n == 512 else NS
        ya, yb, ga, gb_, xt = nxt_ld
        hT = hTs.next()
        if ti + 1 < len(TILES):
            nxt_ld = d_loads(*TILES[ti + 1])
        for e in range(8):
            pA, pB = banks.next(), banks.next()
            for k in range(8):
                mm(P, pA[:, 0:n], wpa[:, k, e * 128:(e + 1) * 128], ya[:, k, 0:n], k == 0, k == 7)
            for k in range(4):
                mm(P, pB[:, 0:n], wpb[:, k, e * 128:(e + 1) * 128], yb[:, k, 0:n], k == 0, k == 3)
            t1, t2 = t1s.next()[:, 0:n], t2s.next()[:, 0:n]
            tt(P, "dve", t1, ga[:, e, 0:n], pA[:, 0:n], ALU.mult)
            tt(P, "dve", t2, gb_[:, e, 0:n], pB[:, 0:n], ALU.mult)
            tt(P, "pool", hT[:, e, 0:n], t1, t2, ALU.add)
        for sb in range(nsub):
            bks = [banks.next(), banks.next()]
            for half in range(2):
                for k in range(8):
                    mm(P, bks[half][0:np_, :], hT[:, k, sb * 128:sb * 128 + np_], wo[:, k, half * 512:(half + 1) * 512],
                       k == 0, k == 7)
            for half in range(2):
                act(P, junk[0:np_, half * 512:(half + 1) * 512], bks[half][0:np_, :], AF.Square)
            st_ = sts.next()[0:np_]
            rsum(P, "dve", st_[:, 0:1], junk[0:np_])
            rsqrt(P, st_[:, 1:2], st_[:, 0:1], EPS, 1.0 / DM)
            ysb = ysbs.next()[0:np_]
            for half in range(2):
                hs = slice(half * 512, (half + 1) * 512)
                stt(P, "dve", ysb[:, hs], bks[half][0:np_, :], st_[:, 1:2], gpost[0:np_, hs], ALU.mult, ALU.mult)
            tt(P, "pool", ysb, ysb, xt[0:np_, sb, :], ALU.add)
            if n == 512:
                P.dma("pool", O["y_p"][t0 + sb * 128:t0 + (sb + 1) * 128, :], ysb)
            else:
                P.dma("pool", O["y_s"], ysb)


def rel_bucket_np(dist):
    import math
    max_exact = 16
    d = np.maximum(dist, 1).astype(np.float32)
    large = max_exact + (np.log(d / max_exact) / math.log(2048 / max_exact) * (32 - max_exact)).astype(np.int32)
    large = np.minimum(large, 31)
    return np.where(dist < max_exact, dist, large)


def host_consts():
    c = {}
    c["c_ident"] = np.eye(128, dtype=np.float32)
    k = np.arange(128)
    c["c_umat"] = (k[:, None] <= k[None, :]).astype(np.float32)
    c["c_maskT"] = np.where(k[None, :] >= k[:, None], 0.0, NEG).astype(np.float32)
    c["c_strictT"] = (k[None, :] > k[:, None]).astype(np.float32)
    oh = np.zeros((3, 32, 384), np.float32)
    ohs = np.zeros((3, 32, 128), np.float32)
    oh0 = np.zeros((3, 32, 1), np.float32)
    for g, dil in enumerate(DILS):
        bk = rel_bucket_np(np.arange(129, dtype=np.int32) * dil)
        for j in range(129):
            oh[g, bk[j], 127 + j] = 1.0
        for i in range(128):
            ohs[g, bk[128 - i], i] = 1.0
        oh0[g, bk[0], 0] = 1.0
    c["c_oh"] = oh
    negm = np.full((128, 384), NEG, np.float32)
    negm[:, 127:256] = 0.0
    c["c_negm"] = negm
    c["c_ohs"] = ohs
    c["c_oh0"] = oh0
    return c


_NC_CACHE = {}


def kernel(x_prompt, x_sample, state_gdn, state_conv, cache_k_w128, cache_v_w128,
           cache_k_w512, cache_v_w512, cache_k_w2048, cache_v_w2048, rel_table,
           g_pre, w_in, conv_w, a_log, dt_bias, g_head_norm, w_proj_a, w_proj_b, w_out, g_post):
    phases = os.environ.get("MK_PHASES", "ABCD")
    ncores = int(os.environ.get("MK_CORES", str(NCORES)))
    if phases not in _NC_CACHE:
        _NC_CACHE[phases] = build(phases)
    nc = _NC_CACHE[phases]
    f = lambda a: np.ascontiguousarray(np.asarray(a, dtype=np.float32))
    consts = host_consts()
    caches = ((cache_k_w128, cache_v_w128), (cache_k_w512, cache_v_w512), (cache_k_w2048, cache_v_w2048))
    in_maps = []
    for c in range(ncores):
        m = {}
        m["x_p"] = f(x_prompt[NSEQ * c:NSEQ * (c + 1)]).reshape(NTP, DM)
        m["x_s"] = f(x_sample[NS * c:NS * (c + 1)]).reshape(NS, DM)
        m["state_gdn"] = f(state_gdn[0, NS * c:NS * (c + 1)])
        m["state_conv"] = f(state_conv[0, NS * c:NS * (c + 1)])
        for g, (ck, cv) in enumerate(caches):
            m["ck%d" % g] = f(ck[0, NS * c:NS * (c + 1)]).reshape(NS, -1, 512)
            m["cv%d" % g] = f(cv[0, NS * c:NS * (c + 1)]).reshape(NS, -1, 512)
        m["rel_table"] = f(rel_table)
        m["g_pre"] = f(g_pre)
        m["w_in"] = f(w_in[0])
        m["conv_w"] = f(conv_w[0])
        m["a_log"] = f(a_log)
        m["dt_bias"] = f(dt_bias)
        m["g_head_norm"] = f(g_head_norm)
        m["w_proj_a"] = f(w_proj_a[0])
        m["w_proj_b"] = f(w_proj_b[0])
        m["w_out"] = f(w_out[0])
        m["g_post"] = f(g_post)
        m.update(consts)
        in_maps.append(m)
    if os.environ.get("MK_TRACE"):
        res = run_bass_kernel_spmd(nc, in_maps, core_ids=list(range(ncores)), trace=True)
        print("EXEC_TIME_NS", phases, res.exec_time_ns)
    else:
        res = run_bass_kernel_spmd(nc, in_maps, core_ids=list(range(ncores)))
    R = res.results
    cat = lambda k: np.concatenate([np.asarray(r[k]) for r in R], axis=0)
    B = NSEQ * ncores
    SB = NS * ncores
    outs = [cat("y_p").reshape(B, L, DM), cat("y_s").reshape(SB, 1, DM),
            cat("p_gdn").reshape(1, B, 8, 128, 128), cat("p_conv").reshape(1, B, 3, 3072)]
    for g, w in enumerate((128, 512, 2048)):
        outs.append(cat("p_k%d" % g).reshape(1, B, w, 4, 128))
        outs.append(cat("p_v%d" % g).reshape(1, B, w, 4, 128))
    outs.append(cat("s_gdn").reshape(1, SB, 8, 128, 128))
    outs.append(cat("s_conv").reshape(1, SB, 3, 3072))
    for g in range(3):
        outs.append(cat("s_k%d" % g).reshape(1, SB, 1, 4, 128))
        outs.append(cat("s_v%d" % g).reshape(1, SB, 1, 4, 128))
    return tuple(o.astype(np.float32) for o in outs)
```
